# Optimizing a Trainium2 kernel written in Bass

```python
import math
import jax, jax.numpy as jnp
from jax import lax
import numpy as np

D_MODEL = 2048
BATCH = 4
SEQ = 4096
DEPTH = 1
DEC_BATCH = 8
DEC_SEQ = 64
PAST_LEN = 1024

CHUNK = 64
N_META = 16
Q_BLOCK = 128
EPS = 1e-6
N_DIFF_HEADS = 8
DIFF_HEAD_DIM = 64
DIFF_V_DIM = 2 * DIFF_HEAD_DIM
D_DIFF_QK = N_DIFF_HEADS * 2 * DIFF_HEAD_DIM
D_DIFF = N_DIFF_HEADS * DIFF_V_DIM
DIFF_SCALE = DIFF_HEAD_DIM ** -0.5
N_MLA_HEADS = 8
MLA_NOPE_DIM = 128
MLA_ROPE_DIM = 64
MLA_V_DIM = 128
MLA_Q_LORA = 512
MLA_KV_LORA = 256
D_MLA = N_MLA_HEADS * MLA_V_DIM
MLA_SCALE = (MLA_NOPE_DIM + MLA_ROPE_DIM) ** -0.5
ROPE_THETA = 10000.0
D_MIX = D_DIFF + D_MLA
D_IN = 2 * D_DIFF_QK + D_DIFF + MLA_Q_LORA + MLA_KV_LORA + MLA_ROPE_DIM
OFF_K = D_DIFF_QK
OFF_V = 2 * D_DIFF_QK
OFF_CQ = OFF_V + D_DIFF
OFF_CKV = OFF_CQ + MLA_Q_LORA
OFF_KR = OFF_CKV + MLA_KV_LORA
REL_BUCKETS = 32
REL_MAX_DIST = 128
D_FF = ((8 * D_MODEL // 3 + 255) // 256) * 256

kernel_name = 'hymba_diff_mla_streaming_encoder_step'


def rmsnorm(x, g):
    xf = x.astype(jnp.float32)
    y = xf * lax.rsqrt(jnp.mean(xf * xf, axis=-1, keepdims=True) + EPS)
    return (y * g.astype(jnp.float32)).astype(x.dtype)


def rope(x, pos):
    half = x.shape[-1] // 2
    inv_freq = ROPE_THETA ** (-jnp.arange(half, dtype=jnp.float32) / half)
    ang = pos.astype(jnp.float32)[:, None] * inv_freq[None, :]
    shape = (1, pos.shape[0]) + (1,) * (x.ndim - 3) + (half,)
    cos = jnp.cos(ang).reshape(shape)
    sin = jnp.sin(ang).reshape(shape)
    xf = x.astype(jnp.float32)
    x1, x2 = xf[..., :half], xf[..., half:]
    return jnp.concatenate([x1 * cos - x2 * sin, x1 * sin + x2 * cos], axis=-1).astype(x.dtype)


def t5_bucket(rel):
    nb = REL_BUCKETS // 2
    max_exact = nb // 2
    ret = jnp.where(rel > 0, nb, 0)
    n = jnp.abs(rel)
    large = max_exact + (jnp.log(jnp.maximum(n, 1).astype(jnp.float32) / max_exact)
                         / math.log(REL_MAX_DIST / max_exact) * (nb - max_exact)).astype(jnp.int32)
    large = jnp.minimum(large, nb - 1)
    return ret + jnp.where(n < max_exact, n, large)


def project(h, pos, lp):
    B, L, _ = h.shape
    z = h @ lp['w_in']
    qd = z[..., :OFF_K].reshape(B, L, N_DIFF_HEADS, 2, DIFF_HEAD_DIM)
    kd = z[..., OFF_K:OFF_V].reshape(B, L, N_DIFF_HEADS, 2, DIFF_HEAD_DIM)
    vd = z[..., OFF_V:OFF_CQ].reshape(B, L, N_DIFF_HEADS, DIFF_V_DIM)
    cq = rmsnorm(z[..., OFF_CQ:OFF_CKV], lp['mla_q_norm_g'])
    qm = (cq @ lp['mla_w_uq']).reshape(B, L, N_MLA_HEADS, MLA_NOPE_DIM + MLA_ROPE_DIM)
    qm = jnp.concatenate([qm[..., :MLA_NOPE_DIM], rope(qm[..., MLA_NOPE_DIM:], pos)], axis=-1)
    ckv = rmsnorm(z[..., OFF_CKV:OFF_KR], lp['mla_kv_norm_g'])
    krope = rope(z[..., OFF_KR:], pos)
    return qd, kd, vd, qm, ckv, krope


def expand_latent(ckv, w_ukv):
    B, K, _ = ckv.shape
    kv = (ckv @ w_ukv).reshape(B, K, N_MLA_HEADS, MLA_NOPE_DIM + MLA_V_DIM)
    return kv[..., :MLA_NOPE_DIM], kv[..., MLA_NOPE_DIM:]


def attend_block(qd, qm, qpos, qchunk, kd, vd, km, krope, vm, kpos, kchunk, rel_bias, lam):
    neg = jnp.finfo(jnp.float32).min
    visible = kchunk[None, :] <= qchunk[:, None]
    bias = jnp.moveaxis(rel_bias[t5_bucket(kpos[None, :] - qpos[:, None])], -1, 0).astype(jnp.float32)
    s = jnp.einsum('bqhcd,bkhcd->bhcqk', qd, kd).astype(jnp.float32) * DIFF_SCALE + bias[None, :, None]
    p = jax.nn.softmax(jnp.where(visible, s, neg), axis=-1)
    p = p[:, :, 0] - lam * p[:, :, 1]
    o_d = jnp.einsum('bhqk,bkhe->bqhe', p.astype(vd.dtype), vd)
    qn, qr = qm[..., :MLA_NOPE_DIM], qm[..., MLA_NOPE_DIM:]
    sm = (jnp.einsum('bqhd,bkhd->bhqk', qn, km)
          + jnp.einsum('bqhr,bkr->bhqk', qr, krope)).astype(jnp.float32) * MLA_SCALE
    pm = jax.nn.softmax(jnp.where(visible, sm, neg), axis=-1)
    o_m = jnp.einsum('bhqk,bkhe->bqhe', pm.astype(vm.dtype), vm)
    return o_d, o_m


def blocked_attention(qd, qm, qpos, qchunk, kd, vd, km, krope, vm, kpos, kchunk, rel_bias, lam):
    L = qd.shape[1]
    nb = -(-L // Q_BLOCK)
    pad = nb * Q_BLOCK - L

    def to_blocks(a):
        a = jnp.pad(a, [(0, 0), (0, pad)] + [(0, 0)] * (a.ndim - 2))
        a = a.reshape((a.shape[0], nb, Q_BLOCK) + a.shape[2:])
        return jnp.moveaxis(a, 1, 0)

    def from_blocks(a):
        a = jnp.moveaxis(a, 0, 1)
        return a.reshape((a.shape[0], nb * Q_BLOCK) + a.shape[3:])[:, :L]

    qpos_b = jnp.pad(qpos, (0, pad), mode='edge').reshape(nb, Q_BLOCK)
    qchunk_b = jnp.pad(qchunk, (0, pad), mode='edge').reshape(nb, Q_BLOCK)

    def one_block(blk):
        bqd, bqm, bpos, bchunk = blk
        return attend_block(bqd, bqm, bpos, bchunk, kd, vd, km, krope, vm, kpos, kchunk, rel_bias, lam)

    o_d, o_m = lax.map(one_block, (to_blocks(qd), to_blocks(qm), qpos_b, qchunk_b))
    return from_blocks(o_d), from_blocks(o_m)


def trunk_layer(x, pos, chunk, past, lp, rel_bias, lam_init):
    B, L, _ = x.shape
    h = rmsnorm(x, lp['norm_attn_g'])
    qd, kd, vd, qm, ckv, krope = project(h, pos, lp)
    lamp = lp['diff_lambda'].astype(jnp.float32)
    lam = (jnp.exp(jnp.sum(lamp[0] * lamp[1])) - jnp.exp(jnp.sum(lamp[2] * lamp[3])) + lam_init)
    if past is None:
        km, vm = expand_latent(ckv, lp['mla_w_ukv'])
        o_d, o_m = blocked_attention(qd, qm, pos, chunk, kd, vd, km, krope, vm, pos, chunk, rel_bias, lam)
    else:
        pk, pv, pckv, pkr = past
        P = pk.shape[1]
        kd_all = jnp.concatenate([pk.astype(kd.dtype), kd], axis=1)
        vd_all = jnp.concatenate([pv.astype(vd.dtype), vd], axis=1)
        ckv_all = jnp.concatenate([pckv.astype(ckv.dtype), ckv], axis=1)
        kr_all = jnp.concatenate([pkr.astype(krope.dtype), krope], axis=1)
        past_pos = jnp.arange(P, dtype=jnp.int32)
        kpos = jnp.concatenate([past_pos, pos])
        kchunk = jnp.concatenate([past_pos // CHUNK, chunk])
        km, vm = expand_latent(ckv_all, lp['mla_w_ukv'])
        o_d, o_m = attend_block(qd, qm, pos, chunk, kd_all, vd_all, km, kr_all, vm, kpos, kchunk, rel_bias, lam)
    o_d = rmsnorm(o_d, lp['diff_subln_g']) * (1.0 - lam_init)
    mix = jnp.concatenate([o_d.reshape(B, L, D_DIFF), o_m.reshape(B, L, D_MLA)], axis=-1)
    x = x + mix @ lp['w_out']
    h = rmsnorm(x, lp['norm_ffn_g'])
    x = x + (jax.nn.silu(h @ lp['ffn_w_gate']) * (h @ lp['ffn_w_up'])) @ lp['ffn_w_down']
    return x, (kd, vd, ckv, krope)


def setup_inputs(seed: int = 0) -> dict:
    key = jax.random.key(seed)
    ks = jax.random.split(key, 24)
    f32 = jnp.float32

    def nrm(k, shape, scale):
        return jax.random.normal(k, shape, f32) * scale

    def gain(k, shape):
        return 1.0 + 0.02 * jax.random.normal(k, shape, f32)

    return {
        'x_prompt': nrm(ks[0], (BATCH, SEQ, D_MODEL), 1.0),
        'x_sample': nrm(ks[1], (DEC_BATCH, DEC_SEQ, D_MODEL), 1.0),
        'cache_diff_k': nrm(ks[2], (DEPTH, DEC_BATCH, PAST_LEN, N_DIFF_HEADS, 2, DIFF_HEAD_DIM), 1.0),
        'cache_diff_v': nrm(ks[3], (DEPTH, DEC_BATCH, PAST_LEN, N_DIFF_HEADS, DIFF_V_DIM), 1.0),
        'cache_mla_ckv': nrm(ks[4], (DEPTH, DEC_BATCH, PAST_LEN, MLA_KV_LORA), 1.0),
        'cache_mla_krope': nrm(ks[5], (DEPTH, DEC_BATCH, PAST_LEN, MLA_ROPE_DIM), 1.0),
        'meta_tokens': nrm(ks[6], (N_META, D_MODEL), 1.0),
        'rel_bias': nrm(ks[7], (REL_BUCKETS, N_DIFF_HEADS), 0.5),
        'norm_attn_g': gain(ks[8], (DEPTH, D_MODEL)),
        'w_in': nrm(ks[9], (DEPTH, D_MODEL, D_IN), D_MODEL ** -0.5),
        'diff_lambda': nrm(ks[10], (DEPTH, 4, DIFF_HEAD_DIM), 0.1),
        'diff_subln_g': gain(ks[11], (DEPTH, DIFF_V_DIM)),
        'mla_q_norm_g': gain(ks[12], (DEPTH, MLA_Q_LORA)),
        'mla_w_uq': nrm(ks[13], (DEPTH, MLA_Q_LORA, N_MLA_HEADS * (MLA_NOPE_DIM + MLA_ROPE_DIM)), MLA_Q_LORA ** -0.5),
        'mla_kv_norm_g': gain(ks[14], (DEPTH, MLA_KV_LORA)),
        'mla_w_ukv': nrm(ks[15], (DEPTH, MLA_KV_LORA, N_MLA_HEADS * (MLA_NOPE_DIM + MLA_V_DIM)), MLA_KV_LORA ** -0.5),
        'w_out': nrm(ks[16], (DEPTH, D_MIX, D_MODEL), D_MIX ** -0.5),
        'norm_ffn_g': gain(ks[17], (DEPTH, D_MODEL)),
        'ffn_w_gate': nrm(ks[18], (DEPTH, D_MODEL, D_FF), D_MODEL ** -0.5),
        'ffn_w_up': nrm(ks[19], (DEPTH, D_MODEL, D_FF), D_MODEL ** -0.5),
        'ffn_w_down': nrm(ks[20], (DEPTH, D_FF, D_MODEL), D_FF ** -0.5),
        'final_norm_g': gain(ks[21], (D_MODEL,)),
    }


def reference(x_prompt, x_sample, cache_diff_k, cache_diff_v, cache_mla_ckv, cache_mla_krope,
              meta_tokens, rel_bias, norm_attn_g, w_in, diff_lambda, diff_subln_g,
              mla_q_norm_g, mla_w_uq, mla_kv_norm_g, mla_w_ukv, w_out, norm_ffn_g,
              ffn_w_gate, ffn_w_up, ffn_w_down, final_norm_g):
    B, S, _ = x_prompt.shape
    L = N_META + S
    meta = jnp.broadcast_to(meta_tokens[None].astype(x_prompt.dtype), (B, N_META, D_MODEL))
    xp = jnp.concatenate([meta, x_prompt], axis=1)
    pos_p = jnp.arange(L, dtype=jnp.int32)
    chunk_p = jnp.where(pos_p < N_META, -1, (pos_p - N_META) // CHUNK)
    P = cache_diff_k.shape[2]
    Ds = x_sample.shape[1]
    pos_s = P + jnp.arange(Ds, dtype=jnp.int32)
    chunk_s = pos_s // CHUNK
    xs = x_sample
    rows_p = []
    rows_s = []
    for l in range(DEPTH):
        lp = {
            'norm_attn_g': norm_attn_g[l], 'w_in': w_in[l], 'diff_lambda': diff_lambda[l],
            'diff_subln_g': diff_subln_g[l], 'mla_q_norm_g': mla_q_norm_g[l], 'mla_w_uq': mla_w_uq[l],
            'mla_kv_norm_g': mla_kv_norm_g[l], 'mla_w_ukv': mla_w_ukv[l], 'w_out': w_out[l],
            'norm_ffn_g': norm_ffn_g[l], 'ffn_w_gate': ffn_w_gate[l], 'ffn_w_up': ffn_w_up[l],
            'ffn_w_down': ffn_w_down[l],
        }
        lam_init = 0.8 - 0.6 * math.exp(-0.3 * l)
        xp, rp = trunk_layer(xp, pos_p, chunk_p, None, lp, rel_bias, lam_init)
        past = (cache_diff_k[l], cache_diff_v[l], cache_mla_ckv[l], cache_mla_krope[l])
        xs, rs = trunk_layer(xs, pos_s, chunk_s, past, lp, rel_bias, lam_init)
        rows_p.append(rp)
        rows_s.append(rs)
    y_prompt = rmsnorm(xp[:, N_META:], final_norm_g)
    y_sample = rmsnorm(xs, final_norm_g)
    new_diff_k_prompt = jnp.stack([r[0] for r in rows_p])
    new_diff_v_prompt = jnp.stack([r[1] for r in rows_p])
    new_mla_ckv_prompt = jnp.stack([r[2] for r in rows_p])
    new_mla_krope_prompt = jnp.stack([r[3] for r in rows_p])
    new_diff_k_sample = jnp.stack([r[0] for r in rows_s])
    new_diff_v_sample = jnp.stack([r[1] for r in rows_s])
    new_mla_ckv_sample = jnp.stack([r[2] for r in rows_s])
    new_mla_krope_sample = jnp.stack([r[3] for r in rows_s])
    return (y_prompt, y_sample, new_diff_k_prompt, new_diff_v_prompt, new_mla_ckv_prompt,
            new_mla_krope_prompt, new_diff_k_sample, new_diff_v_sample, new_mla_ckv_sample,
            new_mla_krope_sample)
```

```python
import contextlib
import math
import numpy as np
import concourse.bass as bass
import concourse.mybir as mybir
from concourse.bass_utils import run_bass_kernel_spmd

F32 = mybir.dt.float32
BF16 = mybir.dt.bfloat16
U8 = mybir.dt.uint8
AF = mybir.ActivationFunctionType
ALU = mybir.AluOpType
AX = mybir.AxisListType

D = 2048
SEQ = 4096
NMETA = 16
NH = 8
DH = 64
DIN = 3904
OFF_K, OFF_V, OFF_CQ, OFF_CKV, OFF_KR = 1024, 2048, 3072, 3584, 3840
QL, KVL, ROPE = 512, 256, 64
DFF = 5632
NFC = DFF // 128
EPS = 1e-6
DIFF_SCALE = DH ** -0.5
MLA_SCALE = 192 ** -0.5
LAM_INIT = 0.8 - 0.6 * math.exp(0.0)
PAST = 1024
DSEQ = 64
NOWN = 16
NKEY = 2 * NOWN * 128 + NMETA
SKEY = PAST + DSEQ
MASKV = -30000.0

ENGINES = ("pe", "act", "dve", "pool", "sp")
EPOCH = 12000
N_EPOCH = {"pe": 8, "act": 10, "dve": 10, "pool": 3, "sp": 1}


class Buf:
    __slots__ = ("name", "w", "r", "excl")

    def __init__(self, name="", excl=False):
        self.name = name
        self.w = None
        self.r = {}
        self.excl = excl


class Prog:
    def __init__(self, nc, n_dma_sp=30, n_dma_pool=14, n_dma_act=2):
        self.nc = nc
        self.ops = {e: [] for e in ENGINES}
        self.sem_handles = []
        self.sem_names = []
        self.sem_owner = {}
        self.eng_sems = {}
        for e in ENGINES:
            self.eng_sems[e] = [self._new_sem(f"s_{e}{i}", e) for i in range(N_EPOCH[e])]
        self.cnt = {e: 0 for e in ENGINES}
        self.know = {e: {} for e in ENGINES}
        self.tok_know = {}
        self.dma_sems = {
            "sp": [[self._new_sem(f"d_sp{i}", None), 0] for i in range(n_dma_sp)],
            "pool": [[self._new_sem(f"d_pool{i}", None), 0] for i in range(n_dma_pool)],
            "act": [[self._new_sem(f"d_act{i}", None), 0] for i in range(n_dma_act)],
        }
        self.dma_rr = {"sp": 0, "pool": 0, "act": 0}
        self.out_tokens = []
        self.pending_dma = []
        self.own_done = {e: {} for e in ENGINES}
        self.n_waits = 0
        self.n_ops = {e: 0 for e in ENGINES}

    def _new_sem(self, name, owner):
        self.sem_names.append(name)
        i = len(self.sem_names) - 1
        self.sem_owner[i] = owner
        return i

    def _next_tok(self, e):
        c = self.cnt[e]
        return (self.eng_sems[e][c // EPOCH], c % EPOCH + 1)

    def _resolve(self, e, reads, writes, extra=()):
        deps = {}

        def add(tok, raw):
            if tok is None:
                return
            s, v = tok
            if (not raw) and self.sem_owner[s] == e and e == "pe":
                return
            if deps.get(s, -1) < v:
                deps[s] = v

        for b in reads:
            add(b.w, True)
            if b.excl:
                for s, v in b.r.items():
                    if self.sem_owner[s] != e:
                        add((s, v), True)
        for b in writes:
            add(b.w, False)
            for s, v in b.r.items():
                add((s, v), False)
        for tok in extra:
            add(tok, True)
        know = self.know[e]
        waits = [(s, v) for s, v in deps.items() if know.get(s, -1) < v]
        for s, v in waits:
            tk = self.tok_know.get((s, v))
            if tk:
                for s2, v2 in tk.items():
                    if know.get(s2, -1) < v2:
                        know[s2] = v2
            if know.get(s, -1) < v:
                know[s] = v
        self.n_waits += len(waits)
        return waits

    def _finish(self, tok, reads, writes, e):
        if tok not in self.tok_know:
            self.tok_know[tok] = dict(self.know[e])
        else:
            self.tok_know[tok].update(self.know[e])
        if self.own_done[e]:
            self.tok_know[tok].update(self.own_done[e])
        s, v = tok
        for b in reads:
            if b.r.get(s, -1) < v:
                b.r[s] = v
        for b in writes:
            b.w = tok
            b.r = {}

    def op(self, e, fn, reads=(), writes=(), signal=True, extra=()):
        waits = self._resolve(e, reads, writes, extra)
        tok = self._next_tok(e)
        self.n_ops[e] += 1
        if signal:
            self.cnt[e] += 1
            self.ops[e].append((waits, fn, (tok[0], 1)))
            if self.cnt[e] % EPOCH == 0:
                self.own_done[e][tok[0]] = EPOCH
        else:
            self.ops[e].append((waits, fn, None))
        self._finish(tok, reads, writes, e)
        return tok

    def dma(self, q, fn, reads=(), writes=(), is_output=False, extra=()):
        pool = self.dma_sems[q]
        i = self.dma_rr[q]
        self.dma_rr[q] = (i + 1) % len(pool)
        s, v = pool[i]
        prev = [(s, v)] if v > 0 else []
        waits = self._resolve(q, reads, writes, tuple(extra) + tuple(prev))
        pool[i][1] = v + 16
        tok = (s, v + 16)
        self.n_ops[q] += 1
        self.ops[q].append((waits, fn, (s, 16)))
        self._finish(tok, reads, writes, q)
        self.pending_dma.append(tok)
        if is_output:
            self.out_tokens.append(tok)
        return tok

    def barrier(self):
        toks = list(self.pending_dma)
        for e in ENGINES:
            c = self.cnt[e]
            if c > 0:
                cc = c - 1
                toks.append((self.eng_sems[e][cc // EPOCH], cc % EPOCH + 1))
        for e in ENGINES:
            waits = self._resolve(e, (), (), tuple(toks))
            self.ops[e].append((waits, None, None))
        self.pending_dma = []

    def finalize(self):
        waits = self._resolve("sp", (), (), tuple(self.out_tokens))
        self.ops["sp"].append((waits, None, None))

    def emit(self):
        nc = self.nc
        with contextlib.ExitStack() as st:
            for nm in self.sem_names:
                self.sem_handles.append(st.enter_context(nc.semaphore(nm)))
            block = st.enter_context(nc.Block())
            H = self.sem_handles

            def run(eng, ops):
                for waits, fn, inc in ops:
                    for s, v in waits:
                        eng.wait_ge(H[s], v)
                    if fn is None:
                        continue
                    ins = fn(eng)
                    if inc is not None:
                        ins.then_inc(H[inc[0]], inc[1])

            @block.sync
            def _(eng):
                run(eng, self.ops["sp"])

            @block.tensor
            def _(eng):
                run(eng, self.ops["pe"])

            @block.scalar
            def _(eng):
                run(eng, self.ops["act"])

            @block.vector
            def _(eng):
                run(eng, self.ops["dve"])

            @block.gpsimd
            def _(eng):
                run(eng, self.ops["pool"])


class Arena:
    def __init__(self, ap):
        self.ap = ap
        self.off = 0
        self.size = ap.shape[1]

    def alloc(self, shape, dt):
        es = 4 if dt == F32 else 2
        n = int(np.prod(shape)) * es
        assert self.off + n <= self.size, f"arena overflow {self.off}+{n}>{self.size}"
        v = self.ap[:, self.off:self.off + n].bitcast(dt)
        self.off += (n + 63) // 64 * 64
        if len(shape) == 2:
            v = v.rearrange("p (a b) -> p a b", b=shape[1])
        elif len(shape) == 3:
            v = v.rearrange("p (a b c) -> p a b c", b=shape[1], c=shape[2])
        return v


def build_program():
    nc = bass.Bass("TRN2", target_bir_lowering=False)
    P = Prog(nc)

    def din(name, shape):
        return nc.dram_tensor(name, list(shape), F32, kind="ExternalInput").ap()

    def dout(name, shape):
        return nc.dram_tensor(name, list(shape), F32, kind="ExternalOutput").ap()

    def dscr(name, shape, dt=BF16):
        return nc.dram_tensor(name, list(shape), dt).ap()

    xo = din("xo", [NOWN * 128, D]); xt = din("xt", [NOWN * 128, D])
    xmeta = din("xmeta", [NMETA, D]); xsam = din("xsam", [DSEQ, D])
    ck = din("ck", [PAST, 1024]); cv = din("cv", [PAST, 1024])
    cckv = din("cckv", [PAST, KVL]); ckr = din("ckr", [PAST, ROPE])
    w_in = din("w_in", [D, DIN]); w_uq = din("w_uq", [QL, 1536]); w_ukv = din("w_ukv", [KVL, 2048])
    w_out = din("w_out", [D, D]); wg = din("wg", [D, DFF]); wu = din("wu", [D, DFF]); wd = din("wd", [DFF, D])
    g_attn = din("g_attn", [D]); g_ffn = din("g_ffn", [D]); g_fin = din("g_fin", [D])
    g_q = din("g_q", [QL]); g_kv = din("g_kv", [KVL]); g_sub = din("g_sub", [128])
    lam4 = din("lam4", [256]); relb = din("relb", [32, 8])
    rope_o = din("rope_o", [NOWN * 128, 64]); rope_t = din("rope_t", [NOWN * 128, 64])
    rope_m = din("rope_m", [NMETA, 64]); rope_s = din("rope_s", [DSEQ, 64])
    oh = din("oh", [3, 32, 255])
    mask_diag = din("mask_diag", [128, 128]); mask_o0 = din("mask_o0", [128, 128])
    sel = din("sel", [128, 4])
    y_o = dout("y_o", [NOWN * 128, D]); kd_o = dout("kd_o", [NOWN * 128, 1024]); vd_o = dout("vd_o", [NOWN * 128, 1024])
    ckv_o = dout("ckv_o", [NOWN * 128, KVL]); kr_o = dout("kr_o", [NOWN * 128, ROPE])
    kd_m = dout("kd_m", [NMETA, 1024]); vd_m = dout("vd_m", [NMETA, 1024])
    ckv_m = dout("ckv_m", [NMETA, KVL]); kr_m = dout("kr_m", [NMETA, ROPE])
    y_s = dout("y_s", [DSEQ, D]); kd_s = dout("kd_s", [DSEQ, 1024]); vd_s = dout("vd_s", [DSEQ, 1024])
    ckv_s = dout("ckv_s", [DSEQ, KVL]); kr_s = dout("kr_s", [DSEQ, ROPE])
    w_in_b = dscr("w_in_b", [D, DIN]); w_uq_b = dscr("w_uq_b", [QL, 1536]); w_ukv_b = dscr("w_ukv_b", [KVL, 2048])
    w_out_b = dscr("w_out_b", [4, 128, 16, 512]); wg_b = dscr("wg_b", [NFC // 2, 128, 16, 256])
    wu_b = dscr("wu_b", [NFC // 2, 128, 16, 256]); wd_b = dscr("wd_b", [16, 128, NFC, 128])
    ck_b = dscr("ck_b", [PAST, 1024]); cckv_b = dscr("cckv_b", [PAST, KVL]); ckr_b = dscr("ckr_b", [PAST, ROPE])

    def seq_scratch(pfx, nkey, nq):
        return dict(
            KdT=dscr(pfx + "KdT", [NH, 128, nkey]), Vd=dscr(pfx + "Vd", [nkey, 1024]),
            KmT=dscr(pfx + "KmT", [NH, 128, nkey]), Vm=dscr(pfx + "Vm", [nkey, 1024]),
            KrT=dscr(pfx + "KrT", [64, nkey]),
            QdT=dscr(pfx + "QdT", [NH, 128, nq]), QnT=dscr(pfx + "QnT", [NH, 128, nq]),
            QrT=dscr(pfx + "QrT", [NH, 64, nq]), MixT=dscr(pfx + "MixT", [16, 128, nq]),
            nkey=nkey, nq=nq)

    SP_ = seq_scratch("p", NKEY, NOWN * 128)
    SS_ = seq_scratch("s", SKEY, DSEQ)

    arena_ap = nc.alloc_sbuf_tensor("arena", [128, 206 * 1024], U8).ap()
    AR = Arena(arena_ap)
    PS = nc.alloc_psum_tensor("ps", [128, 4096], F32).ap()
    bankB = [Buf(f"bank{i}", excl=True) for i in range(8)]

    def bank(i, dt=F32):
        v = PS[:, i * 512:(i + 1) * 512]
        return v.bitcast(dt) if dt != F32 else v

    rr = {"b": 0, "cp": 0}

    def next_bank(lo=0, hi=8):
        i = lo + rr["b"] % (hi - lo)
        rr["b"] += 1
        return i

    def copy_eng():
        rr["cp"] += 1
        return "dve"

    def copy_op(e, out, in_, r, w):
        if e == "act":
            P.op("act", lambda g: g.activation(out=out, in_=in_, func=AF.Copy), r, w)
        else:
            P.op(e, lambda g: g.tensor_copy(out, in_), r, w)

    ident_f = AR.alloc([128], F32); ident = AR.alloc([128], BF16)
    ones_b = AR.alloc([128], BF16); ones_f = AR.alloc([128], F32)
    lamb = AR.alloc([256], F32); lprod = AR.alloc([2, 64], F32); lsum = AR.alloc([2], F32)
    lexp = AR.alloc([2], F32); lam_t = AR.alloc([1], F32); neg_lam = AR.alloc([1], F32)
    gsub = AR.alloc([1], F32); gsub8 = AR.alloc([1], F32)
    sel_s = AR.alloc([4], F32)
    Bdiag = AR.alloc([NH, 128], BF16); Bo0 = AR.alloc([NH, 128], BF16); Bo1 = AR.alloc([NH, 128], BF16)
    Bmeta = AR.alloc([NH, 128], BF16); Bs7 = AR.alloc([NH, 64], BF16); Bsn = AR.alloc([NH, 64], BF16)
    Mdiag = AR.alloc([128], BF16); Mo0 = AR.alloc([128], BF16)
    b_const = Buf("const")
    persist_mark = AR.off

    def setup():
        R = AR.alloc([8], F32); R8 = AR.alloc([8], F32)
        oh_s = AR.alloc([3, 255], F32)
        md_f = AR.alloc([128], F32); mo_f = AR.alloc([128], F32)
        bR = Buf(); bT = Buf()
        P.op("pool", lambda g: g.memset(ident_f, 0.0), (), [bT])
        P.op("pool", lambda g: g.affine_select(ident_f, ident_f, pattern=[[-1, 128]], compare_op=ALU.not_equal,
                                               fill=1.0, base=0, channel_multiplier=1), [bT], [bT])
        P.op("pool", lambda g: g.memset(ones_f, 1.0), (), [bT])
        P.op("dve", lambda g: g.tensor_copy(ident, ident_f), [bT], [b_const])
        P.op("dve", lambda g: g.tensor_copy(ones_b, ones_f), [bT], [b_const])
        P.dma("sp", lambda g: g.dma_start(out=lamb, in_=lam4.partition_broadcast(128)), (), [bR])
        P.dma("sp", lambda g: g.dma_start(out=gsub, in_=g_sub.rearrange("(p o) -> p o", o=1)), (), [bR])
        P.dma("sp", lambda g: g.dma_start(out=sel_s, in_=sel), (), [bR])
        P.dma("sp", lambda g: g.dma_start(out=R[0:32], in_=relb), (), [bR])
        P.dma("sp", lambda g: g.dma_start(out=oh_s[0:32], in_=oh.rearrange("t b r -> b t r")), (), [bR])
        P.dma("sp", lambda g: g.dma_start(out=md_f, in_=mask_diag), (), [bR])
        P.dma("sp", lambda g: g.dma_start(out=mo_f, in_=mask_o0), (), [bR])
        lv = lamb.rearrange("p (a b c) -> p a b c", a=2, b=2, c=64)
        P.op("dve", lambda g: g.tensor_tensor(out=lprod, in0=lv[:, :, 0, :], in1=lv[:, :, 1, :], op=ALU.mult), [bR], [bT])
        P.op("dve", lambda g: g.reduce_sum(out=lsum, in_=lprod, axis=AX.X), [bT], [bT])
        P.op("act", lambda g: g.activation(out=lexp, in_=lsum, func=AF.Exp), [bT], [bT])
        P.op("dve", lambda g: g.tensor_tensor(out=lam_t, in0=lexp[:, 0:1], in1=lexp[:, 1:2], op=ALU.subtract), [bT], [bT])
        P.op("dve", lambda g: g.tensor_scalar(out=neg_lam, in0=lam_t, scalar1=LAM_INIT, scalar2=-1.0,
                                              op0=ALU.add, op1=ALU.mult), [bT], [b_const])
        P.op("dve", lambda g: g.tensor_scalar(out=gsub8, in0=gsub, scalar1=1.0 - LAM_INIT, scalar2=None,
                                              op0=ALU.mult), [bR], [b_const])
        P.op("dve", lambda g: g.tensor_scalar(out=R8[0:32], in0=R[0:32], scalar1=1.0 / DIFF_SCALE, scalar2=None,
                                              op0=ALU.mult), [bR], [bT])
        views = []
        for t in range(3):
            bA, bB = 2 * t, 2 * t + 1
            for q in range(128):
                bi = bA if q < 64 else bB
                o = bank(bi)[:, (q % 64) * 8:(q % 64) * 8 + 8]
                P.op("pe", lambda g, o=o, t=t, q=q: g.matmul(o, lhsT=oh_s[0:32, t, 127 - q:255 - q], rhs=R8[0:32],
                                                             start=True, stop=True),
                     [bT, bR], [bankB[bi]], signal=(q % 64 == 63))
            views.append((bA, bB))

        def tv(t, h, k0, k1, q0, q1):
            res = []
            for half in range(2):
                a, b = max(q0, half * 64), min(q1, half * 64 + 64)
                if a >= b:
                    continue
                bi = views[t][half]
                v = bank(bi).rearrange("p (q h) -> p q h", h=8)[k0:k1, a - half * 64:b - half * 64, h]
                res.append((v, bi, a, b))
            return res

        P.op("dve", lambda g: g.memset(Bmeta, 0.0), (), [b_const])
        P.op("dve", lambda g: g.memset(Bsn, 0.0), (), [b_const])
        for h in range(NH):
            for v, bi, a, b in tv(0, h, 0, 128, 0, 128):
                P.op("dve", lambda g, v=v, a=a, b=b, h=h: g.tensor_tensor(out=Bdiag[:, h, a:b], in0=v, in1=md_f[:, a:b], op=ALU.add),
                     [bankB[bi], bR], [b_const])
            for v, bi, a, b in tv(1, h, 0, 128, 0, 128):
                P.op("dve", lambda g, v=v, a=a, b=b, h=h: g.scalar_tensor_tensor(out=Bo0[:, h, a:b], in0=v, scalar=sel_s[:, 0:1],
                                                                                 in1=mo_f[:, a:b], op0=ALU.mult, op1=ALU.add),
                     [bankB[bi], bR], [b_const])
                P.op("act", lambda g, v=v, a=a, b=b, h=h: g.activation(out=Bo1[:, h, a:b], in_=v, func=AF.Copy, scale=sel_s[:, 1:2]),
                     [bankB[bi], bR], [b_const])
            for v, bi, a, b in tv(2, h, 0, 16, 0, 128):
                P.op("act", lambda g, v=v, a=a, b=b, h=h: g.activation(out=Bmeta[0:16, h, a:b], in_=v, func=AF.Copy, scale=sel_s[0:16, 2:3]),
                     [bankB[bi], bR], [b_const])
            for v, bi, a, b in tv(1, h, 0, 128, 0, 64):
                P.op("dve", lambda g, v=v, a=a, b=b, h=h: g.tensor_copy(Bs7[:, h, a:b], v), [bankB[bi]], [b_const])
            for v, bi, a, b in tv(0, h, 0, 64, 0, 64):
                P.op("dve", lambda g, v=v, a=a, b=b, h=h: g.tensor_copy(Bsn[0:64, h, a:b], v), [bankB[bi]], [b_const])
        P.op("dve", lambda g: g.tensor_copy(Mdiag, md_f), [bR], [b_const])
        P.op("dve", lambda g: g.tensor_copy(Mo0, mo_f), [bR], [b_const])

    def dma_mid(q, out3, in3, step, r, w):
        m = out3.shape[1]
        for a in range(0, m, step):
            b = min(m, a + step)
            P.dma(q, lambda g, a=a, b=b: g.dma_start(out=out3[:, a:b], in_=in3[:, a:b]), r, w)

    wbuf = {}

    deferred = []

    def cast(name, dst, src, rows_per=128, defer=False, slab=None):
        b = Buf(name)
        wbuf[name] = b
        n = src.shape[0]
        for r in range(0, n, rows_per):
            def go(r=r):
                if slab is None:
                    P.dma("pool", lambda g: g.dma_start(out=dst[r:r + rows_per], in_=src[r:r + rows_per]), (), [b])
                else:
                    w = slab
                    dv = dst.rearrange("a p c w -> p a c w")[:, :, r // 128, :]
                    sv = src[r:r + 128].rearrange("p (a w) -> p a w", w=w)
                    P.dma("pool", lambda g: g.dma_start(out=dv, in_=sv), (), [b])
            if defer:
                deferred.append(go)
            else:
                go()
        return b

    def rstd_from_ss(ss, rstd, n, nfeat, rb):
        P.op("dve", lambda g: g.tensor_scalar(out=rstd[0:n], in0=ss[0:n], scalar1=1.0 / nfeat, scalar2=EPS,
                                              op0=ALU.mult, op1=ALU.add), [rb], [rb])
        P.op("act", lambda g: g.activation(out=rstd[0:n], in_=rstd[0:n], func=AF.Sqrt), [rb], [rb])
        P.op("dve", lambda g: g.reciprocal(out=rstd[0:n], in_=rstd[0:n]), [rb], [rb])

    def transposes(src_fn, nblk, n, cols, dst, dstB, srcB, dt=BF16, idn=None):
        idn = ident if dt == BF16 else ident_f
        per = 8 if dt == BF16 else 4
        for j0 in range(0, nblk, per):
            bi = next_bank()
            bv = bank(bi, dt).rearrange("p (j t) -> p j t", t=128)
            m = min(per, nblk - j0)
            for j in range(j0, j0 + m):
                P.op("pe", lambda g, j=j, j0=j0, bv=bv: g.transpose(out=bv[0:cols, j - j0, 0:n], in_=src_fn(j), identity=idn[0:n, 0:n]),
                     [srcB, b_const], [bankB[bi]], signal=(j == j0 + m - 1))
            copy_op(copy_eng(), dst[0:cols, j0:j0 + m, 0:n], bv[0:cols, 0:m, 0:n], [bankB[bi]], [dstB])

    def norm_T(xs, bx, n, gbc, bg, hb, bhb, hT, bhT, ss, rstd, bst):
        import os
        mf = int(os.environ.get("MK_F", "9"))
        if mf == 0:
            return
        P.op("act", lambda g: g.activation(out=hb[0:n], in_=xs[0:n], func=AF.Square, accum_out=ss[0:n]), [bx], [bhb, bst])
        if mf == 1:
            return
        rstd_from_ss(ss, rstd, n, D, bst)
        if mf == 2:
            return
        P.op("dve", lambda g: g.scalar_tensor_tensor(out=hb[0:n], in0=xs[0:n], scalar=rstd[0:n], in1=gbc[0:n],
                                                     op0=ALU.mult, op1=ALU.mult), [bx, bst, bg], [bhb])
        if mf == 3:
            return
        transposes(lambda j: hb[0:n, j * 128:(j + 1) * 128], 16, n, 128, hT, bhT, bhb)

    def mm_group(bi, n, ncols, lhs_fn, rhs_fn, nk, rB, dst=None):
        o = bank(bi)[0:n, 0:ncols] if dst is None else dst
        for c in range(nk):
            P.op("pe", lambda g, c=c: g.matmul(o, lhsT=lhs_fn(c), rhs=rhs_fn(c), start=(c == 0), stop=(c == nk - 1)),
                 rB, [bankB[bi]], signal=(c == nk - 1))

    def rope_tok(src_fn, dst_fn, cos, sin, n, tmp, rB, wB, tB):
        x1, x2 = src_fn(0), src_fn(1)
        t1, t2 = tmp
        P.op("dve", lambda g: g.tensor_tensor(out=t1, in0=x1, in1=cos, op=ALU.mult), rB, [tB])
        P.op("dve", lambda g: g.tensor_tensor(out=t2, in0=x2, in1=sin, op=ALU.mult), rB, [tB])
        P.op("dve", lambda g: g.tensor_tensor(out=dst_fn(0), in0=t1, in1=t2, op=ALU.subtract), [tB], [wB])
        P.op("dve", lambda g: g.tensor_tensor(out=t1, in0=x1, in1=sin, op=ALU.mult), rB + [wB], [tB])
        P.op("dve", lambda g: g.tensor_tensor(out=t2, in0=x2, in1=cos, op=ALU.mult), rB, [tB])
        P.op("dve", lambda g: g.tensor_tensor(out=dst_fn(1), in0=t1, in1=t2, op=ALU.add), [tB], [wB])

    def passA():
        AR.off = persist_mark
        WinK = AR.alloc([16, 2368], BF16); Wukv = AR.alloc([2, 2048], BF16)
        g_bc = AR.alloc([D], F32); gkv_bc = AR.alloc([KVL], F32)
        xs2 = [AR.alloc([D], F32) for _ in range(2)]
        hb = AR.alloc([D], BF16)
        hT2 = [AR.alloc([16, 128], BF16) for _ in range(2)]
        kd_f = AR.alloc([1024], F32); vd_f = AR.alloc([1024], F32)
        kd_b = AR.alloc([1024], BF16); vd_b = AR.alloc([1024], BF16)
        ckv_f = AR.alloc([KVL], F32); ckv_b = AR.alloc([KVL], BF16)
        kr_f = AR.alloc([64], F32); kr_b = AR.alloc([64], BF16)
        rope2 = [AR.alloc([64], F32) for _ in range(2)]
        rt = [AR.alloc([32], F32) for _ in range(2)]
        junk = AR.alloc([KVL], F32)
        ss = AR.alloc([1], F32); rstd = AR.alloc([1], F32); ssk = AR.alloc([1], F32); rstdk = AR.alloc([1], F32)
        KdT_st = AR.alloc([NH, 128], BF16); ckvT = AR.alloc([2, 128], BF16); KrT_st = AR.alloc([1, 128], BF16)
        KmT_st = AR.alloc([NH, 128], BF16); vm_b = AR.alloc([1024], BF16)
        bW = Buf(); bg = Buf()
        bx2 = [Buf(), Buf()]; bhb = Buf(); bhT2 = [Buf(), Buf()]
        bkdf = Buf(); bvdf = Buf(); bkdb = Buf(); bvdb = Buf(); bckf = Buf(); bckb = Buf(); bkrf = Buf(); bkrb = Buf()
        brope2 = [Buf(), Buf()]; brt = Buf(); bjunk = Buf(); bst = Buf(); bstk = Buf()
        bKdT = Buf(); bckvT = Buf(); bKrT = Buf(); bKmT = Buf(); bvmb = Buf()
        wv = w_in.rearrange("(c p) n -> p c n", p=128)
        P.dma("pool", lambda g: g.dma_start(out=Wukv, in_=w_ukv.rearrange("(c p) n -> p c n", p=128)), (), [bW])
        dma_mid("pool", WinK[:, :, 2048:2368], wv[:, :, OFF_CKV:DIN], 4, (), [bW])
        dma_mid("pool", WinK[:, :, 0:2048], wv[:, :, OFF_K:OFF_CQ], 1, (), [bW])
        P.dma("sp", lambda g: g.dma_start(out=g_bc, in_=g_attn.partition_broadcast(128)), (), [bg])
        P.dma("sp", lambda g: g.dma_start(out=gkv_bc, in_=g_kv.partition_broadcast(128)), (), [bg])

        tiles = []
        for j in range(PAST // 128):
            tiles.append(dict(cache=j, n=128, seq=SS_, koff=j * 128))
        tiles.append(dict(x=xmeta, n=NMETA, seq=SP_, koff=2 * NOWN * 128, rope=rope_m, outs=(kd_m, vd_m, ckv_m, kr_m)))
        tiles.append(dict(x=xsam, n=DSEQ, seq=SS_, koff=PAST, rope=rope_s, outs=(kd_s, vd_s, ckv_s, kr_s)))
        for i in range(NOWN):
            r = slice(i * 128, (i + 1) * 128)
            tiles.append(dict(x=xo[r], n=128, seq=SP_, koff=i * 128, rope=rope_o[r],
                              outs=(kd_o[r], vd_o[r], ckv_o[r], kr_o[r])))
            tiles.append(dict(x=xt[r], n=128, seq=SP_, koff=NOWN * 128 + i * 128, rope=rope_t[r], outs=None))

        import os
        flt = os.environ.get("MK_A", "")
        if flt:
            keep = []
            for t in tiles:
                kind = "cache" if "cache" in t else ("meta" if t["n"] == NMETA else ("sam" if t["n"] == DSEQ else "own"))
                if kind in flt.split(","):
                    keep.append(t)
            tiles[:] = keep[:int(os.environ.get("MK_AN", "100"))]

        def load(idx):
            t = tiles[idx]
            if "cache" in t:
                return
            p = t["slot"]
            n = t["n"]
            P.dma("sp", lambda g: g.dma_start(out=xs2[p][0:n], in_=t["x"]), (), [bx2[p]])
            P.dma("sp", lambda g: g.dma_start(out=rope2[p][0:n], in_=t["rope"]), (), [brope2[p]])

        def front(idx):
            t = tiles[idx]
            if "cache" in t:
                return
            p = t["slot"]
            norm_T(xs2[p], bx2[p], t["n"], g_bc, bg, hb, bhb, hT2[p], bhT2[p], ss, rstd, bst)

        cut = int(os.environ.get("MK_CUT", "9"))

        def back(idx):
            t = tiles[idx]
            n = t["n"]; seq = t["seq"]; koff = t["koff"]
            if cut == 0:
                return
            if "cache" in t:
                j = t["cache"]
                r = slice(j * 128, (j + 1) * 128)
                P.dma("sp", lambda g: g.dma_start(out=kd_b[0:n], in_=ck_b[r]), [wbuf["ck"]], [bkdb])
                P.dma("sp", lambda g: g.dma_start(out=ckv_b[0:n], in_=cckv_b[r]), [wbuf["cckv"]], [bckb])
                P.dma("sp", lambda g: g.dma_start(out=kr_b[0:n], in_=ckr_b[r]), [wbuf["ckr"]], [bkrb])
            else:
                p = t["slot"]
                hT = hT2[p]
                rB = [bhT2[p], bW]
                for gi, (c0, dstf, dstb, bf_, bb_) in enumerate([(0, kd_f, kd_b, bkdf, bkdb), (512, kd_f, kd_b, bkdf, bkdb),
                                                                  (1024, vd_f, vd_b, bvdf, bvdb), (1536, vd_f, vd_b, bvdf, bvdb)]):
                    bi = next_bank()
                    mm_group(bi, n, 512, lambda c: hT[:, c, 0:n], lambda c, c0=c0: WinK[:, c, c0:c0 + 512], 16, rB)
                    lc = c0 % 1024
                    P.op("act", lambda g, bi=bi, dstf=dstf, lc=lc: g.activation(out=dstf[0:n, lc:lc + 512], in_=bank(bi)[0:n, 0:512], func=AF.Copy),
                         [bankB[bi]], [bf_])
                    P.op("dve", lambda g, bi=bi, dstb=dstb, lc=lc: g.tensor_copy(dstb[0:n, lc:lc + 512], bank(bi)[0:n, 0:512]),
                         [bankB[bi]], [bb_])
                if cut == 1:
                    return
                bi = next_bank()
                mm_group(bi, n, 320, lambda c: hT[:, c, 0:n], lambda c: WinK[:, c, 2048:2368], 16, rB)
                bk = bank(bi)
                P.op("act", lambda g: g.activation(out=junk[0:n], in_=bk[0:n, 0:KVL], func=AF.Square, accum_out=ssk[0:n]),
                     [bankB[bi]], [bjunk, bstk])
                rstd_from_ss(ssk, rstdk, n, KVL, bstk)
                P.op("dve", lambda g: g.scalar_tensor_tensor(out=ckv_f[0:n], in0=bk[0:n, 0:KVL], scalar=rstdk[0:n], in1=gkv_bc[0:n],
                                                             op0=ALU.mult, op1=ALU.mult), [bankB[bi], bstk, bg], [bckf])
                P.op("act", lambda g: g.activation(out=ckv_b[0:n], in_=ckv_f[0:n], func=AF.Copy), [bckf], [bckb])
                rp = rope2[p]
                rope_tok(lambda hf: bk[0:n, KVL + 32 * hf:KVL + 32 * hf + 32], lambda hf: kr_f[0:n, 32 * hf:32 * hf + 32],
                         rp[0:n, 0:32], rp[0:n, 32:64], n, (rt[0][0:n], rt[1][0:n]), [bankB[bi], brope2[p]], bkrf, brt)
                P.op("act", lambda g: g.activation(out=kr_b[0:n], in_=kr_f[0:n], func=AF.Copy), [bkrf], [bkrb])
                if t["outs"] is not None:
                    o_kd, o_vd, o_ckv, o_kr = t["outs"]
                    P.dma("pool", lambda g: g.dma_start(out=o_kd, in_=kd_f[0:n]), [bkdf], (), is_output=True)
                    P.dma("pool", lambda g: g.dma_start(out=o_vd, in_=vd_f[0:n]), [bvdf], (), is_output=True)
                    P.dma("pool", lambda g: g.dma_start(out=o_ckv, in_=ckv_f[0:n]), [bckf], (), is_output=True)
                    P.dma("pool", lambda g: g.dma_start(out=o_kr, in_=kr_f[0:n]), [bkrf], (), is_output=True)
                P.dma("pool", lambda g: g.dma_start(out=seq["Vd"][koff:koff + n, :], in_=vd_b[0:n]), [bvdb], ())
            if cut == 2:
                return
            transposes(lambda j: kd_b[0:n, j * 128:(j + 1) * 128], NH, n, 128, KdT_st, bKdT, bkdb)
            P.dma("pool", lambda g: g.dma_start(out=seq["KdT"][:, :, koff:koff + n].rearrange("h p k -> p h k"), in_=KdT_st[:, :, 0:n]),
                  [bKdT], ())
            transposes(lambda j: ckv_b[0:n, j * 128:(j + 1) * 128], 2, n, 128, ckvT, bckvT, bckb)
            transposes(lambda j: kr_b[0:n, 0:64], 1, n, 64, KrT_st, bKrT, bkrb)
            P.dma("pool", lambda g: g.dma_start(out=seq["KrT"][:, koff:koff + n], in_=KrT_st[0:64, 0, 0:n]), [bKrT], ())
            wk = Wukv.rearrange("p c (h x) -> p c h x", x=256)
            for hh in range(2):
                bi = next_bank()
                for h4 in range(4):
                    h = hh * 4 + h4
                    mm_group(bi, 128, n, lambda c, h=h: wk[:, c, h, 0:128], lambda c: ckvT[:, c, 0:n], 2, [bckvT, bW],
                             dst=bank(bi)[:, h4 * 128:h4 * 128 + n])
                copy_op(copy_eng(), KmT_st[:, hh * 4:hh * 4 + 4, 0:n],
                        bank(bi).rearrange("p (j t) -> p j t", t=128)[:, :, 0:n], [bankB[bi]], [bKmT])
            P.dma("pool", lambda g: g.dma_start(out=seq["KmT"][:, :, koff:koff + n].rearrange("h p k -> p h k"), in_=KmT_st[:, :, 0:n]),
                  [bKmT], ())
            for hh in range(2):
                bi = next_bank()
                mm_group(bi, n, 512, lambda c: ckvT[:, c, 0:n], lambda c, hh=hh: wk[:, c, hh * 4:hh * 4 + 4, 128:256], 2, [bckvT, bW])
                copy_op(copy_eng(), vm_b[0:n, hh * 512:(hh + 1) * 512], bank(bi)[0:n, 0:512], [bankB[bi]], [bvmb])
            P.dma("pool", lambda g: g.dma_start(out=seq["Vm"][koff:koff + n, :], in_=vm_b[0:n]), [bvmb], ())

        slot = 0
        for t in tiles:
            if "cache" not in t:
                t["slot"] = slot
                slot ^= 1
        nt = len(tiles)
        load(0)
        front(0)
        for i in range(nt):
            if i + 1 < nt:
                load(i + 1)
                front(i + 1)
            back(i)
            for _ in range(3):
                if deferred:
                    deferred.pop(0)()
        while deferred:
            deferred.pop(0)()

    def passB():
        AR.off = persist_mark
        WinQ = AR.alloc([16, 1536], BF16); Wuq = AR.alloc([4, 1536], BF16)
        g_bc = AR.alloc([D], F32); gq_bc = AR.alloc([QL], F32)
        xs2 = [AR.alloc([D], F32) for _ in range(2)]
        hb = AR.alloc([D], BF16)
        hT2 = [AR.alloc([16, 128], BF16) for _ in range(2)]
        qd_b = AR.alloc([1024], BF16); cq_b = AR.alloc([QL], BF16); cqT = AR.alloc([4, 128], BF16)
        qn_b = AR.alloc([NH, 128], BF16); qr_f = AR.alloc([NH, 64], F32); qr_b = AR.alloc([NH, 64], BF16)
        rope2 = [AR.alloc([64], F32) for _ in range(2)]
        rt = [AR.alloc([2, 32], F32) for _ in range(2)]
        junk = AR.alloc([QL], F32)
        ss = AR.alloc([1], F32); rstd = AR.alloc([1], F32); ssq = AR.alloc([1], F32); rstdq = AR.alloc([1], F32)
        QdT_st = AR.alloc([NH, 128], BF16); QnT_st = AR.alloc([NH, 128], BF16); QrT_st = AR.alloc([NH, 128], BF16)
        bW = Buf(); bg = Buf(); bx2 = [Buf(), Buf()]; bhb = Buf(); bhT2 = [Buf(), Buf()]
        bqdb = Buf(); bcqb = Buf(); bcqT = Buf(); bqnb = Buf(); bqrf = Buf(); bqrb = Buf()
        brope2 = [Buf(), Buf()]; brt = Buf(); bjunk = Buf(); bst = Buf(); bstq = Buf()
        bQdT = Buf(); bQnT = Buf(); bQrT = Buf()
        wv = w_in_b.rearrange("(c p) n -> p c n", p=128)
        dma_mid("sp", WinQ[:, :, 0:1024], wv[:, :, 0:OFF_K], 4, [wbuf["w_in"]], [bW])
        dma_mid("sp", WinQ[:, :, 1024:1536], wv[:, :, OFF_CQ:OFF_CKV], 4, [wbuf["w_in"]], [bW])
        P.dma("sp", lambda g: g.dma_start(out=Wuq, in_=w_uq_b.rearrange("(c p) n -> p c n", p=128)), [wbuf["w_uq"]], [bW])
        P.dma("sp", lambda g: g.dma_start(out=g_bc, in_=g_attn.partition_broadcast(128)), (), [bg])
        P.dma("sp", lambda g: g.dma_start(out=gq_bc, in_=g_q.partition_broadcast(128)), (), [bg])
        tiles = [dict(x=xsam, n=DSEQ, seq=SS_, qoff=0, rope=rope_s)]
        for i in range(NOWN):
            r = slice(i * 128, (i + 1) * 128)
            tiles.append(dict(x=xo[r], n=128, seq=SP_, qoff=i * 128, rope=rope_o[r]))

        import os
        if os.environ.get("MK_B", "") == "s":
            tiles[:] = tiles[:1]

        def load(idx):
            t = tiles[idx]; p = idx % 2; n = t["n"]
            P.dma("sp", lambda g: g.dma_start(out=xs2[p][0:n], in_=t["x"]), (), [bx2[p]])
            P.dma("sp", lambda g: g.dma_start(out=rope2[p][0:n], in_=t["rope"]), (), [brope2[p]])

        def front(idx):
            t = tiles[idx]; p = idx % 2
            norm_T(xs2[p], bx2[p], t["n"], g_bc, bg, hb, bhb, hT2[p], bhT2[p], ss, rstd, bst)

        def back(idx):
            t = tiles[idx]; p = idx % 2; n = t["n"]; seq = t["seq"]; qoff = t["qoff"]
            hT = hT2[p]; rB = [bhT2[p], bW]
            for c0 in (0, 512):
                bi = next_bank()
                mm_group(bi, n, 512, lambda c: hT[:, c, 0:n], lambda c, c0=c0: WinQ[:, c, c0:c0 + 512], 16, rB)
                copy_op(copy_eng(), qd_b[0:n, c0:c0 + 512], bank(bi)[0:n, 0:512], [bankB[bi]], [bqdb])
            transposes(lambda j: qd_b[0:n, j * 128:(j + 1) * 128], NH, n, 128, QdT_st, bQdT, bqdb)
            P.dma("pool", lambda g: g.dma_start(out=seq["QdT"][:, :, qoff:qoff + n].rearrange("h p k -> p h k"), in_=QdT_st[:, :, 0:n]),
                  [bQdT], ())
            bi = next_bank()
            mm_group(bi, n, 512, lambda c: hT[:, c, 0:n], lambda c: WinQ[:, c, 1024:1536], 16, rB)
            bk = bank(bi)
            P.op("act", lambda g: g.activation(out=junk[0:n], in_=bk[0:n, 0:QL], func=AF.Square, accum_out=ssq[0:n]),
                 [bankB[bi]], [bjunk, bstq])
            rstd_from_ss(ssq, rstdq, n, QL, bstq)
            P.op("dve", lambda g: g.scalar_tensor_tensor(out=cq_b[0:n], in0=bk[0:n, 0:QL], scalar=rstdq[0:n], in1=gq_bc[0:n],
                                                         op0=ALU.mult, op1=ALU.mult), [bankB[bi], bstq, bg], [bcqb])
            transposes(lambda j: cq_b[0:n, j * 128:(j + 1) * 128], 4, n, 128, cqT, bcqT, bcqb)
            rp = rope2[p]
            for gq in range(4):
                bi = next_bank()
                mm_group(bi, n, 384, lambda c: cqT[:, c, 0:n], lambda c, gq=gq: Wuq[:, c, gq * 384:(gq + 1) * 384], 4, [bcqT, bW])
                bv = bank(bi)[:, 0:384].rearrange("p (h x) -> p h x", x=192)
                P.op("act", lambda g, bv=bv, gq=gq: g.activation(out=qn_b[0:n, 2 * gq:2 * gq + 2, :], in_=bv[0:n, :, 0:128], func=AF.Copy),
                     [bankB[bi]], [bqnb])
                cosb = rp[0:n, 0:32].unsqueeze(1).to_broadcast([n, 2, 32])
                sinb = rp[0:n, 32:64].unsqueeze(1).to_broadcast([n, 2, 32])
                rope_tok(lambda hf, bv=bv: bv[0:n, :, 128 + 32 * hf:128 + 32 * hf + 32],
                         lambda hf, gq=gq: qr_f[0:n, 2 * gq:2 * gq + 2, 32 * hf:32 * hf + 32],
                         cosb, sinb, n, (rt[0][0:n], rt[1][0:n]), [bankB[bi], brope2[p]], bqrf, brt)
            P.op("act", lambda g: g.activation(out=qr_b[0:n], in_=qr_f[0:n], func=AF.Copy), [bqrf], [bqrb])
            transposes(lambda j: qn_b[0:n, j, :], NH, n, 128, QnT_st, bQnT, bqnb)
            P.dma("pool", lambda g: g.dma_start(out=seq["QnT"][:, :, qoff:qoff + n].rearrange("h p k -> p h k"), in_=QnT_st[:, :, 0:n]),
                  [bQnT], ())
            transposes(lambda j: qr_b[0:n, j, :], NH, n, 64, QrT_st, bQrT, bqrb)
            P.dma("pool", lambda g: g.dma_start(out=seq["QrT"][:, :, qoff:qoff + n].rearrange("h p k -> p h k"), in_=QrT_st[0:64, :, 0:n]),
                  [bQrT], ())

        nt = len(tiles)
        load(0)
        front(0)
        for i in range(nt):
            if i + 1 < nt:
                load(i + 1)
                front(i + 1)
            back(i)

    def attention():
        AR.off = persist_mark
        NT = NKEY // 128 + 1
        NKP = NKEY - NMETA + 128
        KT2 = [AR.alloc([NKP], BF16) for _ in range(2)]
        V2 = [AR.alloc([NT, 128], BF16) for _ in range(2)]
        Q2 = [[AR.alloc([NOWN * 128], BF16) for _ in range(2)] for _ in range(2)]
        Qn2 = [AR.alloc([NOWN * 128], BF16) for _ in range(2)]
        QR2 = [AR.alloc([NOWN * 128], BF16) for _ in range(2)]
        KrT = AR.alloc([NKP], BF16)
        PT = [AR.alloc([512], BF16) for _ in range(4)]
        onesM = AR.alloc([128], BF16); onesS = AR.alloc([128], BF16)
        bz = Buf()
        for par_ in range(2):
            P.op("pool", lambda g, par_=par_: g.memset(KT2[par_], 0.0), (), [bz])
            P.op("pool", lambda g, par_=par_: g.memset(V2[par_], 0.0), (), [bz])
            P.op("pool", lambda g, par_=par_: g.memset(Q2[par_][0], 0.0), (), [bz])
            P.op("pool", lambda g, par_=par_: g.memset(Q2[par_][1], 0.0), (), [bz])
            P.op("pool", lambda g, par_=par_: g.memset(QR2[par_], 0.0), (), [bz])
        P.op("pool", lambda g: g.memset(KrT, 0.0), (), [bz])
        for i_ in range(4):
            P.op("pool", lambda g, i_=i_: g.memset(PT[i_], 0.0), (), [bz])
        P.op("pool", lambda g: g.memset(onesM, 0.0), (), [bz])
        P.op("pool", lambda g: g.memset(onesS, 0.0), (), [bz])
        P.op("pool", lambda g: g.memset(onesM[0:NMETA], 1.0), [bz], [bz])
        P.op("pool", lambda g: g.memset(onesS[0:DSEQ], 1.0), [bz], [bz])
        P.barrier()
        rl = AR.alloc([512], F32); on0 = AR.alloc([512], F32); on1 = AR.alloc([512], F32)
        od = AR.alloc([512], F32); sq = AR.alloc([512], F32); rs = AR.alloc([512], F32)
        mix_st = [AR.alloc([512], BF16) for _ in range(2)]
        acc2 = [AR.alloc([512], F32) for _ in range(2)]
        bacc2 = [Buf(), Buf()]
        bKT2 = [Buf(), Buf()]; bV2 = [Buf(), Buf()]; bQ2 = [Buf(), Buf()]; bQR2 = [Buf(), Buf()]; bKrT = Buf()
        bPT = [Buf() for _ in range(4)]
        brl = Buf(); bon0 = Buf(); bon1 = Buf(); bod = Buf(); bsq = Buf(); brs = Buf(); bmix = [Buf(), Buf()]
        SETS = [dict(S=(0, 1), O=2, L=3, PT=(0, 1), acc=0), dict(S=(4, 5), O=6, L=7, PT=(2, 3), acc=1)]

        def tile_list(kind, G, mla):
            tl = []
            if kind == "p":
                nq = 512
                tl.append(dict(koff=2 * NOWN * 128, nk=NMETA, c0=0,
                               bias=([(Bmeta, 0, 128, NMETA)] if (G == 0 and not mla) else [])))
                for j in range(4 * G + 4):
                    c0 = max(j - 4 * G, 0) * 128
                    b = []
                    if j >= 4 * G:
                        b.append((Mdiag if mla else Bdiag, c0, 128, 128))
                    tl.append(dict(koff=j * 128, nk=128, c0=c0, bias=b))
                    b = []
                    if j >= 4 * G:
                        b.append((Mo0 if mla else Bo0, c0, 128, 128))
                    if (not mla) and 4 * G <= j + 1 <= 4 * G + 3:
                        b.append((Bo1, (j + 1 - 4 * G) * 128, 128, 128))
                    tl.append(dict(koff=NOWN * 128 + j * 128, nk=128, c0=c0, bias=b))
                tl.sort(key=lambda d: d["c0"])
            else:
                nq = DSEQ
                for j in range(PAST // 128):
                    tl.append(dict(koff=j * 128, nk=128, c0=0, bias=([(Bs7, 0, 64, 128)] if (j == 7 and not mla) else [])))
                tl.append(dict(koff=PAST, nk=DSEQ, c0=0, bias=([(Bsn, 0, 64, 64)] if not mla else [])))
            return tl, nq

        steps = []
        maps = []
        mi = 0
        import os
        att_f = os.environ.get("MK_ATT", "sp")
        for kind, seq in (("s", SS_), ("p", SP_)):
            if kind not in att_f:
                continue
            nkey = seq["nkey"]; nqt = seq["nq"]
            ngroups = 1 if kind == "s" else 4
            for hi in range(2 * NH):
                mla = hi >= NH
                h = hi % NH
                par = hi % 2
                maps_head = []
                for G in range(ngroups):
                    for comp in ((0,) if mla else (0, 1)):
                        tl, nq = tile_list(kind, G, mla)
                        m = dict(kind=kind, seq=seq, mla=mla, h=h, par=par, G=G, comp=comp, tiles=tl, nq=nq,
                                 qbase=G * 512 if kind == "p" else 0, set=SETS[mi % 2], first_of_head=(G == 0 and comp == 0),
                                 nkey=nkey, nqt=nqt)
                        mi += 1
                        maps.append(m)
                        for ti in range(len(tl)):
                            steps.append((m, ti))

        def load_head(m):
            seq = m["seq"]; h = m["h"]; par = m["par"]; nkey = m["nkey"]; nqt = m["nqt"]; kind = m["kind"]
            nfull = (nkey // 128)
            rem = nkey - nfull * 128
            if m["mla"]:
                P.dma("sp", lambda g: g.dma_start(out=KT2[par][:, 0:nkey], in_=seq["KmT"][h]), (), [bKT2[par]])
                vsrc = seq["Vm"]
                P.dma("sp", lambda g: g.dma_start(out=Qn2[par][:, 0:nqt], in_=seq["QnT"][h]), (), [bQ2[par]])
                P.dma("sp", lambda g: g.dma_start(out=QR2[par][0:64, 0:nqt], in_=seq["QrT"][h]), (), [bQR2[par]])
                if h == 0:
                    P.dma("sp", lambda g: g.dma_start(out=KrT[0:64, 0:nkey], in_=seq["KrT"]), (), [bKrT])
            else:
                P.dma("sp", lambda g: g.dma_start(out=KT2[par][:, 0:nkey], in_=seq["KdT"][h]), (), [bKT2[par]])
                vsrc = seq["Vd"]
                P.dma("sp", lambda g: g.dma_start(out=Q2[par][0][0:64, 0:nqt], in_=seq["QdT"][h][0:64]), (), [bQ2[par]])
                P.dma("sp", lambda g: g.dma_start(out=Q2[par][1][64:128, 0:nqt], in_=seq["QdT"][h][64:128]), (), [bQ2[par]])
            dma_mid("sp", V2[par][:, 0:nfull, :], vsrc[0:nfull * 128, h * 128:(h + 1) * 128].rearrange("(t p) e -> p t e", p=128),
                    4, (), [bV2[par]])
            P.dma("sp", lambda g: g.dma_start(out=V2[par][0:rem, nfull, :], in_=vsrc[nfull * 128:nkey, h * 128:(h + 1) * 128]),
                  (), [bV2[par]])

        def stage1(m, ti):
            kt = m["tiles"][ti]; st = m["set"]; par = m["par"]
            nk = kt["nk"]; c0 = kt["c0"]; nq = m["nq"]; koff = kt["koff"]; qb = m["qbase"]
            Sb = st["S"][ti % 2]
            o = bank(Sb)[:, c0:nq]
            nb = len(kt["bias"])
            if m["mla"]:
                P.op("pe", lambda g: g.matmul(o, lhsT=KT2[par][:, koff:koff + 128], rhs=Qn2[par][:, qb + c0:qb + nq], start=True, stop=False),
                     [bKT2[par], bQ2[par]], [bankB[Sb]], signal=False)
                P.op("pe", lambda g: g.matmul(o, lhsT=KrT[:, koff:koff + 128], rhs=QR2[par][:, qb + c0:qb + nq], start=False, stop=(nb == 0)),
                     [bKrT, bQR2[par]], [bankB[Sb]], signal=(nb == 0))
            else:
                cc = m["comp"]
                P.op("pe", lambda g: g.matmul(o, lhsT=KT2[par][:, koff:koff + 128], rhs=Q2[par][cc][:, qb + c0:qb + nq], start=True, stop=(nb == 0)),
                     [bKT2[par], bQ2[par]], [bankB[Sb]], signal=(nb == 0))
            for bi_, (tab, coff, ncb, nkb) in enumerate(kt["bias"]):
                rhs = tab[:, 0:ncb] if m["mla"] else tab[:, m["h"], 0:ncb]
                P.op("pe", lambda g, rhs=rhs, coff=coff, ncb=ncb, last=(bi_ == nb - 1): g.matmul(
                    bank(Sb)[:, coff:coff + ncb], lhsT=ident, rhs=rhs, start=False, stop=last),
                    [b_const], [bankB[Sb]], signal=(bi_ == nb - 1))
            pt = st["PT"][ti % 2]
            scale = MLA_SCALE if m["mla"] else DIFF_SCALE
            P.op("act", lambda g: g.activation(out=PT[pt][:, c0:nq], in_=o, func=AF.Exp, scale=scale), [bankB[Sb]], [bPT[pt]])

        def stage2(m, ti):
            kt = m["tiles"][ti]; st = m["set"]; par = m["par"]
            nk = kt["nk"]; c0 = kt["c0"]; nq = m["nq"]; koff = kt["koff"]
            pt = st["PT"][ti % 2]
            nt_ = len(m["tiles"])
            first = (ti == 0); last = (ti == nt_ - 1)
            slot = koff // 128
            ones_t = ones_b if nk == 128 else (onesM if nk == NMETA else onesS)
            a_ = st["acc"]
            on_dve = (nk == 128) and (ti % 2 == 1)
            if first:
                P.op("pool", lambda g: g.memset(acc2[a_][:, 0:nq], 0.0), (), [bacc2[a_]])
            P.op("pe", lambda g: g.matmul(bank(st["O"])[:, c0:nq], lhsT=V2[par][:, slot, :], rhs=PT[pt][:, c0:nq], start=first, stop=last),
                 [bV2[par], bPT[pt]], [bankB[st["O"]]], signal=on_dve)
            if on_dve:
                P.op("dve", lambda g: g.tensor_tensor(out=acc2[a_][:, c0:nq], in0=acc2[a_][:, c0:nq], in1=PT[pt][:, c0:nq], op=ALU.add),
                     [bPT[pt], bacc2[a_]], [bacc2[a_]])
            else:
                P.op("pe", lambda g: g.matmul(bank(st["L"])[:, c0:nq], lhsT=ones_t, rhs=PT[pt][:, c0:nq], start=first, stop=False),
                     [b_const, bPT[pt]], [bankB[st["L"]]], signal=True)
            if last:
                P.op("pe", lambda g: g.matmul(bank(st["L"])[:, 0:nq], lhsT=ones_f, rhs=acc2[a_][:, 0:nq], start=False, stop=True),
                     [b_const, bacc2[a_]], [bankB[st["L"]]], signal=True)
                finish(m)

        def finish(m):
            st = m["set"]; nq = m["nq"]; seq = m["seq"]; qb = m["qbase"]; h = m["h"]
            O = bank(st["O"])[:, 0:nq]; L = bank(st["L"])[:, 0:nq]
            bO = bankB[st["O"]]; bL = bankB[st["L"]]
            P.op("dve", lambda g: g.reciprocal(out=rl[:, 0:nq], in_=L), [bL], [brl])
            if m["mla"]:
                ms = m["G"] % 2
                P.op("dve", lambda g: g.tensor_tensor(out=mix_st[ms][:, 0:nq], in0=O, in1=rl[:, 0:nq], op=ALU.mult), [bO, brl], [bmix[ms]])
                P.dma("pool", lambda g: g.dma_start(out=seq["MixT"][NH + h][:, qb:qb + nq], in_=mix_st[ms][:, 0:nq]), [bmix[ms]], ())
                return
            if m["comp"] == 0:
                P.op("dve", lambda g: g.tensor_tensor(out=on0[:, 0:nq], in0=O, in1=rl[:, 0:nq], op=ALU.mult), [bO, brl], [bon0])
                return
            P.op("dve", lambda g: g.tensor_tensor(out=on1[:, 0:nq], in0=O, in1=rl[:, 0:nq], op=ALU.mult), [bO, brl], [bon1])
            P.op("dve", lambda g: g.scalar_tensor_tensor(out=od[:, 0:nq], in0=on1[:, 0:nq], scalar=neg_lam, in1=on0[:, 0:nq],
                                                         op0=ALU.mult, op1=ALU.add), [bon0, bon1, b_const], [bod])
            P.op("act", lambda g: g.activation(out=sq[:, 0:nq], in_=od[:, 0:nq], func=AF.Square), [bod], [bsq])
            Sb = st["S"][0]
            P.op("pe", lambda g: g.matmul(bank(Sb)[:, 0:nq], lhsT=ones_f, rhs=sq[:, 0:nq], start=True, stop=True), [bsq, b_const], [bankB[Sb]])
            P.op("dve", lambda g: g.tensor_scalar(out=rs[:, 0:nq], in0=bank(Sb)[:, 0:nq], scalar1=1.0 / 128, scalar2=EPS,
                                                  op0=ALU.mult, op1=ALU.add), [bankB[Sb]], [brs])
            P.op("act", lambda g: g.activation(out=rs[:, 0:nq], in_=rs[:, 0:nq], func=AF.Sqrt), [brs], [brs])
            P.op("dve", lambda g: g.reciprocal(out=rs[:, 0:nq], in_=rs[:, 0:nq]), [brs], [brs])
            ms = m["G"] % 2
            P.op("dve", lambda g: g.scalar_tensor_tensor(out=mix_st[ms][:, 0:nq], in0=od[:, 0:nq], scalar=gsub8, in1=rs[:, 0:nq],
                                                         op0=ALU.mult, op1=ALU.mult), [bod, brs, b_const], [bmix[ms]])
            P.dma("pool", lambda g: g.dma_start(out=seq["MixT"][h][:, qb:qb + nq], in_=mix_st[ms][:, 0:nq]), [bmix[ms]], ())

        head_first = [i for i, mm in enumerate(maps) if mm["first_of_head"]]
        load_head(maps[head_first[0]])
        nxt = {head_first[k]: head_first[k + 1] for k in range(len(head_first) - 1)}
        seen = set()
        prev = None
        for (m, ti) in steps:
            stage1(m, ti)
            if prev is not None:
                stage2(*prev)
            prev = (m, ti)
            idm = id(m)
            if idm not in seen:
                seen.add(idm)
                k = maps.index(m)
                if k in nxt:
                    load_head(maps[nxt[k]])
        stage2(*prev)

    def phase3():
        AR.off = persist_mark
        TMAX = 576
        g_bc = AR.alloc([D], F32)
        mixh = AR.alloc([16, TMAX], BF16)
        x1 = AR.alloc([5, D], F32)
        gated = AR.alloc([NFC, TMAX], BF16)
        _o = AR.off
        wo2 = [AR.alloc([16, 512], BF16) for _ in range(2)]
        AR.off = _o
        wgu = [[AR.alloc([16, 256], BF16) for _ in range(2)] for _ in range(2)]
        wd2 = [AR.alloc([NFC, 128], BF16) for _ in range(2)]
        hb = AR.alloc([D], BF16)
        sg = [AR.alloc([TMAX], BF16) for _ in range(2)]
        dT = [AR.alloc([TMAX], F32) for _ in range(2)]
        ss = AR.alloc([1], F32); rstd = AR.alloc([1], F32)
        bg = Buf(); bmixh = Buf(); bx1 = [Buf() for _ in range(5)]; bgated = Buf()
        bwgu = [[Buf(), Buf()], [Buf(), Buf()]]; bwd2 = [Buf(), Buf()]; bhb = Buf(); bsg = [Buf(), Buf()]; bdT = [Buf(), Buf()]
        bst = Buf()
        own = lambda i: dict(seq=SP_, q0=i * 128, n=128, x=xo[i * 128:(i + 1) * 128], y=y_o[i * 128:(i + 1) * 128])
        groups = [[dict(seq=SS_, q0=0, n=DSEQ, x=xsam, y=y_s)] + [own(i) for i in range(4)]]
        for G in range(1, 4):
            groups.append([own(i) for i in range(4 * G, 4 * G + 4)])
        cnt = {"wo": 0, "gu": 0, "wd": 0, "sg": 0, "dT": 0}

        def do_group(tiles):
            off = 0
            for t in tiles:
                t["off"] = off
                off += t["n"]
            T = off
            chunks = []
            cur = []
            for t in tiles:
                if cur and (t["off"] + t["n"] - cur[0]["off"] > 512 or t["seq"] is not cur[0]["seq"]):
                    chunks.append(cur); cur = []
                cur.append(t)
            chunks.append(cur)
            chunks = [(c[0]["off"], c[-1]["off"] + c[-1]["n"], c) for c in chunks]
            P.dma("sp", lambda g: g.dma_start(out=g_bc, in_=g_ffn.partition_broadcast(128)), (), [bg])
            for (c0, c1, ct) in chunks:
                seq = ct[0]["seq"]; q0 = ct[0]["q0"]
                dma_mid("sp", mixh[:, :, c0:c1], seq["MixT"][:, :, q0:q0 + (c1 - c0)].rearrange("c p t -> p c t"), 4, (), [bmixh])
            for k, t in enumerate(tiles):
                P.dma("sp", lambda g, k=k, t=t: g.dma_start(out=x1[0:t["n"], k, :], in_=t["x"]), (), [bx1[k]])
            for cg in range(4):
                p = cnt["wo"] % 2; cnt["wo"] += 1
                P.dma("sp", lambda g, p=p, cg=cg: g.dma_start(out=wo2[p], in_=w_out_b[cg]), [wbuf["w_out"]], bwgu[p])
                for k, t in enumerate(tiles):
                    n = t["n"]; o_ = t["off"]
                    bi = next_bank()
                    mm_group(bi, n, 512, lambda c, o_=o_, n=n: mixh[:, c, o_:o_ + n], lambda c, p=p: wo2[p][:, c, :], 16, [bmixh] + bwgu[p])
                    P.op("dve", lambda g, bi=bi, k=k, cg=cg, n=n: g.tensor_tensor(out=x1[0:n, k, cg * 512:(cg + 1) * 512], in0=bank(bi)[0:n, 0:512],
                                                                                   in1=x1[0:n, k, cg * 512:(cg + 1) * 512], op=ALU.add),
                         [bankB[bi], bx1[k]], [bx1[k]])
            for k, t in enumerate(tiles):
                n = t["n"]; o_ = t["off"]
                P.op("act", lambda g, k=k, n=n: g.activation(out=hb[0:n], in_=x1[0:n, k, :], func=AF.Square, accum_out=ss[0:n]), [bx1[k]], [bhb, bst])
                rstd_from_ss(ss, rstd, n, D, bst)
                P.op("dve", lambda g, k=k, n=n: g.scalar_tensor_tensor(out=hb[0:n], in0=x1[0:n, k, :], scalar=rstd[0:n], in1=g_bc[0:n],
                                                                        op0=ALU.mult, op1=ALU.mult), [bx1[k], bst, bg], [bhb])
                for j0 in (0, 8):
                    bi = next_bank()
                    bv = bank(bi, BF16).rearrange("p (j t) -> p j t", t=128)
                    for j in range(j0, j0 + 8):
                        P.op("pe", lambda g, j=j, j0=j0, bv=bv, n=n: g.transpose(out=bv[:, j - j0, 0:n], in_=hb[0:n, j * 128:(j + 1) * 128], identity=ident[0:n, 0:n]),
                             [bhb, b_const], [bankB[bi]], signal=(j == j0 + 7))
                    copy_op("dve", mixh[:, j0:j0 + 8, o_:o_ + n], bv[:, :, 0:n], [bankB[bi]], [bmixh])
            P.dma("sp", lambda g: g.dma_start(out=g_bc, in_=g_fin.partition_broadcast(128)), (), [bg])
            for f2 in range(NFC // 2):
                p = cnt["gu"] % 2; cnt["gu"] += 1
                P.dma("sp", lambda g, p=p, f2=f2: g.dma_start(out=wgu[0][p], in_=wg_b[f2]), [wbuf["wg"]], [bwgu[0][p]])
                P.dma("sp", lambda g, p=p, f2=f2: g.dma_start(out=wgu[1][p], in_=wu_b[f2]), [wbuf["wu"]], [bwgu[1][p]])
                for sub in range(2):
                    f = f2 * 2 + sub
                    s_ = cnt["sg"] % 2; cnt["sg"] += 1
                    for (c0, c1, ct) in chunks:
                        bg_ = next_bank(); bu_ = next_bank()
                        mm_group(bg_, 128, c1 - c0, lambda c, p=p, sub=sub: wgu[0][p][:, c, sub * 128:(sub + 1) * 128],
                                 lambda c, c0=c0, c1=c1: mixh[:, c, c0:c1], 16, [bmixh, bwgu[0][p]])
                        mm_group(bu_, 128, c1 - c0, lambda c, p=p, sub=sub: wgu[1][p][:, c, sub * 128:(sub + 1) * 128],
                                 lambda c, c0=c0, c1=c1: mixh[:, c, c0:c1], 16, [bmixh, bwgu[1][p]])
                        w = c1 - c0
                        P.op("act", lambda g, bg_=bg_, s_=s_, c0=c0, c1=c1, w=w: g.activation(out=sg[s_][:, c0:c1], in_=bank(bg_)[:, 0:w], func=AF.Silu),
                             [bankB[bg_]], [bsg[s_]])
                        P.op("dve", lambda g, bu_=bu_, s_=s_, f=f, c0=c0, c1=c1, w=w: g.tensor_tensor(out=gated[:, f, c0:c1], in0=bank(bu_)[:, 0:w],
                                                                                                     in1=sg[s_][:, c0:c1], op=ALU.mult),
                             [bankB[bu_], bsg[s_]], [bgated])
            for j in range(16):
                p = cnt["wd"] % 2; cnt["wd"] += 1
                P.dma("sp", lambda g, p=p, j=j: g.dma_start(out=wd2[p], in_=wd_b[j]), [wbuf["wd"]], [bwd2[p]])
                d_ = cnt["dT"] % 2; cnt["dT"] += 1
                for (c0, c1, ct) in chunks:
                    w = c1 - c0
                    bi = next_bank()
                    mm_group(bi, 128, w, lambda c, p=p: wd2[p][:, c, :], lambda c, c0=c0, c1=c1: gated[:, c, c0:c1], NFC, [bgated, bwd2[p]])
                    P.op("act", lambda g, bi=bi, d_=d_, c0=c0, c1=c1, w=w: g.activation(out=dT[d_][:, c0:c1], in_=bank(bi)[:, 0:w], func=AF.Copy),
                         [bankB[bi]], [bdT[d_]])
                    bt = next_bank()
                    for kk, t in enumerate(ct):
                        n = t["n"]; o_ = t["off"]
                        P.op("pe", lambda g, kk=kk, bt=bt, d_=d_, n=n, o_=o_: g.transpose(out=bank(bt)[0:n, kk * 128:(kk + 1) * 128],
                                                                                          in_=dT[d_][:, o_:o_ + n], identity=ident_f),
                             [bdT[d_], b_const], [bankB[bt]], signal=(kk == len(ct) - 1))
                    for kk, t in enumerate(ct):
                        n = t["n"]; k = tiles.index(t)
                        P.op("dve", lambda g, kk=kk, bt=bt, j=j, n=n, k=k: g.tensor_tensor(out=x1[0:n, k, j * 128:(j + 1) * 128],
                                                                                           in0=bank(bt)[0:n, kk * 128:(kk + 1) * 128],
                                                                                           in1=x1[0:n, k, j * 128:(j + 1) * 128], op=ALU.add),
                             [bankB[bt], bx1[k]], [bx1[k]])
            for k, t in enumerate(tiles):
                n = t["n"]
                P.op("act", lambda g, k=k, n=n: g.activation(out=hb[0:n], in_=x1[0:n, k, :], func=AF.Square, accum_out=ss[0:n]), [bx1[k]], [bhb, bst])
                rstd_from_ss(ss, rstd, n, D, bst)
                P.op("dve", lambda g, k=k, n=n: g.scalar_tensor_tensor(out=x1[0:n, k, :], in0=x1[0:n, k, :], scalar=rstd[0:n], in1=g_bc[0:n],
                                                                        op0=ALU.mult, op1=ALU.mult), [bx1[k], bst, bg], [bx1[k]])
                P.dma("pool", lambda g, k=k, t=t, n=n: g.dma_start(out=t["y"], in_=x1[0:n, k, :]), [bx1[k]], (), is_output=True)

        import os
        if os.environ.get("MK_P3", "") == "s":
            groups[:] = [groups[0][:1]]
        for tiles in groups:
            do_group(tiles)

    cast("ck", ck_b, ck)
    cast("cckv", cckv_b, cckv)
    cast("ckr", ckr_b, ckr)
    cast("cv", SS_["Vd"][0:PAST], cv)
    cast("w_in", w_in_b, w_in, defer=True)
    cast("w_uq", w_uq_b, w_uq, defer=True)
    cast("w_out", w_out_b, w_out, defer=True, slab=512)
    cast("wg", wg_b, wg, defer=True, slab=256)
    cast("wu", wu_b, wu, defer=True, slab=256)
    cast("wd", wd_b, wd, defer=True, slab=128)
    import os
    stop = int(os.environ.get("MK_STOP", "9"))
    setup()
    P.barrier()
    if stop >= 1:
        passA()
        P.barrier()
    if stop >= 2:
        passB()
        P.barrier()
    if stop >= 3:
        attention()
        P.barrier()
    if stop >= 4:
        phase3()
    while deferred:
        deferred.pop(0)()
    P.finalize()
    P.emit()
    return nc, P


def _t5_onehot(delta):
    rel = (np.arange(255, dtype=np.int32) - 127 + delta)
    try:
        import jax
        import jax.numpy as jnp
        with jax.default_device(jax.devices("cpu")[0]):
            r = jnp.asarray(rel)
            nb = 16
            max_exact = 8
            ret = jnp.where(r > 0, nb, 0)
            n = jnp.abs(r)
            large = max_exact + (jnp.log(jnp.maximum(n, 1).astype(jnp.float32) / max_exact)
                                 / math.log(128 / max_exact) * (nb - max_exact)).astype(jnp.int32)
            large = jnp.minimum(large, nb - 1)
            bucket = np.asarray(ret + jnp.where(n < max_exact, n, large))
    except Exception:
        nb, max_exact = 16, 8
        ret = np.where(rel > 0, nb, 0)
        n = np.abs(rel)
        large = max_exact + (np.log(np.maximum(n, 1).astype(np.float32) / np.float32(max_exact))
                             / np.float32(math.log(128 / max_exact)) * np.float32(nb - max_exact)).astype(np.int32)
        large = np.minimum(large, nb - 1)
        bucket = ret + np.where(n < max_exact, n, large)
    ohm = np.zeros((32, 255), np.float32)
    ohm[bucket, np.arange(255)] = 1.0
    ohm[15, :] -= 1.0
    return ohm


def _rope_table(pos):
    half = 32
    inv_freq = (np.float32(10000.0) ** (-np.arange(half, dtype=np.float32) / np.float32(half))).astype(np.float32)
    ang = pos.astype(np.float32)[:, None] * inv_freq[None, :]
    return np.concatenate([np.cos(ang), np.sin(ang)], axis=1).astype(np.float32)


_CACHE = {}


def kernel(x_prompt, x_sample, cache_diff_k, cache_diff_v, cache_mla_ckv, cache_mla_krope,
           meta_tokens, rel_bias, norm_attn_g, w_in, diff_lambda, diff_subln_g,
           mla_q_norm_g, mla_w_uq, mla_kv_norm_g, mla_w_ukv, w_out, norm_ffn_g,
           ffn_w_gate, ffn_w_up, ffn_w_down, final_norm_g):
    f = lambda a: np.ascontiguousarray(np.asarray(a, dtype=np.float32))
    x_prompt = f(x_prompt); x_sample = f(x_sample)
    if "nc" not in _CACHE:
        _CACHE["nc"] = build_program()[0]
    nc = _CACHE["nc"]
    ohs = np.stack([_t5_onehot(0), _t5_onehot(-128), _t5_onehot(-16)])
    kq = np.arange(128)
    md = np.where((kq[None, :] < 64) & (kq[:, None] >= 64), MASKV, 0.0).astype(np.float32)
    shared = dict(
        w_in=f(w_in[0]), w_uq=f(mla_w_uq[0]), w_ukv=f(mla_w_ukv[0]), w_out=f(w_out[0]),
        wg=f(ffn_w_gate[0]), wu=f(ffn_w_up[0]), wd=f(ffn_w_down[0]),
        g_attn=f(norm_attn_g[0]), g_ffn=f(norm_ffn_g[0]), g_fin=f(final_norm_g),
        g_q=f(mla_q_norm_g[0]), g_kv=f(mla_kv_norm_g[0]), g_sub=f(diff_subln_g[0]),
        lam4=f(diff_lambda[0]).reshape(256), relb=f(rel_bias), xmeta=f(meta_tokens),
        oh=ohs, mask_diag=md, rope_m=_rope_table(np.arange(NMETA)), rope_s=_rope_table(PAST + np.arange(DSEQ)),
    )
    in_maps = []
    for c in range(8):
        b, par = c // 2, c % 2
        own = np.arange(NOWN) * 2 + par
        oth = np.arange(NOWN) * 2 + (1 - par)
        xb = x_prompt[b].reshape(32, 128, D)
        pos_o = (NMETA + own[:, None] * 128 + np.arange(128)[None, :]).reshape(-1)
        pos_t = (NMETA + oth[:, None] * 128 + np.arange(128)[None, :]).reshape(-1)
        selv = np.zeros((128, 4), np.float32)
        if par == 1:
            selv[:, 0] = 1.0
            mo = np.zeros((128, 128), np.float32)
        else:
            selv[:, 1] = 1.0
            selv[:, 2] = 1.0
            mo = np.full((128, 128), MASKV, np.float32)
        m = dict(shared)
        m.update(
            xo=np.ascontiguousarray(xb[own].reshape(-1, D)), xt=np.ascontiguousarray(xb[oth].reshape(-1, D)),
            xsam=x_sample[c],
            ck=f(cache_diff_k[0, c]).reshape(PAST, 1024), cv=f(cache_diff_v[0, c]).reshape(PAST, 1024),
            cckv=f(cache_mla_ckv[0, c]), ckr=f(cache_mla_krope[0, c]),
            rope_o=_rope_table(pos_o), rope_t=_rope_table(pos_t), mask_o0=mo, sel=selv,
        )
        in_maps.append(m)
    if _CACHE.get("prep_only"):
        return in_maps
    res = run_bass_kernel_spmd(nc, in_maps, core_ids=list(range(8))).results
    B = 4
    L = NMETA + SEQ
    y_prompt = np.zeros((B, SEQ, D), np.float32)
    nk = np.zeros((1, B, L, 1024), np.float32); nv = np.zeros((1, B, L, 1024), np.float32)
    nckv = np.zeros((1, B, L, KVL), np.float32); nkr = np.zeros((1, B, L, ROPE), np.float32)
    y_sample = np.zeros((8, DSEQ, D), np.float32)
    sk = np.zeros((1, 8, DSEQ, 1024), np.float32); sv = np.zeros((1, 8, DSEQ, 1024), np.float32)
    sckv = np.zeros((1, 8, DSEQ, KVL), np.float32); skr = np.zeros((1, 8, DSEQ, ROPE), np.float32)
    for c in range(8):
        b, par = c // 2, c % 2
        r = res[c]
        for i in range(NOWN):
            gt = 2 * i + par
            sl = slice(i * 128, (i + 1) * 128)
            y_prompt[b, gt * 128:(gt + 1) * 128] = r["y_o"][sl]
            dst = slice(NMETA + gt * 128, NMETA + (gt + 1) * 128)
            nk[0, b, dst] = r["kd_o"][sl]; nv[0, b, dst] = r["vd_o"][sl]
            nckv[0, b, dst] = r["ckv_o"][sl]; nkr[0, b, dst] = r["kr_o"][sl]
        if par == 0:
            nk[0, b, 0:NMETA] = r["kd_m"]; nv[0, b, 0:NMETA] = r["vd_m"]
            nckv[0, b, 0:NMETA] = r["ckv_m"]; nkr[0, b, 0:NMETA] = r["kr_m"]
        y_sample[c] = r["y_s"]
        sk[0, c] = r["kd_s"]; sv[0, c] = r["vd_s"]; sckv[0, c] = r["ckv_s"]; skr[0, c] = r["kr_s"]
    return (y_prompt, y_sample,
            nk.reshape(1, B, L, NH, 2, DH), nv.reshape(1, B, L, NH, 128), nckv, nkr,
            sk.reshape(1, 8, DSEQ, NH, 2, DH), sv.reshape(1, 8, DSEQ, NH, 128), sckv, skr)
```

```python
import contextlib
import math
import numpy as np
import concourse.bass as bass
import concourse.mybir as mybir
from concourse.bass_utils import run_bass_kernel_spmd

F32 = mybir.dt.float32
BF16 = mybir.dt.bfloat16
U8 = mybir.dt.uint8
AF = mybir.ActivationFunctionType
ALU = mybir.AluOpType
AX = mybir.AxisListType

D = 2048
SEQ = 4096
NMETA = 16
NH = 8
DH = 64
DIN = 3904
OFF_K, OFF_V, OFF_CQ, OFF_CKV, OFF_KR = 1024, 2048, 3072, 3584, 3840
QL, KVL, ROPE = 512, 256, 64
DFF = 5632
NFC = DFF // 128
EPS = 1e-6
DIFF_SCALE = DH ** -0.5
MLA_SCALE = 192 ** -0.5
LAM_INIT = 0.8 - 0.6 * math.exp(0.0)
PAST = 1024
DSEQ = 64
NOWN = 16
NKEY = 2 * NOWN * 128 + NMETA
SKEY = PAST + DSEQ
MASKV = -30000.0

ENGINES = ("pe", "act", "dve", "pool", "sp")
EPOCH = 12000
N_EPOCH = {"pe": 8, "act": 10, "dve": 10, "pool": 3, "sp": 1}


class Buf:
    __slots__ = ("name", "w", "r", "excl")

    def __init__(self, name="", excl=False):
        self.name = name
        self.w = None
        self.r = {}
        self.excl = excl


class Prog:
    def __init__(self, nc, n_dma_sp=30, n_dma_pool=14, n_dma_act=2):
        self.nc = nc
        self.ops = {e: [] for e in ENGINES}
        self.sem_handles = []
        self.sem_names = []
        self.sem_owner = {}
        self.eng_sems = {}
        for e in ENGINES:
            self.eng_sems[e] = [self._new_sem(f"s_{e}{i}", e) for i in range(N_EPOCH[e])]
        self.cnt = {e: 0 for e in ENGINES}
        self.know = {e: {} for e in ENGINES}
        self.tok_know = {}
        self.dma_sems = {
            "sp": [[self._new_sem(f"d_sp{i}", None), 0] for i in range(n_dma_sp)],
            "pool": [[self._new_sem(f"d_pool{i}", None), 0] for i in range(n_dma_pool)],
            "act": [[self._new_sem(f"d_act{i}", None), 0] for i in range(n_dma_act)],
        }
        self.dma_rr = {"sp": 0, "pool": 0, "act": 0}
        self.out_tokens = []
        self.pending_dma = []
        self.own_done = {e: {} for e in ENGINES}
        self.n_waits = 0
        self.n_ops = {e: 0 for e in ENGINES}

    def _new_sem(self, name, owner):
        self.sem_names.append(name)
        i = len(self.sem_names) - 1
        self.sem_owner[i] = owner
        return i

    def _next_tok(self, e):
        c = self.cnt[e]
        return (self.eng_sems[e][c // EPOCH], c % EPOCH + 1)

    def _resolve(self, e, reads, writes, extra=()):
        deps = {}

        def add(tok, raw):
            if tok is None:
                return
            s, v = tok
            if (not raw) and self.sem_owner[s] == e and e == "pe":
                return
            if deps.get(s, -1) < v:
                deps[s] = v

        for b in reads:
            add(b.w, True)
            if b.excl:
                for s, v in b.r.items():
                    if self.sem_owner[s] != e:
                        add((s, v), True)
        for b in writes:
            add(b.w, False)
            for s, v in b.r.items():
                add((s, v), False)
        for tok in extra:
            add(tok, True)
        know = self.know[e]
        waits = [(s, v) for s, v in deps.items() if know.get(s, -1) < v]
        for s, v in waits:
            tk = self.tok_know.get((s, v))
            if tk:
                for s2, v2 in tk.items():
                    if know.get(s2, -1) < v2:
                        know[s2] = v2
            if know.get(s, -1) < v:
                know[s] = v
        self.n_waits += len(waits)
        return waits

    def _finish(self, tok, reads, writes, e):
        if tok not in self.tok_know:
            self.tok_know[tok] = dict(self.know[e])
        else:
            self.tok_know[tok].update(self.know[e])
        if self.own_done[e]:
            self.tok_know[tok].update(self.own_done[e])
        s, v = tok
        for b in reads:
            if b.r.get(s, -1) < v:
                b.r[s] = v
        for b in writes:
            b.w = tok
            b.r = {}

    def op(self, e, fn, reads=(), writes=(), signal=True, extra=()):
        waits = self._resolve(e, reads, writes, extra)
        tok = self._next_tok(e)
        self.n_ops[e] += 1
        if signal:
            self.cnt[e] += 1
            self.ops[e].append((waits, fn, (tok[0], 1)))
            if self.cnt[e] % EPOCH == 0:
                self.own_done[e][tok[0]] = EPOCH
        else:
            self.ops[e].append((waits, fn, None))
        self._finish(tok, reads, writes, e)
        return tok

    def dma(self, q, fn, reads=(), writes=(), is_output=False, extra=()):
        pool = self.dma_sems[q]
        i = self.dma_rr[q]
        self.dma_rr[q] = (i + 1) % len(pool)
        s, v = pool[i]
        prev = [(s, v)] if v > 0 else []
        waits = self._resolve(q, reads, writes, tuple(extra) + tuple(prev))
        pool[i][1] = v + 16
        tok = (s, v + 16)
        self.n_ops[q] += 1
        self.ops[q].append((waits, fn, (s, 16)))
        self._finish(tok, reads, writes, q)
        self.pending_dma.append(tok)
        if is_output:
            self.out_tokens.append(tok)
        return tok

    def barrier(self):
        toks = list(self.pending_dma)
        for e in ENGINES:
            c = self.cnt[e]
            if c > 0:
                cc = c - 1
                toks.append((self.eng_sems[e][cc // EPOCH], cc % EPOCH + 1))
        for e in ENGINES:
            waits = self._resolve(e, (), (), tuple(toks))
            self.ops[e].append((waits, None, None))
        self.pending_dma = []

    def finalize(self):
        waits = self._resolve("sp", (), (), tuple(self.out_tokens))
        self.ops["sp"].append((waits, None, None))

    def emit(self):
        nc = self.nc
        with contextlib.ExitStack() as st:
            for nm in self.sem_names:
                self.sem_handles.append(st.enter_context(nc.semaphore(nm)))
            block = st.enter_context(nc.Block())
            H = self.sem_handles

            def run(eng, ops):
                for waits, fn, inc in ops:
                    for s, v in waits:
                        eng.wait_ge(H[s], v)
                    if fn is None:
                        continue
                    ins = fn(eng)
                    if inc is not None:
                        ins.then_inc(H[inc[0]], inc[1])

            @block.sync
            def _(eng):
                run(eng, self.ops["sp"])

            @block.tensor
            def _(eng):
                run(eng, self.ops["pe"])

            @block.scalar
            def _(eng):
                run(eng, self.ops["act"])

            @block.vector
            def _(eng):
                run(eng, self.ops["dve"])

            @block.gpsimd
            def _(eng):
                run(eng, self.ops["pool"])


class Arena:
    def __init__(self, ap):
        self.ap = ap
        self.off = 0
        self.size = ap.shape[1]

    def alloc(self, shape, dt):
        es = 4 if dt == F32 else 2
        n = int(np.prod(shape)) * es
        assert self.off + n <= self.size, f"arena overflow {self.off}+{n}>{self.size}"
        v = self.ap[:, self.off:self.off + n].bitcast(dt)
        self.off += (n + 63) // 64 * 64
        if len(shape) == 2:
            v = v.rearrange("p (a b) -> p a b", b=shape[1])
        elif len(shape) == 3:
            v = v.rearrange("p (a b c) -> p a b c", b=shape[1], c=shape[2])
        return v


def build_program():
    nc = bass.Bass("TRN2", target_bir_lowering=False)
    P = Prog(nc)

    def din(name, shape):
        return nc.dram_tensor(name, list(shape), F32, kind="ExternalInput").ap()

    def dout(name, shape):
        return nc.dram_tensor(name, list(shape), F32, kind="ExternalOutput").ap()

    def dscr(name, shape, dt=BF16):
        return nc.dram_tensor(name, list(shape), dt).ap()

    xo = din("xo", [NOWN * 128, D]); xt = din("xt", [NOWN * 128, D])
    xmeta = din("xmeta", [NMETA, D]); xsam = din("xsam", [DSEQ, D])
    ck = din("ck", [PAST, 1024]); cv = din("cv", [PAST, 1024])
    cckv = din("cckv", [PAST, KVL]); ckr = din("ckr", [PAST, ROPE])
    w_in = din("w_in", [D, DIN]); w_uq = din("w_uq", [QL, 1536]); w_ukv = din("w_ukv", [KVL, 2048])
    w_out = din("w_out", [D, D]); wg = din("wg", [D, DFF]); wu = din("wu", [D, DFF]); wd = din("wd", [DFF, D])
    g_attn = din("g_attn", [D]); g_ffn = din("g_ffn", [D]); g_fin = din("g_fin", [D])
    g_q = din("g_q", [QL]); g_kv = din("g_kv", [KVL]); g_sub = din("g_sub", [128])
    lam4 = din("lam4", [256]); relb = din("relb", [32, 8])
    rope_o = din("rope_o", [NOWN * 128, 64]); rope_t = din("rope_t", [NOWN * 128, 64])
    rope_m = din("rope_m", [NMETA, 64]); rope_s = din("rope_s", [DSEQ, 64])
    oh = din("oh", [3, 32, 255])
    mask_diag = din("mask_diag", [128, 128]); mask_o0 = din("mask_o0", [128, 128])
    sel = din("sel", [128, 4])
    y_o = dout("y_o", [NOWN * 128, D]); kd_o = dout("kd_o", [NOWN * 128, 1024]); vd_o = dout("vd_o", [NOWN * 128, 1024])
    ckv_o = dout("ckv_o", [NOWN * 128, KVL]); kr_o = dout("kr_o", [NOWN * 128, ROPE])
    kd_m = dout("kd_m", [NMETA, 1024]); vd_m = dout("vd_m", [NMETA, 1024])
    ckv_m = dout("ckv_m", [NMETA, KVL]); kr_m = dout("kr_m", [NMETA, ROPE])
    y_s = dout("y_s", [DSEQ, D]); kd_s = dout("kd_s", [DSEQ, 1024]); vd_s = dout("vd_s", [DSEQ, 1024])
    ckv_s = dout("ckv_s", [DSEQ, KVL]); kr_s = dout("kr_s", [DSEQ, ROPE])
    w_in_b = dscr("w_in_b", [D, DIN]); w_uq_b = dscr("w_uq_b", [QL, 1536]); w_ukv_b = dscr("w_ukv_b", [KVL, 2048])
    w_out_b = dscr("w_out_b", [4, 128, 16, 512]); wg_b = dscr("wg_b", [NFC // 2, 128, 16, 256])
    wu_b = dscr("wu_b", [NFC // 2, 128, 16, 256]); wd_b = dscr("wd_b", [16, 128, NFC, 128])
    ck_b = dscr("ck_b", [PAST, 1024]); cckv_b = dscr("cckv_b", [PAST, KVL]); ckr_b = dscr("ckr_b", [PAST, ROPE])

    def seq_scratch(pfx, nkey, nq):
        return dict(
            KdT=dscr(pfx + "KdT", [NH, 128, nkey]), Vd=dscr(pfx + "Vd", [nkey, 1024]),
            KmT=dscr(pfx + "KmT", [NH, 128, nkey]), Vm=dscr(pfx + "Vm", [nkey, 1024]),
            KrT=dscr(pfx + "KrT", [64, nkey]),
            QdT=dscr(pfx + "QdT", [NH, 128, nq]), QnT=dscr(pfx + "QnT", [NH, 128, nq]),
            QrT=dscr(pfx + "QrT", [NH, 64, nq]), MixT=dscr(pfx + "MixT", [16, 128, nq]),
            nkey=nkey, nq=nq)

    SP_ = seq_scratch("p", NKEY, NOWN * 128)
    SS_ = seq_scratch("s", SKEY, DSEQ)

    arena_ap = nc.alloc_sbuf_tensor("arena", [128, 206 * 1024], U8).ap()
    AR = Arena(arena_ap)
    PS = nc.alloc_psum_tensor("ps", [128, 4096], F32).ap()
    bankB = [Buf(f"bank{i}", excl=True) for i in range(8)]

    def bank(i, dt=F32):
        v = PS[:, i * 512:(i + 1) * 512]
        return v.bitcast(dt) if dt != F32 else v

    rr = {"b": 0, "cp": 0}

    def next_bank(lo=0, hi=8):
        i = lo + rr["b"] % (hi - lo)
        rr["b"] += 1
        return i

    def copy_eng():
        rr["cp"] += 1
        return "dve"

    def copy_op(e, out, in_, r, w):
        if e == "act":
            P.op("act", lambda g: g.activation(out=out, in_=in_, func=AF.Copy), r, w)
        else:
            P.op(e, lambda g: g.tensor_copy(out, in_), r, w)

    ident_f = AR.alloc([128], F32); ident = AR.alloc([128], BF16)
    ones_b = AR.alloc([128], BF16); ones_f = AR.alloc([128], F32)
    lamb = AR.alloc([256], F32); lprod = AR.alloc([2, 64], F32); lsum = AR.alloc([2], F32)
    lexp = AR.alloc([2], F32); lam_t = AR.alloc([1], F32); neg_lam = AR.alloc([1], F32)
    gsub = AR.alloc([1], F32); gsub8 = AR.alloc([1], F32)
    sel_s = AR.alloc([4], F32)
    Bdiag = AR.alloc([NH, 128], BF16); Bo0 = AR.alloc([NH, 128], BF16); Bo1 = AR.alloc([NH, 128], BF16)
    Bmeta = AR.alloc([NH, 128], BF16); Bs7 = AR.alloc([NH, 64], BF16); Bsn = AR.alloc([NH, 64], BF16)
    Mdiag = AR.alloc([128], BF16); Mo0 = AR.alloc([128], BF16)
    b_const = Buf("const")
    persist_mark = AR.off

    def setup():
        R = AR.alloc([8], F32); R8 = AR.alloc([8], F32)
        oh_s = AR.alloc([3, 255], F32)
        md_f = AR.alloc([128], F32); mo_f = AR.alloc([128], F32)
        bR = Buf(); bT = Buf()
        P.op("pool", lambda g: g.memset(ident_f, 0.0), (), [bT])
        P.op("pool", lambda g: g.affine_select(ident_f, ident_f, pattern=[[-1, 128]], compare_op=ALU.not_equal,
                                               fill=1.0, base=0, channel_multiplier=1), [bT], [bT])
        P.op("pool", lambda g: g.memset(ones_f, 1.0), (), [bT])
        P.op("dve", lambda g: g.tensor_copy(ident, ident_f), [bT], [b_const])
        P.op("dve", lambda g: g.tensor_copy(ones_b, ones_f), [bT], [b_const])
        P.dma("sp", lambda g: g.dma_start(out=lamb, in_=lam4.partition_broadcast(128)), (), [bR])
        P.dma("sp", lambda g: g.dma_start(out=gsub, in_=g_sub.rearrange("(p o) -> p o", o=1)), (), [bR])
        P.dma("sp", lambda g: g.dma_start(out=sel_s, in_=sel), (), [bR])
        P.dma("sp", lambda g: g.dma_start(out=R[0:32], in_=relb), (), [bR])
        P.dma("sp", lambda g: g.dma_start(out=oh_s[0:32], in_=oh.rearrange("t b r -> b t r")), (), [bR])
        P.dma("sp", lambda g: g.dma_start(out=md_f, in_=mask_diag), (), [bR])
        P.dma("sp", lambda g: g.dma_start(out=mo_f, in_=mask_o0), (), [bR])
        lv = lamb.rearrange("p (a b c) -> p a b c", a=2, b=2, c=64)
        P.op("dve", lambda g: g.tensor_tensor(out=lprod, in0=lv[:, :, 0, :], in1=lv[:, :, 1, :], op=ALU.mult), [bR], [bT])
        P.op("dve", lambda g: g.reduce_sum(out=lsum, in_=lprod, axis=AX.X), [bT], [bT])
        P.op("act", lambda g: g.activation(out=lexp, in_=lsum, func=AF.Exp), [bT], [bT])
        P.op("dve", lambda g: g.tensor_tensor(out=lam_t, in0=lexp[:, 0:1], in1=lexp[:, 1:2], op=ALU.subtract), [bT], [bT])
        P.op("dve", lambda g: g.tensor_scalar(out=neg_lam, in0=lam_t, scalar1=LAM_INIT, scalar2=-1.0,
                                              op0=ALU.add, op1=ALU.mult), [bT], [b_const])
        P.op("dve", lambda g: g.tensor_scalar(out=gsub8, in0=gsub, scalar1=1.0 - LAM_INIT, scalar2=None,
                                              op0=ALU.mult), [bR], [b_const])
        P.op("dve", lambda g: g.tensor_scalar(out=R8[0:32], in0=R[0:32], scalar1=1.0 / DIFF_SCALE, scalar2=None,
                                              op0=ALU.mult), [bR], [bT])
        views = []
        for t in range(3):
            bA, bB = 2 * t, 2 * t + 1
            for q in range(128):
                bi = bA if q < 64 else bB
                o = bank(bi)[:, (q % 64) * 8:(q % 64) * 8 + 8]
                P.op("pe", lambda g, o=o, t=t, q=q: g.matmul(o, lhsT=oh_s[0:32, t, 127 - q:255 - q], rhs=R8[0:32],
                                                             start=True, stop=True),
                     [bT, bR], [bankB[bi]], signal=(q % 64 == 63))
            views.append((bA, bB))

        def tv(t, h, k0, k1, q0, q1):
            res = []
            for half in range(2):
                a, b = max(q0, half * 64), min(q1, half * 64 + 64)
                if a >= b:
                    continue
                bi = views[t][half]
                v = bank(bi).rearrange("p (q h) -> p q h", h=8)[k0:k1, a - half * 64:b - half * 64, h]
                res.append((v, bi, a, b))
            return res

        P.op("dve", lambda g: g.memset(Bmeta, 0.0), (), [b_const])
        P.op("dve", lambda g: g.memset(Bsn, 0.0), (), [b_const])
        for h in range(NH):
            for v, bi, a, b in tv(0, h, 0, 128, 0, 128):
                P.op("dve", lambda g, v=v, a=a, b=b, h=h: g.tensor_tensor(out=Bdiag[:, h, a:b], in0=v, in1=md_f[:, a:b], op=ALU.add),
                     [bankB[bi], bR], [b_const])
            for v, bi, a, b in tv(1, h, 0, 128, 0, 128):
                P.op("dve", lambda g, v=v, a=a, b=b, h=h: g.scalar_tensor_tensor(out=Bo0[:, h, a:b], in0=v, scalar=sel_s[:, 0:1],
                                                                                 in1=mo_f[:, a:b], op0=ALU.mult, op1=ALU.add),
                     [bankB[bi], bR], [b_const])
                P.op("act", lambda g, v=v, a=a, b=b, h=h: g.activation(out=Bo1[:, h, a:b], in_=v, func=AF.Copy, scale=sel_s[:, 1:2]),
                     [bankB[bi], bR], [b_const])
            for v, bi, a, b in tv(2, h, 0, 16, 0, 128):
                P.op("act", lambda g, v=v, a=a, b=b, h=h: g.activation(out=Bmeta[0:16, h, a:b], in_=v, func=AF.Copy, scale=sel_s[0:16, 2:3]),
                     [bankB[bi], bR], [b_const])
            for v, bi, a, b in tv(1, h, 0, 128, 0, 64):
                P.op("dve", lambda g, v=v, a=a, b=b, h=h: g.tensor_copy(Bs7[:, h, a:b], v), [bankB[bi]], [b_const])
            for v, bi, a, b in tv(0, h, 0, 64, 0, 64):
                P.op("dve", lambda g, v=v, a=a, b=b, h=h: g.tensor_copy(Bsn[0:64, h, a:b], v), [bankB[bi]], [b_const])
        P.op("dve", lambda g: g.tensor_copy(Mdiag, md_f), [bR], [b_const])
        P.op("dve", lambda g: g.tensor_copy(Mo0, mo_f), [bR], [b_const])

    def dma_mid(q, out3, in3, step, r, w):
        m = out3.shape[1]
        for a in range(0, m, step):
            b = min(m, a + step)
            P.dma(q, lambda g, a=a, b=b: g.dma_start(out=out3[:, a:b], in_=in3[:, a:b]), r, w)

    wbuf = {}

    deferred = []

    def cast(name, dst, src, rows_per=128, defer=False, slab=None):
        b = Buf(name)
        wbuf[name] = b
        n = src.shape[0]
        for r in range(0, n, rows_per):
            def go(r=r):
                if slab is None:
                    P.dma("pool", lambda g: g.dma_start(out=dst[r:r + rows_per], in_=src[r:r + rows_per]), (), [b])
                else:
                    w = slab
                    dv = dst.rearrange("a p c w -> p a c w")[:, :, r // 128, :]
                    sv = src[r:r + 128].rearrange("p (a w) -> p a w", w=w)
                    P.dma("pool", lambda g: g.dma_start(out=dv, in_=sv), (), [b])
            if defer:
                deferred.append(go)
            else:
                go()
        return b

    def rstd_from_ss(ss, rstd, n, nfeat, rb):
        P.op("dve", lambda g: g.tensor_scalar(out=rstd[0:n], in0=ss[0:n], scalar1=1.0 / nfeat, scalar2=EPS,
                                              op0=ALU.mult, op1=ALU.add), [rb], [rb])
        P.op("act", lambda g: g.activation(out=rstd[0:n], in_=rstd[0:n], func=AF.Sqrt), [rb], [rb])
        P.op("dve", lambda g: g.reciprocal(out=rstd[0:n], in_=rstd[0:n]), [rb], [rb])

    def transposes(src_fn, nblk, n, cols, dst, dstB, srcB, dt=BF16, idn=None):
        idn = ident if dt == BF16 else ident_f
        per = 8 if dt == BF16 else 4
        for j0 in range(0, nblk, per):
            bi = next_bank()
            bv = bank(bi, dt).rearrange("p (j t) -> p j t", t=128)
            m = min(per, nblk - j0)
            for j in range(j0, j0 + m):
                P.op("pe", lambda g, j=j, j0=j0, bv=bv: g.transpose(out=bv[0:cols, j - j0, 0:n], in_=src_fn(j), identity=idn[0:n, 0:n]),
                     [srcB, b_const], [bankB[bi]], signal=(j == j0 + m - 1))
            copy_op(copy_eng(), dst[0:cols, j0:j0 + m, 0:n], bv[0:cols, 0:m, 0:n], [bankB[bi]], [dstB])

    def norm_T(xs, bx, n, gbc, bg, hb, bhb, hT, bhT, ss, rstd, bst):
        import os
        mf = int(os.environ.get("MK_F", "9"))
        if mf == 0:
            return
        P.op("act", lambda g: g.activation(out=hb[0:n], in_=xs[0:n], func=AF.Square, accum_out=ss[0:n]), [bx], [bhb, bst])
        if mf == 1:
            return
        rstd_from_ss(ss, rstd, n, D, bst)
        if mf == 2:
            return
        P.op("dve", lambda g: g.scalar_tensor_tensor(out=hb[0:n], in0=xs[0:n], scalar=rstd[0:n], in1=gbc[0:n],
                                                     op0=ALU.mult, op1=ALU.mult), [bx, bst, bg], [bhb])

    def norm_T2(n, hb, bhb, hT, bhT):
        transposes(lambda j: hb[0:n, j * 128:(j + 1) * 128], 16, n, 128, hT, bhT, bhb)

    def mm_group(bi, n, ncols, lhs_fn, rhs_fn, nk, rB, dst=None):
        o = bank(bi)[0:n, 0:ncols] if dst is None else dst
        for c in range(nk):
            P.op("pe", lambda g, c=c: g.matmul(o, lhsT=lhs_fn(c), rhs=rhs_fn(c), start=(c == 0), stop=(c == nk - 1)),
                 rB, [bankB[bi]], signal=(c == nk - 1))

    def rope_tok(src_fn, dst_fn, cos, sin, n, tmp, rB, wB, tB):
        x1, x2 = src_fn(0), src_fn(1)
        t1, t2 = tmp
        P.op("dve", lambda g: g.tensor_tensor(out=t1, in0=x1, in1=cos, op=ALU.mult), rB, [tB])
        P.op("dve", lambda g: g.tensor_tensor(out=t2, in0=x2, in1=sin, op=ALU.mult), rB, [tB])
        P.op("dve", lambda g: g.tensor_tensor(out=dst_fn(0), in0=t1, in1=t2, op=ALU.subtract), [tB], [wB])
        P.op("dve", lambda g: g.tensor_tensor(out=t1, in0=x1, in1=sin, op=ALU.mult), rB + [wB], [tB])
        P.op("dve", lambda g: g.tensor_tensor(out=t2, in0=x2, in1=cos, op=ALU.mult), rB, [tB])
        P.op("dve", lambda g: g.tensor_tensor(out=dst_fn(1), in0=t1, in1=t2, op=ALU.add), [tB], [wB])

    def passA():
        AR.off = persist_mark
        WinK = AR.alloc([16, 2368], BF16); Wukv = AR.alloc([2, 2048], BF16)
        g_bc = AR.alloc([D], F32); gkv_bc = AR.alloc([KVL], F32)
        xs2 = [AR.alloc([D], F32) for _ in range(2)]
        hb = AR.alloc([D], BF16)
        hT2 = [AR.alloc([16, 128], BF16) for _ in range(2)]
        kd_f = AR.alloc([1024], F32); vd_f = AR.alloc([1024], F32)
        kd_b = AR.alloc([1024], BF16); vd_b = AR.alloc([1024], BF16)
        ckv_f = AR.alloc([KVL], F32); ckv_b = AR.alloc([KVL], BF16)
        kr_f = AR.alloc([64], F32); kr_b = AR.alloc([64], BF16)
        rope2 = [AR.alloc([64], F32) for _ in range(2)]
        rt = [AR.alloc([32], F32) for _ in range(2)]
        junk = AR.alloc([KVL], F32)
        ss = AR.alloc([1], F32); rstd = AR.alloc([1], F32); ssk = AR.alloc([1], F32); rstdk = AR.alloc([1], F32)
        KdT_st = AR.alloc([NH, 128], BF16); ckvT = AR.alloc([2, 128], BF16); KrT_st = AR.alloc([1, 128], BF16)
        KmT_st = AR.alloc([NH, 128], BF16); vm_b = AR.alloc([1024], BF16)
        bW = Buf(); bg = Buf()
        bx2 = [Buf(), Buf()]; bhb = Buf(); bhT2 = [Buf(), Buf()]
        bkdf = Buf(); bvdf = Buf(); bkdb = Buf(); bvdb = Buf(); bckf = Buf(); bckb = Buf(); bkrf = Buf(); bkrb = Buf()
        brope2 = [Buf(), Buf()]; brt = Buf(); bjunk = Buf(); bst = Buf(); bstk = Buf()
        bKdT = Buf(); bckvT = Buf(); bKrT = Buf(); bKmT = Buf(); bvmb = Buf()
        wv = w_in_b.rearrange("(c p) n -> p c n", p=128)
        dma_mid("sp", WinK[:, :, 0:2048], wv[:, :, OFF_K:OFF_CQ], 2, [wbuf["w_in"]], [bW])
        dma_mid("sp", WinK[:, :, 2048:2368], wv[:, :, OFF_CKV:DIN], 4, [wbuf["w_in"]], [bW])
        P.dma("sp", lambda g: g.dma_start(out=Wukv, in_=w_ukv_b.rearrange("(c p) n -> p c n", p=128)), [wbuf["w_ukv"]], [bW])
        P.dma("sp", lambda g: g.dma_start(out=g_bc, in_=g_attn.partition_broadcast(128)), (), [bg])
        P.dma("sp", lambda g: g.dma_start(out=gkv_bc, in_=g_kv.partition_broadcast(128)), (), [bg])

        tiles = []
        for j in range(PAST // 128):
            tiles.append(dict(cache=j, n=128, seq=SS_, koff=j * 128))
        tiles.append(dict(x=xmeta, n=NMETA, seq=SP_, koff=2 * NOWN * 128, rope=rope_m, outs=(kd_m, vd_m, ckv_m, kr_m)))
        tiles.append(dict(x=xsam, n=DSEQ, seq=SS_, koff=PAST, rope=rope_s, outs=(kd_s, vd_s, ckv_s, kr_s)))
        for i in range(NOWN):
            r = slice(i * 128, (i + 1) * 128)
            tiles.append(dict(x=xo[r], n=128, seq=SP_, koff=i * 128, rope=rope_o[r],
                              outs=(kd_o[r], vd_o[r], ckv_o[r], kr_o[r])))
            tiles.append(dict(x=xt[r], n=128, seq=SP_, koff=NOWN * 128 + i * 128, rope=rope_t[r], outs=None))

        import os
        flt = os.environ.get("MK_A", "")
        if flt:
            keep = []
            for t in tiles:
                kind = "cache" if "cache" in t else ("meta" if t["n"] == NMETA else ("sam" if t["n"] == DSEQ else "own"))
                if kind in flt.split(","):
                    keep.append(t)
            tiles[:] = keep[:int(os.environ.get("MK_AN", "100"))]

        def load(idx):
            t = tiles[idx]
            if "cache" in t:
                return
            p = t["slot"]
            n = t["n"]
            P.dma("sp", lambda g: g.dma_start(out=xs2[p][0:n], in_=t["x"]), (), [bx2[p]])
            P.dma("sp", lambda g: g.dma_start(out=rope2[p][0:n], in_=t["rope"]), (), [brope2[p]])

        def front(idx):
            t = tiles[idx]
            if "cache" in t:
                return
            p = t["slot"]
            norm_T(xs2[p], bx2[p], t["n"], g_bc, bg, hb, bhb, hT2[p], bhT2[p], ss, rstd, bst)

        def front_T(idx):
            if idx >= len(tiles):
                return
            t = tiles[idx]
            if "cache" in t:
                return
            p = t["slot"]
            norm_T2(t["n"], hb, bhb, hT2[p], bhT2[p])

        cut = int(os.environ.get("MK_CUT", "9"))

        def back(idx, mid=lambda: None):
            t = tiles[idx]
            n = t["n"]; seq = t["seq"]; koff = t["koff"]
            if cut == 0:
                return
            if "cache" in t:
                mid()
                j = t["cache"]
                r = slice(j * 128, (j + 1) * 128)
                P.dma("sp", lambda g: g.dma_start(out=kd_b[0:n], in_=ck_b[r]), [wbuf["ck"]], [bkdb])
                P.dma("sp", lambda g: g.dma_start(out=ckv_b[0:n], in_=cckv_b[r]), [wbuf["cckv"]], [bckb])
                P.dma("sp", lambda g: g.dma_start(out=kr_b[0:n], in_=ckr_b[r]), [wbuf["ckr"]], [bkrb])
            else:
                p = t["slot"]
                hT = hT2[p]
                rB = [bhT2[p], bW]
                for gi, (c0, dstf, dstb, bf_, bb_) in enumerate([(0, kd_f, kd_b, bkdf, bkdb), (512, kd_f, kd_b, bkdf, bkdb),
                                                                  (1024, vd_f, vd_b, bvdf, bvdb), (1536, vd_f, vd_b, bvdf, bvdb)]):
                    bi = next_bank()
                    mm_group(bi, n, 512, lambda c: hT[:, c, 0:n], lambda c, c0=c0: WinK[:, c, c0:c0 + 512], 16, rB)
                    lc = c0 % 1024
                    P.op("act", lambda g, bi=bi, dstf=dstf, lc=lc: g.activation(out=dstf[0:n, lc:lc + 512], in_=bank(bi)[0:n, 0:512], func=AF.Copy),
                         [bankB[bi]], [bf_])
                    P.op("dve", lambda g, bi=bi, dstb=dstb, lc=lc: g.tensor_copy(dstb[0:n, lc:lc + 512], bank(bi)[0:n, 0:512]),
                         [bankB[bi]], [bb_])
                if cut == 1:
                    return
                bi = next_bank()
                mm_group(bi, n, 320, lambda c: hT[:, c, 0:n], lambda c: WinK[:, c, 2048:2368], 16, rB)
                bk = bank(bi)
                mid()
                P.op("act", lambda g: g.activation(out=junk[0:n], in_=bk[0:n, 0:KVL], func=AF.Square, accum_out=ssk[0:n]),
                     [bankB[bi]], [bjunk, bstk])
                rstd_from_ss(ssk, rstdk, n, KVL, bstk)
                P.op("dve", lambda g: g.scalar_tensor_tensor(out=ckv_f[0:n], in0=bk[0:n, 0:KVL], scalar=rstdk[0:n], in1=gkv_bc[0:n],
                                                             op0=ALU.mult, op1=ALU.mult), [bankB[bi], bstk, bg], [bckf])
                P.op("act", lambda g: g.activation(out=ckv_b[0:n], in_=ckv_f[0:n], func=AF.Copy), [bckf], [bckb])
                rp = rope2[p]
                rope_tok(lambda hf: bk[0:n, KVL + 32 * hf:KVL + 32 * hf + 32], lambda hf: kr_f[0:n, 32 * hf:32 * hf + 32],
                         rp[0:n, 0:32], rp[0:n, 32:64], n, (rt[0][0:n], rt[1][0:n]), [bankB[bi], brope2[p]], bkrf, brt)
                P.op("act", lambda g: g.activation(out=kr_b[0:n], in_=kr_f[0:n], func=AF.Copy), [bkrf], [bkrb])
                if t["outs"] is not None:
                    o_kd, o_vd, o_ckv, o_kr = t["outs"]
                    P.dma("pool", lambda g: g.dma_start(out=o_kd, in_=kd_f[0:n]), [bkdf], (), is_output=True)
                    P.dma("pool", lambda g: g.dma_start(out=o_vd, in_=vd_f[0:n]), [bvdf], (), is_output=True)
                    P.dma("pool", lambda g: g.dma_start(out=o_ckv, in_=ckv_f[0:n]), [bckf], (), is_output=True)
                    P.dma("pool", lambda g: g.dma_start(out=o_kr, in_=kr_f[0:n]), [bkrf], (), is_output=True)
                P.dma("pool", lambda g: g.dma_start(out=seq["Vd"][koff:koff + n, :], in_=vd_b[0:n]), [bvdb], ())
            if cut == 2:
                return
            transposes(lambda j: kd_b[0:n, j * 128:(j + 1) * 128], NH, n, 128, KdT_st, bKdT, bkdb)
            P.dma("pool", lambda g: g.dma_start(out=seq["KdT"][:, :, koff:koff + n].rearrange("h p k -> p h k"), in_=KdT_st[:, :, 0:n]),
                  [bKdT], ())
            transposes(lambda j: ckv_b[0:n, j * 128:(j + 1) * 128], 2, n, 128, ckvT, bckvT, bckb)
            transposes(lambda j: kr_b[0:n, 0:64], 1, n, 64, KrT_st, bKrT, bkrb)
            P.dma("pool", lambda g: g.dma_start(out=seq["KrT"][:, koff:koff + n], in_=KrT_st[0:64, 0, 0:n]), [bKrT], ())
            wk = Wukv.rearrange("p c (h x) -> p c h x", x=256)
            for hh in range(2):
                bi = next_bank()
                for h4 in range(4):
                    h = hh * 4 + h4
                    mm_group(bi, 128, n, lambda c, h=h: wk[:, c, h, 0:128], lambda c: ckvT[:, c, 0:n], 2, [bckvT, bW],
                             dst=bank(bi)[:, h4 * 128:h4 * 128 + n])
                copy_op(copy_eng(), KmT_st[:, hh * 4:hh * 4 + 4, 0:n],
                        bank(bi).rearrange("p (j t) -> p j t", t=128)[:, :, 0:n], [bankB[bi]], [bKmT])
            P.dma("pool", lambda g: g.dma_start(out=seq["KmT"][:, :, koff:koff + n].rearrange("h p k -> p h k"), in_=KmT_st[:, :, 0:n]),
                  [bKmT], ())
            for hh in range(2):
                bi = next_bank()
                mm_group(bi, n, 512, lambda c: ckvT[:, c, 0:n], lambda c, hh=hh: wk[:, c, hh * 4:hh * 4 + 4, 128:256], 2, [bckvT, bW])
                copy_op(copy_eng(), vm_b[0:n, hh * 512:(hh + 1) * 512], bank(bi)[0:n, 0:512], [bankB[bi]], [bvmb])
            P.dma("pool", lambda g: g.dma_start(out=seq["Vm"][koff:koff + n, :], in_=vm_b[0:n]), [bvmb], ())

        slot = 0
        for t in tiles:
            if "cache" not in t:
                t["slot"] = slot
                slot ^= 1
        nt = len(tiles)
        load(0)
        front(0)
        front_T(0)
        for i in range(nt):
            if i + 1 < nt:
                load(i + 1)
                front(i + 1)
            back(i, mid=lambda i=i: front_T(i + 1))
            for _ in range(3):
                if deferred:
                    deferred.pop(0)()
        while deferred:
            deferred.pop(0)()

    def passB():
        AR.off = persist_mark
        WinQ = AR.alloc([16, 1536], BF16); Wuq = AR.alloc([4, 1536], BF16)
        g_bc = AR.alloc([D], F32); gq_bc = AR.alloc([QL], F32)
        xs2 = [AR.alloc([D], F32) for _ in range(2)]
        hb = AR.alloc([D], BF16)
        hT2 = [AR.alloc([16, 128], BF16) for _ in range(2)]
        qd_b = AR.alloc([1024], BF16); cq_b = AR.alloc([QL], BF16); cqT = AR.alloc([4, 128], BF16)
        qn_b = AR.alloc([NH, 128], BF16); qr_f = AR.alloc([NH, 64], F32); qr_b = AR.alloc([NH, 64], BF16)
        rope2 = [AR.alloc([64], F32) for _ in range(2)]
        rt = [AR.alloc([2, 32], F32) for _ in range(2)]
        junk = AR.alloc([QL], F32)
        ss = AR.alloc([1], F32); rstd = AR.alloc([1], F32); ssq = AR.alloc([1], F32); rstdq = AR.alloc([1], F32)
        QdT_st = AR.alloc([NH, 128], BF16); QnT_st = AR.alloc([NH, 128], BF16); QrT_st = AR.alloc([NH, 128], BF16)
        bW = Buf(); bg = Buf(); bx2 = [Buf(), Buf()]; bhb = Buf(); bhT2 = [Buf(), Buf()]
        bqdb = Buf(); bcqb = Buf(); bcqT = Buf(); bqnb = Buf(); bqrf = Buf(); bqrb = Buf()
        brope2 = [Buf(), Buf()]; brt = Buf(); bjunk = Buf(); bst = Buf(); bstq = Buf()
        bQdT = Buf(); bQnT = Buf(); bQrT = Buf()
        wv = w_in_b.rearrange("(c p) n -> p c n", p=128)
        dma_mid("sp", WinQ[:, :, 0:1024], wv[:, :, 0:OFF_K], 4, [wbuf["w_in"]], [bW])
        dma_mid("sp", WinQ[:, :, 1024:1536], wv[:, :, OFF_CQ:OFF_CKV], 4, [wbuf["w_in"]], [bW])
        P.dma("sp", lambda g: g.dma_start(out=Wuq, in_=w_uq_b.rearrange("(c p) n -> p c n", p=128)), [wbuf["w_uq"]], [bW])
        P.dma("sp", lambda g: g.dma_start(out=g_bc, in_=g_attn.partition_broadcast(128)), (), [bg])
        P.dma("sp", lambda g: g.dma_start(out=gq_bc, in_=g_q.partition_broadcast(128)), (), [bg])
        tiles = [dict(x=xsam, n=DSEQ, seq=SS_, qoff=0, rope=rope_s)]
        for i in range(NOWN):
            r = slice(i * 128, (i + 1) * 128)
            tiles.append(dict(x=xo[r], n=128, seq=SP_, qoff=i * 128, rope=rope_o[r]))

        import os
        if os.environ.get("MK_B", "") == "s":
            tiles[:] = tiles[:1]

        def load(idx):
            t = tiles[idx]; p = idx % 2; n = t["n"]
            P.dma("sp", lambda g: g.dma_start(out=xs2[p][0:n], in_=t["x"]), (), [bx2[p]])
            P.dma("sp", lambda g: g.dma_start(out=rope2[p][0:n], in_=t["rope"]), (), [brope2[p]])

        def front(idx):
            t = tiles[idx]; p = idx % 2
            norm_T(xs2[p], bx2[p], t["n"], g_bc, bg, hb, bhb, hT2[p], bhT2[p], ss, rstd, bst)

        def front_T(idx):
            if idx >= len(tiles):
                return
            t = tiles[idx]; p = idx % 2
            norm_T2(t["n"], hb, bhb, hT2[p], bhT2[p])

        def back(idx, mid=lambda: None):
            t = tiles[idx]; p = idx % 2; n = t["n"]; seq = t["seq"]; qoff = t["qoff"]
            hT = hT2[p]; rB = [bhT2[p], bW]
            for c0 in (0, 512):
                bi = next_bank()
                mm_group(bi, n, 512, lambda c: hT[:, c, 0:n], lambda c, c0=c0: WinQ[:, c, c0:c0 + 512], 16, rB)
                copy_op(copy_eng(), qd_b[0:n, c0:c0 + 512], bank(bi)[0:n, 0:512], [bankB[bi]], [bqdb])
            transposes(lambda j: qd_b[0:n, j * 128:(j + 1) * 128], NH, n, 128, QdT_st, bQdT, bqdb)
            P.dma("pool", lambda g: g.dma_start(out=seq["QdT"][:, :, qoff:qoff + n].rearrange("h p k -> p h k"), in_=QdT_st[:, :, 0:n]),
                  [bQdT], ())
            bi = next_bank()
            mm_group(bi, n, 512, lambda c: hT[:, c, 0:n], lambda c: WinQ[:, c, 1024:1536], 16, rB)
            bk = bank(bi)
            mid()
            P.op("act", lambda g: g.activation(out=junk[0:n], in_=bk[0:n, 0:QL], func=AF.Square, accum_out=ssq[0:n]),
                 [bankB[bi]], [bjunk, bstq])
            rstd_from_ss(ssq, rstdq, n, QL, bstq)
            P.op("dve", lambda g: g.scalar_tensor_tensor(out=cq_b[0:n], in0=bk[0:n, 0:QL], scalar=rstdq[0:n], in1=gq_bc[0:n],
                                                         op0=ALU.mult, op1=ALU.mult), [bankB[bi], bstq, bg], [bcqb])
            transposes(lambda j: cq_b[0:n, j * 128:(j + 1) * 128], 4, n, 128, cqT, bcqT, bcqb)
            rp = rope2[p]
            for gq in range(4):
                bi = next_bank()
                mm_group(bi, n, 384, lambda c: cqT[:, c, 0:n], lambda c, gq=gq: Wuq[:, c, gq * 384:(gq + 1) * 384], 4, [bcqT, bW])
                bv = bank(bi)[:, 0:384].rearrange("p (h x) -> p h x", x=192)
                P.op("act", lambda g, bv=bv, gq=gq: g.activation(out=qn_b[0:n, 2 * gq:2 * gq + 2, :], in_=bv[0:n, :, 0:128], func=AF.Copy),
                     [bankB[bi]], [bqnb])
                cosb = rp[0:n, 0:32].unsqueeze(1).to_broadcast([n, 2, 32])
                sinb = rp[0:n, 32:64].unsqueeze(1).to_broadcast([n, 2, 32])
                rope_tok(lambda hf, bv=bv: bv[0:n, :, 128 + 32 * hf:128 + 32 * hf + 32],
                         lambda hf, gq=gq: qr_f[0:n, 2 * gq:2 * gq + 2, 32 * hf:32 * hf + 32],
                         cosb, sinb, n, (rt[0][0:n], rt[1][0:n]), [bankB[bi], brope2[p]], bqrf, brt)
            P.op("act", lambda g: g.activation(out=qr_b[0:n], in_=qr_f[0:n], func=AF.Copy), [bqrf], [bqrb])
            transposes(lambda j: qn_b[0:n, j, :], NH, n, 128, QnT_st, bQnT, bqnb)
            P.dma("pool", lambda g: g.dma_start(out=seq["QnT"][:, :, qoff:qoff + n].rearrange("h p k -> p h k"), in_=QnT_st[:, :, 0:n]),
                  [bQnT], ())
            transposes(lambda j: qr_b[0:n, j, :], NH, n, 64, QrT_st, bQrT, bqrb)
            P.dma("pool", lambda g: g.dma_start(out=seq["QrT"][:, :, qoff:qoff + n].rearrange("h p k -> p h k"), in_=QrT_st[0:64, :, 0:n]),
                  [bQrT], ())

        nt = len(tiles)
        load(0)
        front(0)
        front_T(0)
        for i in range(nt):
            if i + 1 < nt:
                load(i + 1)
                front(i + 1)
            back(i, mid=lambda i=i: front_T(i + 1))

    def attention():
        AR.off = persist_mark
        NT = NKEY // 128 + 1
        NKP = NKEY - NMETA + 128
        KT2 = [AR.alloc([NKP], BF16) for _ in range(2)]
        V2 = [AR.alloc([NT, 128], BF16) for _ in range(2)]
        Q2 = [[AR.alloc([NOWN * 128], BF16) for _ in range(2)] for _ in range(2)]
        Qn2 = [AR.alloc([NOWN * 128], BF16) for _ in range(2)]
        QR2 = [AR.alloc([NOWN * 128], BF16) for _ in range(2)]
        KrT = AR.alloc([NKP], BF16)
        PT = [AR.alloc([512], BF16) for _ in range(4)]
        onesM = AR.alloc([128], BF16); onesS = AR.alloc([128], BF16)
        bz = Buf()
        for par_ in range(2):
            P.op("pool", lambda g, par_=par_: g.memset(KT2[par_], 0.0), (), [bz])
            P.op("pool", lambda g, par_=par_: g.memset(V2[par_], 0.0), (), [bz])
            P.op("pool", lambda g, par_=par_: g.memset(Q2[par_][0], 0.0), (), [bz])
            P.op("pool", lambda g, par_=par_: g.memset(Q2[par_][1], 0.0), (), [bz])
            P.op("pool", lambda g, par_=par_: g.memset(QR2[par_], 0.0), (), [bz])
        P.op("pool", lambda g: g.memset(KrT, 0.0), (), [bz])
        for i_ in range(4):
            P.op("pool", lambda g, i_=i_: g.memset(PT[i_], 0.0), (), [bz])
        P.op("pool", lambda g: g.memset(onesM, 0.0), (), [bz])
        P.op("pool", lambda g: g.memset(onesS, 0.0), (), [bz])
        P.op("pool", lambda g: g.memset(onesM[0:NMETA], 1.0), [bz], [bz])
        P.op("pool", lambda g: g.memset(onesS[0:DSEQ], 1.0), [bz], [bz])
        P.barrier()
        rl = AR.alloc([512], F32); on0 = AR.alloc([512], F32); on1 = AR.alloc([512], F32)
        od = AR.alloc([512], F32); sq = AR.alloc([512], F32); rs = AR.alloc([512], F32)
        mix_st = [AR.alloc([512], BF16) for _ in range(2)]
        bKT2 = [Buf(), Buf()]; bV2 = [Buf(), Buf()]; bQ2 = [Buf(), Buf()]; bQR2 = [Buf(), Buf()]; bKrT = Buf()
        bPT = [Buf() for _ in range(4)]
        brl = Buf(); bon0 = Buf(); bon1 = Buf(); bod = Buf(); bsq = Buf(); brs = Buf(); bmix = [Buf(), Buf()]
        SETS = [dict(S=(0, 1), O=2, L=3, PT=(0, 1)), dict(S=(4, 5), O=6, L=7, PT=(2, 3))]

        def tile_list(kind, G, mla):
            tl = []
            if kind == "p":
                nq = 512
                tl.append(dict(koff=2 * NOWN * 128, nk=NMETA, c0=0,
                               bias=([(Bmeta, 0, 128, NMETA)] if (G == 0 and not mla) else [])))
                for j in range(4 * G + 4):
                    c0 = max(j - 4 * G, 0) * 128
                    b = []
                    if j >= 4 * G:
                        b.append((Mdiag if mla else Bdiag, c0, 128, 128))
                    tl.append(dict(koff=j * 128, nk=128, c0=c0, bias=b))
                    b = []
                    if j >= 4 * G:
                        b.append((Mo0 if mla else Bo0, c0, 128, 128))
                    if (not mla) and 4 * G <= j + 1 <= 4 * G + 3:
                        b.append((Bo1, (j + 1 - 4 * G) * 128, 128, 128))
                    tl.append(dict(koff=NOWN * 128 + j * 128, nk=128, c0=c0, bias=b))
                tl.sort(key=lambda d: d["c0"])
            else:
                nq = DSEQ
                for j in range(PAST // 128):
                    tl.append(dict(koff=j * 128, nk=128, c0=0, bias=([(Bs7, 0, 64, 128)] if (j == 7 and not mla) else [])))
                tl.append(dict(koff=PAST, nk=DSEQ, c0=0, bias=([(Bsn, 0, 64, 64)] if not mla else [])))
            return tl, nq

        steps = []
        maps = []
        mi = 0
        import os
        att_f = os.environ.get("MK_ATT", "sp")
        for kind, seq in (("s", SS_), ("p", SP_)):
            if kind not in att_f:
                continue
            nkey = seq["nkey"]; nqt = seq["nq"]
            ngroups = 1 if kind == "s" else 4
            for hi in range(2 * NH):
                mla = hi >= NH
                h = hi % NH
                par = hi % 2
                maps_head = []
                for G in range(ngroups):
                    for comp in ((0,) if mla else (0, 1)):
                        tl, nq = tile_list(kind, G, mla)
                        m = dict(kind=kind, seq=seq, mla=mla, h=h, par=par, G=G, comp=comp, tiles=tl, nq=nq,
                                 qbase=G * 512 if kind == "p" else 0, set=SETS[mi % 2], first_of_head=(G == 0 and comp == 0),
                                 nkey=nkey, nqt=nqt)
                        mi += 1
                        maps.append(m)
                        for ti in range(len(tl)):
                            steps.append((m, ti))

        def load_head(m):
            seq = m["seq"]; h = m["h"]; par = m["par"]; nkey = m["nkey"]; nqt = m["nqt"]; kind = m["kind"]
            nfull = (nkey // 128)
            rem = nkey - nfull * 128
            if m["mla"]:
                P.dma("sp", lambda g: g.dma_start(out=KT2[par][:, 0:nkey], in_=seq["KmT"][h]), (), [bKT2[par]])
                vsrc = seq["Vm"]
                P.dma("sp", lambda g: g.dma_start(out=Qn2[par][:, 0:nqt], in_=seq["QnT"][h]), (), [bQ2[par]])
                P.dma("sp", lambda g: g.dma_start(out=QR2[par][0:64, 0:nqt], in_=seq["QrT"][h]), (), [bQR2[par]])
                if h == 0:
                    P.dma("sp", lambda g: g.dma_start(out=KrT[0:64, 0:nkey], in_=seq["KrT"]), (), [bKrT])
            else:
                P.dma("sp", lambda g: g.dma_start(out=KT2[par][:, 0:nkey], in_=seq["KdT"][h]), (), [bKT2[par]])
                vsrc = seq["Vd"]
                P.dma("sp", lambda g: g.dma_start(out=Q2[par][0][0:64, 0:nqt], in_=seq["QdT"][h][0:64]), (), [bQ2[par]])
                P.dma("sp", lambda g: g.dma_start(out=Q2[par][1][64:128, 0:nqt], in_=seq["QdT"][h][64:128]), (), [bQ2[par]])
            dma_mid("sp", V2[par][:, 0:nfull, :], vsrc[0:nfull * 128, h * 128:(h + 1) * 128].rearrange("(t p) e -> p t e", p=128),
                    4, (), [bV2[par]])
            P.dma("sp", lambda g: g.dma_start(out=V2[par][0:rem, nfull, :], in_=vsrc[nfull * 128:nkey, h * 128:(h + 1) * 128]),
                  (), [bV2[par]])

        def stage1(m, ti):
            kt = m["tiles"][ti]; st = m["set"]; par = m["par"]
            nk = kt["nk"]; c0 = kt["c0"]; nq = m["nq"]; koff = kt["koff"]; qb = m["qbase"]
            Sb = st["S"][ti % 2]
            o = bank(Sb)[:, c0:nq]
            nb = len(kt["bias"])
            if m["mla"]:
                P.op("pe", lambda g: g.matmul(o, lhsT=KT2[par][:, koff:koff + 128], rhs=Qn2[par][:, qb + c0:qb + nq], start=True, stop=False),
                     [bKT2[par], bQ2[par]], [bankB[Sb]], signal=False)
                P.op("pe", lambda g: g.matmul(o, lhsT=KrT[:, koff:koff + 128], rhs=QR2[par][:, qb + c0:qb + nq], start=False, stop=(nb == 0)),
                     [bKrT, bQR2[par]], [bankB[Sb]], signal=(nb == 0))
            else:
                cc = m["comp"]
                P.op("pe", lambda g: g.matmul(o, lhsT=KT2[par][:, koff:koff + 128], rhs=Q2[par][cc][:, qb + c0:qb + nq], start=True, stop=(nb == 0)),
                     [bKT2[par], bQ2[par]], [bankB[Sb]], signal=(nb == 0))
            for bi_, (tab, coff, ncb, nkb) in enumerate(kt["bias"]):
                rhs = tab[:, 0:ncb] if m["mla"] else tab[:, m["h"], 0:ncb]
                P.op("pe", lambda g, rhs=rhs, coff=coff, ncb=ncb, last=(bi_ == nb - 1): g.matmul(
                    bank(Sb)[:, coff:coff + ncb], lhsT=ident, rhs=rhs, start=False, stop=last),
                    [b_const], [bankB[Sb]], signal=(bi_ == nb - 1))
            pt = st["PT"][ti % 2]
            scale = MLA_SCALE if m["mla"] else DIFF_SCALE
            P.op("act", lambda g: g.activation(out=PT[pt][:, c0:nq], in_=o, func=AF.Exp, scale=scale), [bankB[Sb]], [bPT[pt]])

        def stage2(m, ti):
            kt = m["tiles"][ti]; st = m["set"]; par = m["par"]
            nk = kt["nk"]; c0 = kt["c0"]; nq = m["nq"]; koff = kt["koff"]
            pt = st["PT"][ti % 2]
            nt_ = len(m["tiles"])
            first = (ti == 0); last = (ti == nt_ - 1)
            slot = koff // 128
            ones_t = ones_b if nk == 128 else (onesM if nk == NMETA else onesS)
            P.op("pe", lambda g: g.matmul(bank(st["O"])[:, c0:nq], lhsT=V2[par][:, slot, :], rhs=PT[pt][:, c0:nq], start=first, stop=last),
                 [bV2[par], bPT[pt]], [bankB[st["O"]]], signal=False)
            P.op("pe", lambda g: g.matmul(bank(st["L"])[:, c0:nq], lhsT=ones_t, rhs=PT[pt][:, c0:nq], start=first, stop=last),
                 [b_const, bPT[pt]], [bankB[st["L"]]], signal=True)
            if last:
                finish(m)

        def finish(m):
            st = m["set"]; nq = m["nq"]; seq = m["seq"]; qb = m["qbase"]; h = m["h"]
            O = bank(st["O"])[:, 0:nq]; L = bank(st["L"])[:, 0:nq]
            bO = bankB[st["O"]]; bL = bankB[st["L"]]
            P.op("dve", lambda g: g.reciprocal(out=rl[:, 0:nq], in_=L), [bL], [brl])
            if m["mla"]:
                ms = m["G"] % 2
                P.op("dve", lambda g: g.tensor_tensor(out=mix_st[ms][:, 0:nq], in0=O, in1=rl[:, 0:nq], op=ALU.mult), [bO, brl], [bmix[ms]])
                P.dma("pool", lambda g: g.dma_start(out=seq["MixT"][NH + h][:, qb:qb + nq], in_=mix_st[ms][:, 0:nq]), [bmix[ms]], ())
                return
            if m["comp"] == 0:
                P.op("dve", lambda g: g.tensor_tensor(out=on0[:, 0:nq], in0=O, in1=rl[:, 0:nq], op=ALU.mult), [bO, brl], [bon0])
                return
            P.op("dve", lambda g: g.tensor_tensor(out=on1[:, 0:nq], in0=O, in1=rl[:, 0:nq], op=ALU.mult), [bO, brl], [bon1])
            P.op("dve", lambda g: g.scalar_tensor_tensor(out=od[:, 0:nq], in0=on1[:, 0:nq], scalar=neg_lam, in1=on0[:, 0:nq],
                                                         op0=ALU.mult, op1=ALU.add), [bon0, bon1, b_const], [bod])
            P.op("act", lambda g: g.activation(out=sq[:, 0:nq], in_=od[:, 0:nq], func=AF.Square), [bod], [bsq])
            Sb = st["S"][0]
            P.op("pe", lambda g: g.matmul(bank(Sb)[:, 0:nq], lhsT=ones_f, rhs=sq[:, 0:nq], start=True, stop=True), [bsq, b_const], [bankB[Sb]])
            P.op("dve", lambda g: g.tensor_scalar(out=rs[:, 0:nq], in0=bank(Sb)[:, 0:nq], scalar1=1.0 / 128, scalar2=EPS,
                                                  op0=ALU.mult, op1=ALU.add), [bankB[Sb]], [brs])
            P.op("act", lambda g: g.activation(out=rs[:, 0:nq], in_=rs[:, 0:nq], func=AF.Sqrt), [brs], [brs])
            P.op("dve", lambda g: g.reciprocal(out=rs[:, 0:nq], in_=rs[:, 0:nq]), [brs], [brs])
            ms = m["G"] % 2
            P.op("dve", lambda g: g.scalar_tensor_tensor(out=mix_st[ms][:, 0:nq], in0=od[:, 0:nq], scalar=gsub8, in1=rs[:, 0:nq],
                                                         op0=ALU.mult, op1=ALU.mult), [bod, brs, b_const], [bmix[ms]])
            P.dma("pool", lambda g: g.dma_start(out=seq["MixT"][h][:, qb:qb + nq], in_=mix_st[ms][:, 0:nq]), [bmix[ms]], ())

        head_first = [i for i, mm in enumerate(maps) if mm["first_of_head"]]
        load_head(maps[head_first[0]])
        nxt = {head_first[k]: head_first[k + 1] for k in range(len(head_first) - 1)}
        seen = set()
        prev = None
        for (m, ti) in steps:
            stage1(m, ti)
            if prev is not None:
                stage2(*prev)
            prev = (m, ti)
            idm = id(m)
            if idm not in seen:
                seen.add(idm)
                k = maps.index(m)
                if k in nxt:
                    load_head(maps[nxt[k]])
        stage2(*prev)

    def phase3():
        AR.off = persist_mark
        TMAX = 576
        g_bc = AR.alloc([D], F32)
        mixh = AR.alloc([16, TMAX], BF16)
        x1 = AR.alloc([5, D], F32)
        gated = AR.alloc([NFC, TMAX], BF16)
        _o = AR.off
        wo2 = [AR.alloc([16, 512], BF16) for _ in range(2)]
        AR.off = _o
        wgu = [[AR.alloc([16, 256], BF16) for _ in range(2)] for _ in range(2)]
        wd2 = [AR.alloc([NFC, 128], BF16) for _ in range(2)]
        hb = AR.alloc([D], BF16)
        sg = [AR.alloc([TMAX], BF16) for _ in range(2)]
        dT = [AR.alloc([TMAX], F32) for _ in range(2)]
        ss = AR.alloc([1], F32); rstd = AR.alloc([1], F32)
        bg = Buf(); bmixh = Buf(); bx1 = [Buf() for _ in range(5)]; bgated = Buf()
        bwgu = [[Buf(), Buf()], [Buf(), Buf()]]; bwd2 = [Buf(), Buf()]; bhb = Buf(); bsg = [Buf(), Buf()]; bdT = [Buf(), Buf()]
        bst = Buf()
        own = lambda i: dict(seq=SP_, q0=i * 128, n=128, x=xo[i * 128:(i + 1) * 128], y=y_o[i * 128:(i + 1) * 128])
        groups = [[dict(seq=SS_, q0=0, n=DSEQ, x=xsam, y=y_s)] + [own(i) for i in range(4)]]
        for G in range(1, 4):
            groups.append([own(i) for i in range(4 * G, 4 * G + 4)])
        cnt = {"wo": 0, "gu": 0, "wd": 0, "sg": 0, "dT": 0}

        def do_group(tiles):
            off = 0
            for t in tiles:
                t["off"] = off
                off += t["n"]
            T = off
            chunks = []
            cur = []
            for t in tiles:
                if cur and (t["off"] + t["n"] - cur[0]["off"] > 512 or t["seq"] is not cur[0]["seq"]):
                    chunks.append(cur); cur = []
                cur.append(t)
            chunks.append(cur)
            chunks = [(c[0]["off"], c[-1]["off"] + c[-1]["n"], c) for c in chunks]
            P.dma("sp", lambda g: g.dma_start(out=g_bc, in_=g_ffn.partition_broadcast(128)), (), [bg])
            for (c0, c1, ct) in chunks:
                seq = ct[0]["seq"]; q0 = ct[0]["q0"]
                dma_mid("sp", mixh[:, :, c0:c1], seq["MixT"][:, :, q0:q0 + (c1 - c0)].rearrange("c p t -> p c t"), 4, (), [bmixh])
            for k, t in enumerate(tiles):
                P.dma("sp", lambda g, k=k, t=t: g.dma_start(out=x1[0:t["n"], k, :], in_=t["x"]), (), [bx1[k]])
            for cg in range(4):
                p = cnt["wo"] % 2; cnt["wo"] += 1
                P.dma("sp", lambda g, p=p, cg=cg: g.dma_start(out=wo2[p], in_=w_out_b[cg]), [wbuf["w_out"]], bwgu[p])
                for k, t in enumerate(tiles):
                    n = t["n"]; o_ = t["off"]
                    bi = next_bank()
                    mm_group(bi, n, 512, lambda c, o_=o_, n=n: mixh[:, c, o_:o_ + n], lambda c, p=p: wo2[p][:, c, :], 16, [bmixh] + bwgu[p])
                    P.op("dve", lambda g, bi=bi, k=k, cg=cg, n=n: g.tensor_tensor(out=x1[0:n, k, cg * 512:(cg + 1) * 512], in0=bank(bi)[0:n, 0:512],
                                                                                   in1=x1[0:n, k, cg * 512:(cg + 1) * 512], op=ALU.add),
                         [bankB[bi], bx1[k]], [bx1[k]])
            for k, t in enumerate(tiles):
                n = t["n"]; o_ = t["off"]
                P.op("act", lambda g, k=k, n=n: g.activation(out=hb[0:n], in_=x1[0:n, k, :], func=AF.Square, accum_out=ss[0:n]), [bx1[k]], [bhb, bst])
                rstd_from_ss(ss, rstd, n, D, bst)
                P.op("dve", lambda g, k=k, n=n: g.scalar_tensor_tensor(out=hb[0:n], in0=x1[0:n, k, :], scalar=rstd[0:n], in1=g_bc[0:n],
                                                                        op0=ALU.mult, op1=ALU.mult), [bx1[k], bst, bg], [bhb])
                for j0 in (0, 8):
                    bi = next_bank()
                    bv = bank(bi, BF16).rearrange("p (j t) -> p j t", t=128)
                    for j in range(j0, j0 + 8):
                        P.op("pe", lambda g, j=j, j0=j0, bv=bv, n=n: g.transpose(out=bv[:, j - j0, 0:n], in_=hb[0:n, j * 128:(j + 1) * 128], identity=ident[0:n, 0:n]),
                             [bhb, b_const], [bankB[bi]], signal=(j == j0 + 7))
                    copy_op("dve", mixh[:, j0:j0 + 8, o_:o_ + n], bv[:, :, 0:n], [bankB[bi]], [bmixh])
            P.dma("sp", lambda g: g.dma_start(out=g_bc, in_=g_fin.partition_broadcast(128)), (), [bg])
            for f2 in range(NFC // 2):
                p = cnt["gu"] % 2; cnt["gu"] += 1
                P.dma("sp", lambda g, p=p, f2=f2: g.dma_start(out=wgu[0][p], in_=wg_b[f2]), [wbuf["wg"]], [bwgu[0][p]])
                P.dma("sp", lambda g, p=p, f2=f2: g.dma_start(out=wgu[1][p], in_=wu_b[f2]), [wbuf["wu"]], [bwgu[1][p]])
                for sub in range(2):
                    f = f2 * 2 + sub
                    s_ = cnt["sg"] % 2; cnt["sg"] += 1
                    for (c0, c1, ct) in chunks:
                        bg_ = next_bank(); bu_ = next_bank()
                        mm_group(bg_, 128, c1 - c0, lambda c, p=p, sub=sub: wgu[0][p][:, c, sub * 128:(sub + 1) * 128],
                                 lambda c, c0=c0, c1=c1: mixh[:, c, c0:c1], 16, [bmixh, bwgu[0][p]])
                        mm_group(bu_, 128, c1 - c0, lambda c, p=p, sub=sub: wgu[1][p][:, c, sub * 128:(sub + 1) * 128],
                                 lambda c, c0=c0, c1=c1: mixh[:, c, c0:c1], 16, [bmixh, bwgu[1][p]])
                        w = c1 - c0
                        P.op("act", lambda g, bg_=bg_, s_=s_, c0=c0, c1=c1, w=w: g.activation(out=sg[s_][:, c0:c1], in_=bank(bg_)[:, 0:w], func=AF.Silu),
                             [bankB[bg_]], [bsg[s_]])
                        P.op("dve", lambda g, bu_=bu_, s_=s_, f=f, c0=c0, c1=c1, w=w: g.tensor_tensor(out=gated[:, f, c0:c1], in0=bank(bu_)[:, 0:w],
                                                                                                     in1=sg[s_][:, c0:c1], op=ALU.mult),
                             [bankB[bu_], bsg[s_]], [bgated])
            for j in range(16):
                p = cnt["wd"] % 2; cnt["wd"] += 1
                P.dma("sp", lambda g, p=p, j=j: g.dma_start(out=wd2[p], in_=wd_b[j]), [wbuf["wd"]], [bwd2[p]])
                d_ = cnt["dT"] % 2; cnt["dT"] += 1
                for (c0, c1, ct) in chunks:
                    w = c1 - c0
                    bi = next_bank()
                    mm_group(bi, 128, w, lambda c, p=p: wd2[p][:, c, :], lambda c, c0=c0, c1=c1: gated[:, c, c0:c1], NFC, [bgated, bwd2[p]])
                    P.op("act", lambda g, bi=bi, d_=d_, c0=c0, c1=c1, w=w: g.activation(out=dT[d_][:, c0:c1], in_=bank(bi)[:, 0:w], func=AF.Copy),
                         [bankB[bi]], [bdT[d_]])
                    bt = next_bank()
                    for kk, t in enumerate(ct):
                        n = t["n"]; o_ = t["off"]
                        P.op("pe", lambda g, kk=kk, bt=bt, d_=d_, n=n, o_=o_: g.transpose(out=bank(bt)[0:n, kk * 128:(kk + 1) * 128],
                                                                                          in_=dT[d_][:, o_:o_ + n], identity=ident_f),
                             [bdT[d_], b_const], [bankB[bt]], signal=(kk == len(ct) - 1))
                    for kk, t in enumerate(ct):
                        n = t["n"]; k = tiles.index(t)
                        P.op("dve", lambda g, kk=kk, bt=bt, j=j, n=n, k=k: g.tensor_tensor(out=x1[0:n, k, j * 128:(j + 1) * 128],
                                                                                           in0=bank(bt)[0:n, kk * 128:(kk + 1) * 128],
                                                                                           in1=x1[0:n, k, j * 128:(j + 1) * 128], op=ALU.add),
                             [bankB[bt], bx1[k]], [bx1[k]])
            for k, t in enumerate(tiles):
                n = t["n"]
                P.op("act", lambda g, k=k, n=n: g.activation(out=hb[0:n], in_=x1[0:n, k, :], func=AF.Square, accum_out=ss[0:n]), [bx1[k]], [bhb, bst])
                rstd_from_ss(ss, rstd, n, D, bst)
                P.op("dve", lambda g, k=k, n=n: g.scalar_tensor_tensor(out=x1[0:n, k, :], in0=x1[0:n, k, :], scalar=rstd[0:n], in1=g_bc[0:n],
                                                                        op0=ALU.mult, op1=ALU.mult), [bx1[k], bst, bg], [bx1[k]])
                P.dma("pool", lambda g, k=k, t=t, n=n: g.dma_start(out=t["y"], in_=x1[0:n, k, :]), [bx1[k]], (), is_output=True)

        import os
        if os.environ.get("MK_P3", "") == "s":
            groups[:] = [groups[0][:1]]
        for tiles in groups:
            do_group(tiles)

    cast("w_ukv", w_ukv_b, w_ukv)
    cast("ck", ck_b, ck)
    cast("cckv", cckv_b, cckv)
    cast("ckr", ckr_b, ckr)
    cast("w_in", w_in_b, w_in)
    cast("cv", SS_["Vd"][0:PAST], cv)
    cast("w_uq", w_uq_b, w_uq)
    cast("w_out", w_out_b, w_out, defer=True, slab=512)
    cast("wg", wg_b, wg, defer=True, slab=256)
    cast("wu", wu_b, wu, defer=True, slab=256)
    cast("wd", wd_b, wd, defer=True, slab=128)
    import os
    stop = int(os.environ.get("MK_STOP", "9"))
    setup()
    P.barrier()
    if stop >= 1:
        passA()
        P.barrier()
    if stop >= 2:
        passB()
        P.barrier()
    if stop >= 3:
        attention()
        P.barrier()
    if stop >= 4:
        phase3()
    while deferred:
        deferred.pop(0)()
    P.finalize()
    P.emit()
    return nc, P


def _t5_onehot(delta):
    rel = (np.arange(255, dtype=np.int32) - 127 + delta)
    try:
        import jax
        import jax.numpy as jnp
        with jax.default_device(jax.devices("cpu")[0]):
            r = jnp.asarray(rel)
            nb = 16
            max_exact = 8
            ret = jnp.where(r > 0, nb, 0)
            n = jnp.abs(r)
            large = max_exact + (jnp.log(jnp.maximum(n, 1).astype(jnp.float32) / max_exact)
                                 / math.log(128 / max_exact) * (nb - max_exact)).astype(jnp.int32)
            large = jnp.minimum(large, nb - 1)
            bucket = np.asarray(ret + jnp.where(n < max_exact, n, large))
    except Exception:
        nb, max_exact = 16, 8
        ret = np.where(rel > 0, nb, 0)
        n = np.abs(rel)
        large = max_exact + (np.log(np.maximum(n, 1).astype(np.float32) / np.float32(max_exact))
                             / np.float32(math.log(128 / max_exact)) * np.float32(nb - max_exact)).astype(np.int32)
        large = np.minimum(large, nb - 1)
        bucket = ret + np.where(n < max_exact, n, large)
    ohm = np.zeros((32, 255), np.float32)
    ohm[bucket, np.arange(255)] = 1.0
    ohm[15, :] -= 1.0
    return ohm


def _rope_table(pos):
    half = 32
    inv_freq = (np.float32(10000.0) ** (-np.arange(half, dtype=np.float32) / np.float32(half))).astype(np.float32)
    ang = pos.astype(np.float32)[:, None] * inv_freq[None, :]
    return np.concatenate([np.cos(ang), np.sin(ang)], axis=1).astype(np.float32)


_CACHE = {}


def kernel(x_prompt, x_sample, cache_diff_k, cache_diff_v, cache_mla_ckv, cache_mla_krope,
           meta_tokens, rel_bias, norm_attn_g, w_in, diff_lambda, diff_subln_g,
           mla_q_norm_g, mla_w_uq, mla_kv_norm_g, mla_w_ukv, w_out, norm_ffn_g,
           ffn_w_gate, ffn_w_up, ffn_w_down, final_norm_g):
    f = lambda a: np.ascontiguousarray(np.asarray(a, dtype=np.float32))
    x_prompt = f(x_prompt); x_sample = f(x_sample)
    if "nc" not in _CACHE:
        _CACHE["nc"] = build_program()[0]
    nc = _CACHE["nc"]
    ohs = np.stack([_t5_onehot(0), _t5_onehot(-128), _t5_onehot(-16)])
    kq = np.arange(128)
    md = np.where((kq[None, :] < 64) & (kq[:, None] >= 64), MASKV, 0.0).astype(np.float32)
    shared = dict(
        w_in=f(w_in[0]), w_uq=f(mla_w_uq[0]), w_ukv=f(mla_w_ukv[0]), w_out=f(w_out[0]),
        wg=f(ffn_w_gate[0]), wu=f(ffn_w_up[0]), wd=f(ffn_w_down[0]),
        g_attn=f(norm_attn_g[0]), g_ffn=f(norm_ffn_g[0]), g_fin=f(final_norm_g),
        g_q=f(mla_q_norm_g[0]), g_kv=f(mla_kv_norm_g[0]), g_sub=f(diff_subln_g[0]),
        lam4=f(diff_lambda[0]).reshape(256), relb=f(rel_bias), xmeta=f(meta_tokens),
        oh=ohs, mask_diag=md, rope_m=_rope_table(np.arange(NMETA)), rope_s=_rope_table(PAST + np.arange(DSEQ)),
    )
    in_maps = []
    for c in range(8):
        b, par = c // 2, c % 2
        own = np.arange(NOWN) * 2 + par
        oth = np.arange(NOWN) * 2 + (1 - par)
        xb = x_prompt[b].reshape(32, 128, D)
        pos_o = (NMETA + own[:, None] * 128 + np.arange(128)[None, :]).reshape(-1)
        pos_t = (NMETA + oth[:, None] * 128 + np.arange(128)[None, :]).reshape(-1)
        selv = np.zeros((128, 4), np.float32)
        if par == 1:
            selv[:, 0] = 1.0
            mo = np.zeros((128, 128), np.float32)
        else:
            selv[:, 1] = 1.0
            selv[:, 2] = 1.0
            mo = np.full((128, 128), MASKV, np.float32)
        m = dict(shared)
        m.update(
            xo=np.ascontiguousarray(xb[own].reshape(-1, D)), xt=np.ascontiguousarray(xb[oth].reshape(-1, D)),
            xsam=x_sample[c],
            ck=f(cache_diff_k[0, c]).reshape(PAST, 1024), cv=f(cache_diff_v[0, c]).reshape(PAST, 1024),
            cckv=f(cache_mla_ckv[0, c]), ckr=f(cache_mla_krope[0, c]),
            rope_o=_rope_table(pos_o), rope_t=_rope_table(pos_t), mask_o0=mo, sel=selv,
        )
        in_maps.append(m)
    if _CACHE.get("prep_only"):
        return in_maps
    res = run_bass_kernel_spmd(nc, in_maps, core_ids=list(range(8))).results
    B = 4
    L = NMETA + SEQ
    y_prompt = np.zeros((B, SEQ, D), np.float32)
    nk = np.zeros((1, B, L, 1024), np.float32); nv = np.zeros((1, B, L, 1024), np.float32)
    nckv = np.zeros((1, B, L, KVL), np.float32); nkr = np.zeros((1, B, L, ROPE), np.float32)
    y_sample = np.zeros((8, DSEQ, D), np.float32)
    sk = np.zeros((1, 8, DSEQ, 1024), np.float32); sv = np.zeros((1, 8, DSEQ, 1024), np.float32)
    sckv = np.zeros((1, 8, DSEQ, KVL), np.float32); skr = np.zeros((1, 8, DSEQ, ROPE), np.float32)
    for c in range(8):
        b, par = c // 2, c % 2
        r = res[c]
        for i in range(NOWN):
            gt = 2 * i + par
            sl = slice(i * 128, (i + 1) * 128)
            y_prompt[b, gt * 128:(gt + 1) * 128] = r["y_o"][sl]
            dst = slice(NMETA + gt * 128, NMETA + (gt + 1) * 128)
            nk[0, b, dst] = r["kd_o"][sl]; nv[0, b, dst] = r["vd_o"][sl]
            nckv[0, b, dst] = r["ckv_o"][sl]; nkr[0, b, dst] = r["kr_o"][sl]
        if par == 0:
            nk[0, b, 0:NMETA] = r["kd_m"]; nv[0, b, 0:NMETA] = r["vd_m"]
            nckv[0, b, 0:NMETA] = r["ckv_m"]; nkr[0, b, 0:NMETA] = r["kr_m"]
        y_sample[c] = r["y_s"]
        sk[0, c] = r["kd_s"]; sv[0, c] = r["vd_s"]; sckv[0, c] = r["ckv_s"]; skr[0, c] = r["kr_s"]
    return (y_prompt, y_sample,
            nk.reshape(1, B, L, NH, 2, DH), nv.reshape(1, B, L, NH, 128), nckv, nkr,
            sk.reshape(1, 8, DSEQ, NH, 2, DH), sv.reshape(1, 8, DSEQ, NH, 128), sckv, skr)
```

```python
import contextlib
import math
import numpy as np
import concourse.bass as bass
import concourse.mybir as mybir
from concourse.bass_utils import run_bass_kernel_spmd

F32 = mybir.dt.float32
BF16 = mybir.dt.bfloat16
U8 = mybir.dt.uint8
AF = mybir.ActivationFunctionType
ALU = mybir.AluOpType
AX = mybir.AxisListType

D = 2048
SEQ = 4096
NMETA = 16
NH = 8
DH = 64
DIN = 3904
OFF_K, OFF_V, OFF_CQ, OFF_CKV, OFF_KR = 1024, 2048, 3072, 3584, 3840
QL, KVL, ROPE = 512, 256, 64
DFF = 5632
NFC = DFF // 128
EPS = 1e-6
DIFF_SCALE = DH ** -0.5
MLA_SCALE = 192 ** -0.5
LAM_INIT = 0.8 - 0.6 * math.exp(0.0)
PAST = 1024
DSEQ = 64
NOWN = 16
NKEY = 2 * NOWN * 128 + NMETA
SKEY = PAST + DSEQ
MASKV = -30000.0

ENGINES = ("pe", "act", "dve", "pool", "sp")
EPOCH = 12000
N_EPOCH = {"pe": 8, "act": 10, "dve": 10, "pool": 3, "sp": 1}


class Buf:
    __slots__ = ("name", "w", "r", "excl")

    def __init__(self, name="", excl=False):
        self.name = name
        self.w = None
        self.r = {}
        self.excl = excl


class Prog:
    def __init__(self, nc, n_dma_sp=30, n_dma_pool=14, n_dma_act=2):
        self.nc = nc
        self.ops = {e: [] for e in ENGINES}
        self.sem_handles = []
        self.sem_names = []
        self.sem_owner = {}
        self.eng_sems = {}
        for e in ENGINES:
            self.eng_sems[e] = [self._new_sem(f"s_{e}{i}", e) for i in range(N_EPOCH[e])]
        self.cnt = {e: 0 for e in ENGINES}
        self.know = {e: {} for e in ENGINES}
        self.tok_know = {}
        self.dma_sems = {
            "sp": [[self._new_sem(f"d_sp{i}", None), 0] for i in range(n_dma_sp)],
            "pool": [[self._new_sem(f"d_pool{i}", None), 0] for i in range(n_dma_pool)],
            "act": [[self._new_sem(f"d_act{i}", None), 0] for i in range(n_dma_act)],
        }
        self.dma_rr = {"sp": 0, "pool": 0, "act": 0}
        self.out_tokens = []
        self.pending_dma = []
        self.late_dma = []
        self.own_done = {e: {} for e in ENGINES}
        self.n_waits = 0
        self.n_ops = {e: 0 for e in ENGINES}

    def _new_sem(self, name, owner):
        self.sem_names.append(name)
        i = len(self.sem_names) - 1
        self.sem_owner[i] = owner
        return i

    def _next_tok(self, e):
        c = self.cnt[e]
        return (self.eng_sems[e][c // EPOCH], c % EPOCH + 1)

    def _resolve(self, e, reads, writes, extra=()):
        deps = {}

        def add(tok, raw):
            if tok is None:
                return
            s, v = tok
            if (not raw) and self.sem_owner[s] == e and e == "pe":
                return
            if deps.get(s, -1) < v:
                deps[s] = v

        for b in reads:
            add(b.w, True)
            if b.excl:
                for s, v in b.r.items():
                    if self.sem_owner[s] != e:
                        add((s, v), True)
        for b in writes:
            add(b.w, False)
            for s, v in b.r.items():
                add((s, v), False)
        for tok in extra:
            add(tok, True)
        know = self.know[e]
        waits = [(s, v) for s, v in deps.items() if know.get(s, -1) < v]
        for s, v in waits:
            tk = self.tok_know.get((s, v))
            if tk:
                for s2, v2 in tk.items():
                    if know.get(s2, -1) < v2:
                        know[s2] = v2
            if know.get(s, -1) < v:
                know[s] = v
        self.n_waits += len(waits)
        return waits

    def _finish(self, tok, reads, writes, e):
        if tok not in self.tok_know:
            self.tok_know[tok] = dict(self.know[e])
        else:
            self.tok_know[tok].update(self.know[e])
        if self.own_done[e]:
            self.tok_know[tok].update(self.own_done[e])
        s, v = tok
        for b in reads:
            if b.r.get(s, -1) < v:
                b.r[s] = v
        for b in writes:
            b.w = tok
            b.r = {}

    def op(self, e, fn, reads=(), writes=(), signal=True, extra=()):
        waits = self._resolve(e, reads, writes, extra)
        tok = self._next_tok(e)
        self.n_ops[e] += 1
        if signal:
            self.cnt[e] += 1
            self.ops[e].append((waits, fn, (tok[0], 1)))
            if self.cnt[e] % EPOCH == 0:
                self.own_done[e][tok[0]] = EPOCH
        else:
            self.ops[e].append((waits, fn, None))
        self._finish(tok, reads, writes, e)
        return tok

    def dma(self, q, fn, reads=(), writes=(), is_output=False, extra=(), late=False):
        pool = self.dma_sems[q]
        i = self.dma_rr[q]
        self.dma_rr[q] = (i + 1) % len(pool)
        s, v = pool[i]
        prev = [(s, v)] if v > 0 else []
        waits = self._resolve(q, reads, writes, tuple(extra) + tuple(prev))
        pool[i][1] = v + 16
        tok = (s, v + 16)
        self.n_ops[q] += 1
        self.ops[q].append((waits, fn, (s, 16)))
        self._finish(tok, reads, writes, q)
        (self.late_dma if late else self.pending_dma).append(tok)
        if is_output:
            self.out_tokens.append(tok)
        return tok

    def barrier(self, final=False):
        toks = list(self.pending_dma)
        if final:
            toks += self.late_dma
            self.late_dma = []
        for e in ENGINES:
            c = self.cnt[e]
            if c > 0:
                cc = c - 1
                toks.append((self.eng_sems[e][cc // EPOCH], cc % EPOCH + 1))
        for e in ENGINES:
            waits = self._resolve(e, (), (), tuple(toks))
            self.ops[e].append((waits, None, None))
        self.pending_dma = []

    def finalize(self):
        waits = self._resolve("sp", (), (), tuple(self.out_tokens))
        self.ops["sp"].append((waits, None, None))

    def emit(self):
        nc = self.nc
        with contextlib.ExitStack() as st:
            for nm in self.sem_names:
                self.sem_handles.append(st.enter_context(nc.semaphore(nm)))
            block = st.enter_context(nc.Block())
            H = self.sem_handles

            def run(eng, ops):
                for waits, fn, inc in ops:
                    for s, v in waits:
                        eng.wait_ge(H[s], v)
                    if fn is None:
                        continue
                    ins = fn(eng)
                    if inc is not None:
                        ins.then_inc(H[inc[0]], inc[1])

            @block.sync
            def _(eng):
                run(eng, self.ops["sp"])

            @block.tensor
            def _(eng):
                run(eng, self.ops["pe"])

            @block.scalar
            def _(eng):
                run(eng, self.ops["act"])

            @block.vector
            def _(eng):
                run(eng, self.ops["dve"])

            @block.gpsimd
            def _(eng):
                run(eng, self.ops["pool"])


class Arena:
    def __init__(self, ap):
        self.ap = ap
        self.off = 0
        self.size = ap.shape[1]

    def alloc(self, shape, dt):
        es = 4 if dt == F32 else 2
        n = int(np.prod(shape)) * es
        assert self.off + n <= self.size, f"arena overflow {self.off}+{n}>{self.size}"
        v = self.ap[:, self.off:self.off + n].bitcast(dt)
        self.off += (n + 63) // 64 * 64
        if len(shape) == 2:
            v = v.rearrange("p (a b) -> p a b", b=shape[1])
        elif len(shape) == 3:
            v = v.rearrange("p (a b c) -> p a b c", b=shape[1], c=shape[2])
        return v


def build_program():
    nc = bass.Bass("TRN2", target_bir_lowering=False)
    P = Prog(nc)

    def din(name, shape):
        return nc.dram_tensor(name, list(shape), F32, kind="ExternalInput").ap()

    def dout(name, shape):
        return nc.dram_tensor(name, list(shape), F32, kind="ExternalOutput").ap()

    def dscr(name, shape, dt=BF16):
        return nc.dram_tensor(name, list(shape), dt).ap()

    xo = din("xo", [NOWN * 128, D]); xt = din("xt", [NOWN * 128, D])
    xmeta = din("xmeta", [NMETA, D]); xsam = din("xsam", [DSEQ, D])
    ck = din("ck", [PAST, 1024]); cv = din("cv", [PAST, 1024])
    cckv = din("cckv", [PAST, KVL]); ckr = din("ckr", [PAST, ROPE])
    w_in = din("w_in", [D, DIN]); w_uq = din("w_uq", [QL, 1536]); w_ukv = din("w_ukv", [KVL, 2048])
    w_out = din("w_out", [D, D]); wg = din("wg", [D, DFF]); wu = din("wu", [D, DFF]); wd = din("wd", [DFF, D])
    g_attn = din("g_attn", [D]); g_ffn = din("g_ffn", [D]); g_fin = din("g_fin", [D])
    g_q = din("g_q", [QL]); g_kv = din("g_kv", [KVL]); g_sub = din("g_sub", [128])
    lam4 = din("lam4", [256]); relb = din("relb", [32, 8])
    rope_o = din("rope_o", [NOWN * 128, 64]); rope_t = din("rope_t", [NOWN * 128, 64])
    rope_m = din("rope_m", [NMETA, 64]); rope_s = din("rope_s", [DSEQ, 64])
    oh = din("oh", [3, 32, 255])
    mask_diag = din("mask_diag", [128, 128]); mask_o0 = din("mask_o0", [128, 128])
    sel = din("sel", [128, 4])
    y_o = dout("y_o", [NOWN * 128, D]); kd_o = dout("kd_o", [NOWN * 128, 1024]); vd_o = dout("vd_o", [NOWN * 128, 1024])
    ckv_o = dout("ckv_o", [NOWN * 128, KVL]); kr_o = dout("kr_o", [NOWN * 128, ROPE])
    kd_m = dout("kd_m", [NMETA, 1024]); vd_m = dout("vd_m", [NMETA, 1024])
    ckv_m = dout("ckv_m", [NMETA, KVL]); kr_m = dout("kr_m", [NMETA, ROPE])
    y_s = dout("y_s", [DSEQ, D]); kd_s = dout("kd_s", [DSEQ, 1024]); vd_s = dout("vd_s", [DSEQ, 1024])
    ckv_s = dout("ckv_s", [DSEQ, KVL]); kr_s = dout("kr_s", [DSEQ, ROPE])
    w_in_b = dscr("w_in_b", [D, DIN]); w_uq_b = dscr("w_uq_b", [QL, 1536]); w_ukv_b = dscr("w_ukv_b", [KVL, 2048])
    w_out_b = dscr("w_out_b", [4, 128, 16, 512]); wg_b = dscr("wg_b", [NFC // 2, 128, 16, 256])
    wu_b = dscr("wu_b", [NFC // 2, 128, 16, 256]); wd_b = dscr("wd_b", [16, 128, NFC, 128])
    ck_b = dscr("ck_b", [PAST, 1024]); cckv_b = dscr("cckv_b", [PAST, KVL]); ckr_b = dscr("ckr_b", [PAST, ROPE])

    def seq_scratch(pfx, nkey, nq):
        return dict(
            KdT=dscr(pfx + "KdT", [NH, 128, nkey]), Vd=dscr(pfx + "Vd", [nkey, 1024]),
            KmT=dscr(pfx + "KmT", [NH, 128, nkey]), Vm=dscr(pfx + "Vm", [nkey, 1024]),
            KrT=dscr(pfx + "KrT", [64, nkey]),
            QdT=dscr(pfx + "QdT", [NH, 128, nq]), QnT=dscr(pfx + "QnT", [NH, 128, nq]),
            QrT=dscr(pfx + "QrT", [NH, 64, nq]), MixT=dscr(pfx + "MixT", [16, 128, nq]),
            nkey=nkey, nq=nq)

    SP_ = seq_scratch("p", NKEY, NOWN * 128)
    SS_ = seq_scratch("s", SKEY, DSEQ)

    arena_ap = nc.alloc_sbuf_tensor("arena", [128, 206 * 1024], U8).ap()
    AR = Arena(arena_ap)
    PS = nc.alloc_psum_tensor("ps", [128, 4096], F32).ap()
    bankB = [Buf(f"bank{i}", excl=True) for i in range(8)]

    def bank(i, dt=F32):
        v = PS[:, i * 512:(i + 1) * 512]
        return v.bitcast(dt) if dt != F32 else v

    rr = {"b": 0, "cp": 0}

    def next_bank(lo=0, hi=8):
        i = lo + rr["b"] % (hi - lo)
        rr["b"] += 1
        return i

    def copy_eng():
        rr["cp"] += 1
        return "dve"

    def copy_op(e, out, in_, r, w):
        if e == "act":
            P.op("act", lambda g: g.activation(out=out, in_=in_, func=AF.Copy), r, w)
        else:
            P.op(e, lambda g: g.tensor_copy(out, in_), r, w)

    ident_f = AR.alloc([128], F32); ident = AR.alloc([128], BF16)
    ones_b = AR.alloc([128], BF16); ones_f = AR.alloc([128], F32)
    lamb = AR.alloc([256], F32); lprod = AR.alloc([2, 64], F32); lsum = AR.alloc([2], F32)
    lexp = AR.alloc([2], F32); lam_t = AR.alloc([1], F32); neg_lam = AR.alloc([1], F32)
    gsub = AR.alloc([1], F32); gsub8 = AR.alloc([1], F32)
    sel_s = AR.alloc([4], F32)
    Bdiag = AR.alloc([NH, 128], BF16); Bo0 = AR.alloc([NH, 128], BF16); Bo1 = AR.alloc([NH, 128], BF16)
    Bmeta = AR.alloc([NH, 128], BF16); Bs7 = AR.alloc([NH, 64], BF16); Bsn = AR.alloc([NH, 64], BF16)
    Mdiag = AR.alloc([128], BF16); Mo0 = AR.alloc([128], BF16)
    b_const = Buf("const")
    persist_mark = AR.off

    def setup():
        R = AR.alloc([8], F32); R8 = AR.alloc([8], F32)
        oh_s = AR.alloc([3, 255], F32)
        md_f = AR.alloc([128], F32); mo_f = AR.alloc([128], F32)
        bR = Buf(); bT = Buf()
        P.op("pool", lambda g: g.memset(ident_f, 0.0), (), [bT])
        P.op("pool", lambda g: g.affine_select(ident_f, ident_f, pattern=[[-1, 128]], compare_op=ALU.not_equal,
                                               fill=1.0, base=0, channel_multiplier=1), [bT], [bT])
        P.op("pool", lambda g: g.memset(ones_f, 1.0), (), [bT])
        P.op("dve", lambda g: g.tensor_copy(ident, ident_f), [bT], [b_const])
        P.op("dve", lambda g: g.tensor_copy(ones_b, ones_f), [bT], [b_const])
        P.dma("sp", lambda g: g.dma_start(out=lamb, in_=lam4.partition_broadcast(128)), (), [bR])
        P.dma("sp", lambda g: g.dma_start(out=gsub, in_=g_sub.rearrange("(p o) -> p o", o=1)), (), [bR])
        P.dma("sp", lambda g: g.dma_start(out=sel_s, in_=sel), (), [bR])
        P.dma("sp", lambda g: g.dma_start(out=R[0:32], in_=relb), (), [bR])
        P.dma("sp", lambda g: g.dma_start(out=oh_s[0:32], in_=oh.rearrange("t b r -> b t r")), (), [bR])
        P.dma("sp", lambda g: g.dma_start(out=md_f, in_=mask_diag), (), [bR])
        P.dma("sp", lambda g: g.dma_start(out=mo_f, in_=mask_o0), (), [bR])
        lv = lamb.rearrange("p (a b c) -> p a b c", a=2, b=2, c=64)
        P.op("dve", lambda g: g.tensor_tensor(out=lprod, in0=lv[:, :, 0, :], in1=lv[:, :, 1, :], op=ALU.mult), [bR], [bT])
        P.op("dve", lambda g: g.reduce_sum(out=lsum, in_=lprod, axis=AX.X), [bT], [bT])
        P.op("act", lambda g: g.activation(out=lexp, in_=lsum, func=AF.Exp), [bT], [bT])
        P.op("dve", lambda g: g.tensor_tensor(out=lam_t, in0=lexp[:, 0:1], in1=lexp[:, 1:2], op=ALU.subtract), [bT], [bT])
        P.op("dve", lambda g: g.tensor_scalar(out=neg_lam, in0=lam_t, scalar1=LAM_INIT, scalar2=-1.0,
                                              op0=ALU.add, op1=ALU.mult), [bT], [b_const])
        P.op("dve", lambda g: g.tensor_scalar(out=gsub8, in0=gsub, scalar1=1.0 - LAM_INIT, scalar2=None,
                                              op0=ALU.mult), [bR], [b_const])
        P.op("dve", lambda g: g.tensor_scalar(out=R8[0:32], in0=R[0:32], scalar1=1.0 / DIFF_SCALE, scalar2=None,
                                              op0=ALU.mult), [bR], [bT])
        views = []
        for t in range(3):
            bA, bB = 2 * t, 2 * t + 1
            for q in range(128):
                bi = bA if q < 64 else bB
                o = bank(bi)[:, (q % 64) * 8:(q % 64) * 8 + 8]
                P.op("pe", lambda g, o=o, t=t, q=q: g.matmul(o, lhsT=oh_s[0:32, t, 127 - q:255 - q], rhs=R8[0:32],
                                                             start=True, stop=True),
                     [bT, bR], [bankB[bi]], signal=(q % 64 == 63))
            views.append((bA, bB))

        def tv(t, h, k0, k1, q0, q1):
            res = []
            for half in range(2):
                a, b = max(q0, half * 64), min(q1, half * 64 + 64)
                if a >= b:
                    continue
                bi = views[t][half]
                v = bank(bi).rearrange("p (q h) -> p q h", h=8)[k0:k1, a - half * 64:b - half * 64, h]
                res.append((v, bi, a, b))
            return res

        P.op("dve", lambda g: g.memset(Bmeta, 0.0), (), [b_const])
        P.op("dve", lambda g: g.memset(Bsn, 0.0), (), [b_const])
        for h in range(NH):
            for v, bi, a, b in tv(0, h, 0, 128, 0, 128):
                P.op("dve", lambda g, v=v, a=a, b=b, h=h: g.tensor_tensor(out=Bdiag[:, h, a:b], in0=v, in1=md_f[:, a:b], op=ALU.add),
                     [bankB[bi], bR], [b_const])
            for v, bi, a, b in tv(1, h, 0, 128, 0, 128):
                P.op("dve", lambda g, v=v, a=a, b=b, h=h: g.scalar_tensor_tensor(out=Bo0[:, h, a:b], in0=v, scalar=sel_s[:, 0:1],
                                                                                 in1=mo_f[:, a:b], op0=ALU.mult, op1=ALU.add),
                     [bankB[bi], bR], [b_const])
                P.op("act", lambda g, v=v, a=a, b=b, h=h: g.activation(out=Bo1[:, h, a:b], in_=v, func=AF.Copy, scale=sel_s[:, 1:2]),
                     [bankB[bi], bR], [b_const])
            for v, bi, a, b in tv(2, h, 0, 16, 0, 128):
                P.op("act", lambda g, v=v, a=a, b=b, h=h: g.activation(out=Bmeta[0:16, h, a:b], in_=v, func=AF.Copy, scale=sel_s[0:16, 2:3]),
                     [bankB[bi], bR], [b_const])
            for v, bi, a, b in tv(1, h, 0, 128, 0, 64):
                P.op("dve", lambda g, v=v, a=a, b=b, h=h: g.tensor_copy(Bs7[:, h, a:b], v), [bankB[bi]], [b_const])
            for v, bi, a, b in tv(0, h, 0, 64, 0, 64):
                P.op("dve", lambda g, v=v, a=a, b=b, h=h: g.tensor_copy(Bsn[0:64, h, a:b], v), [bankB[bi]], [b_const])
        P.op("dve", lambda g: g.tensor_copy(Mdiag, md_f), [bR], [b_const])
        P.op("dve", lambda g: g.tensor_copy(Mo0, mo_f), [bR], [b_const])

    def dma_mid(q, out3, in3, step, r, w):
        m = out3.shape[1]
        for a in range(0, m, step):
            b = min(m, a + step)
            P.dma(q, lambda g, a=a, b=b: g.dma_start(out=out3[:, a:b], in_=in3[:, a:b]), r, w)

    wbuf = {}

    deferred = []
    deferred_late = []

    def cast(name, dst, src, rows_per=128, defer=False, slab=None, late=False):
        b = Buf(name)
        wbuf[name] = b
        n = src.shape[0]
        for r in range(0, n, rows_per):
            def go(r=r):
                if slab is None:
                    P.dma("pool", lambda g: g.dma_start(out=dst[r:r + rows_per], in_=src[r:r + rows_per]), (), [b])
                else:
                    w = slab
                    dv = dst.rearrange("a p c w -> p a c w")[:, :, r // 128, :]
                    sv = src[r:r + 128].rearrange("p (a w) -> p a w", w=w)
                    P.dma("pool", lambda g: g.dma_start(out=dv, in_=sv), (), [b], late=late)
            if late:
                deferred_late.append(go)
            elif defer:
                deferred.append(go)
            else:
                go()
        return b

    def rstd_from_ss(ss, rstd, n, nfeat, rb):
        P.op("dve", lambda g: g.tensor_scalar(out=rstd[0:n], in0=ss[0:n], scalar1=1.0 / nfeat, scalar2=EPS,
                                              op0=ALU.mult, op1=ALU.add), [rb], [rb])
        P.op("act", lambda g: g.activation(out=rstd[0:n], in_=rstd[0:n], func=AF.Sqrt), [rb], [rb])
        P.op("dve", lambda g: g.reciprocal(out=rstd[0:n], in_=rstd[0:n]), [rb], [rb])

    def transposes(src_fn, nblk, n, cols, dst, dstB, srcB, dt=BF16, idn=None):
        idn = ident if dt == BF16 else ident_f
        per = 8 if dt == BF16 else 4
        for j0 in range(0, nblk, per):
            bi = next_bank()
            bv = bank(bi, dt).rearrange("p (j t) -> p j t", t=128)
            m = min(per, nblk - j0)
            for j in range(j0, j0 + m):
                P.op("pe", lambda g, j=j, j0=j0, bv=bv: g.transpose(out=bv[0:cols, j - j0, 0:n], in_=src_fn(j), identity=idn[0:n, 0:n]),
                     [srcB, b_const], [bankB[bi]], signal=(j == j0 + m - 1))
            copy_op(copy_eng(), dst[0:cols, j0:j0 + m, 0:n], bv[0:cols, 0:m, 0:n], [bankB[bi]], [dstB])

    def norm_T(xs, bx, n, gbc, bg, hb, bhb, hT, bhT, ss, rstd, bst):
        import os
        mf = int(os.environ.get("MK_F", "9"))
        if mf == 0:
            return
        P.op("act", lambda g: g.activation(out=hb[0:n], in_=xs[0:n], func=AF.Square, accum_out=ss[0:n]), [bx], [bhb, bst])
        if mf == 1:
            return
        rstd_from_ss(ss, rstd, n, D, bst)
        if mf == 2:
            return
        P.op("dve", lambda g: g.scalar_tensor_tensor(out=hb[0:n], in0=xs[0:n], scalar=rstd[0:n], in1=gbc[0:n],
                                                     op0=ALU.mult, op1=ALU.mult), [bx, bst, bg], [bhb])

    def norm_T2(n, hb, bhb, hT, bhT):
        transposes(lambda j: hb[0:n, j * 128:(j + 1) * 128], 16, n, 128, hT, bhT, bhb)

    def mm_group(bi, n, ncols, lhs_fn, rhs_fn, nk, rB, dst=None):
        o = bank(bi)[0:n, 0:ncols] if dst is None else dst
        for c in range(nk):
            P.op("pe", lambda g, c=c: g.matmul(o, lhsT=lhs_fn(c), rhs=rhs_fn(c), start=(c == 0), stop=(c == nk - 1)),
                 rB, [bankB[bi]], signal=(c == nk - 1))

    def rope_tok(src_fn, dst_fn, cos, sin, n, tmp, rB, wB, tB):
        x1, x2 = src_fn(0), src_fn(1)
        t1, t2 = tmp
        P.op("dve", lambda g: g.tensor_tensor(out=t1, in0=x1, in1=cos, op=ALU.mult), rB, [tB])
        P.op("dve", lambda g: g.tensor_tensor(out=t2, in0=x2, in1=sin, op=ALU.mult), rB, [tB])
        P.op("dve", lambda g: g.tensor_tensor(out=dst_fn(0), in0=t1, in1=t2, op=ALU.subtract), [tB], [wB])
        P.op("dve", lambda g: g.tensor_tensor(out=t1, in0=x1, in1=sin, op=ALU.mult), rB + [wB], [tB])
        P.op("dve", lambda g: g.tensor_tensor(out=t2, in0=x2, in1=cos, op=ALU.mult), rB, [tB])
        P.op("dve", lambda g: g.tensor_tensor(out=dst_fn(1), in0=t1, in1=t2, op=ALU.add), [tB], [wB])

    def passA():
        AR.off = persist_mark
        WinK = AR.alloc([16, 2368], BF16); Wukv = AR.alloc([2, 2048], BF16)
        g_bc = AR.alloc([D], F32); gkv_bc = AR.alloc([KVL], F32)
        xs2 = [AR.alloc([D], F32) for _ in range(2)]
        hb = AR.alloc([D], BF16)
        hT2 = [AR.alloc([16, 128], BF16) for _ in range(2)]
        kd_f = AR.alloc([1024], F32); vd_f = AR.alloc([1024], F32)
        kd_b = AR.alloc([1024], BF16); vd_b = AR.alloc([1024], BF16)
        ckv_f = AR.alloc([KVL], F32); ckv_b = AR.alloc([KVL], BF16)
        kr_f = AR.alloc([64], F32); kr_b = AR.alloc([64], BF16)
        rope2 = [AR.alloc([64], F32) for _ in range(2)]
        rt = [AR.alloc([32], F32) for _ in range(2)]
        junk = AR.alloc([KVL], F32)
        ss = AR.alloc([1], F32); rstd = AR.alloc([1], F32); ssk = AR.alloc([1], F32); rstdk = AR.alloc([1], F32)
        KdT_st = AR.alloc([NH, 128], BF16); ckvT = AR.alloc([2, 128], BF16); KrT_st = AR.alloc([1, 128], BF16)
        KmT_st = AR.alloc([NH, 128], BF16); vm_b = AR.alloc([1024], BF16)
        bW = Buf(); bg = Buf()
        bx2 = [Buf(), Buf()]; bhb = Buf(); bhT2 = [Buf(), Buf()]
        bkdf = Buf(); bvdf = Buf(); bkdb = Buf(); bvdb = Buf(); bckf = Buf(); bckb = Buf(); bkrf = Buf(); bkrb = Buf()
        brope2 = [Buf(), Buf()]; brt = Buf(); bjunk = Buf(); bst = Buf(); bstk = Buf()
        bKdT = Buf(); bckvT = Buf(); bKrT = Buf(); bKmT = Buf(); bvmb = Buf()
        wv = w_in_b.rearrange("(c p) n -> p c n", p=128)
        dma_mid("sp", WinK[:, :, 0:2048], wv[:, :, OFF_K:OFF_CQ], 2, [wbuf["w_in"]], [bW])
        dma_mid("sp", WinK[:, :, 2048:2368], wv[:, :, OFF_CKV:DIN], 4, [wbuf["w_in"]], [bW])
        P.dma("sp", lambda g: g.dma_start(out=Wukv, in_=w_ukv_b.rearrange("(c p) n -> p c n", p=128)), [wbuf["w_ukv"]], [bW])
        P.dma("sp", lambda g: g.dma_start(out=g_bc, in_=g_attn.partition_broadcast(128)), (), [bg])
        P.dma("sp", lambda g: g.dma_start(out=gkv_bc, in_=g_kv.partition_broadcast(128)), (), [bg])

        tiles = []
        for j in range(PAST // 128):
            tiles.append(dict(cache=j, n=128, seq=SS_, koff=j * 128))
        tiles.append(dict(x=xmeta, n=NMETA, seq=SP_, koff=2 * NOWN * 128, rope=rope_m, outs=(kd_m, vd_m, ckv_m, kr_m)))
        tiles.append(dict(x=xsam, n=DSEQ, seq=SS_, koff=PAST, rope=rope_s, outs=(kd_s, vd_s, ckv_s, kr_s)))
        for i in range(NOWN):
            r = slice(i * 128, (i + 1) * 128)
            tiles.append(dict(x=xo[r], n=128, seq=SP_, koff=i * 128, rope=rope_o[r],
                              outs=(kd_o[r], vd_o[r], ckv_o[r], kr_o[r])))
            tiles.append(dict(x=xt[r], n=128, seq=SP_, koff=NOWN * 128 + i * 128, rope=rope_t[r], outs=None))

        import os
        flt = os.environ.get("MK_A", "")
        if flt:
            keep = []
            for t in tiles:
                kind = "cache" if "cache" in t else ("meta" if t["n"] == NMETA else ("sam" if t["n"] == DSEQ else "own"))
                if kind in flt.split(","):
                    keep.append(t)
            tiles[:] = keep[:int(os.environ.get("MK_AN", "100"))]

        def load(idx):
            t = tiles[idx]
            if "cache" in t:
                return
            p = t["slot"]
            n = t["n"]
            P.dma("sp", lambda g: g.dma_start(out=xs2[p][0:n], in_=t["x"]), (), [bx2[p]])
            P.dma("sp", lambda g: g.dma_start(out=rope2[p][0:n], in_=t["rope"]), (), [brope2[p]])

        def front(idx):
            t = tiles[idx]
            if "cache" in t:
                return
            p = t["slot"]
            norm_T(xs2[p], bx2[p], t["n"], g_bc, bg, hb, bhb, hT2[p], bhT2[p], ss, rstd, bst)

        def front_T(idx):
            if idx >= len(tiles):
                return
            t = tiles[idx]
            if "cache" in t:
                return
            p = t["slot"]
            norm_T2(t["n"], hb, bhb, hT2[p], bhT2[p])

        cut = int(os.environ.get("MK_CUT", "9"))

        def back(idx, mid=lambda: None):
            t = tiles[idx]
            n = t["n"]; seq = t["seq"]; koff = t["koff"]
            if cut == 0:
                return
            if "cache" in t:
                mid()
                j = t["cache"]
                r = slice(j * 128, (j + 1) * 128)
                P.dma("sp", lambda g: g.dma_start(out=kd_b[0:n], in_=ck_b[r]), [wbuf["ck"]], [bkdb])
                P.dma("sp", lambda g: g.dma_start(out=ckv_b[0:n], in_=cckv_b[r]), [wbuf["cckv"]], [bckb])
                P.dma("sp", lambda g: g.dma_start(out=kr_b[0:n], in_=ckr_b[r]), [wbuf["ckr"]], [bkrb])
            else:
                p = t["slot"]
                hT = hT2[p]
                rB = [bhT2[p], bW]
                for gi, (c0, dstf, dstb, bf_, bb_) in enumerate([(0, kd_f, kd_b, bkdf, bkdb), (512, kd_f, kd_b, bkdf, bkdb),
                                                                  (1024, vd_f, vd_b, bvdf, bvdb), (1536, vd_f, vd_b, bvdf, bvdb)]):
                    bi = next_bank()
                    mm_group(bi, n, 512, lambda c: hT[:, c, 0:n], lambda c, c0=c0: WinK[:, c, c0:c0 + 512], 16, rB)
                    lc = c0 % 1024
                    P.op("act", lambda g, bi=bi, dstf=dstf, lc=lc: g.activation(out=dstf[0:n, lc:lc + 512], in_=bank(bi)[0:n, 0:512], func=AF.Copy),
                         [bankB[bi]], [bf_])
                    P.op("dve", lambda g, bi=bi, dstb=dstb, lc=lc: g.tensor_copy(dstb[0:n, lc:lc + 512], bank(bi)[0:n, 0:512]),
                         [bankB[bi]], [bb_])
                if cut == 1:
                    return
                bi = next_bank()
                mm_group(bi, n, 320, lambda c: hT[:, c, 0:n], lambda c: WinK[:, c, 2048:2368], 16, rB)
                bk = bank(bi)
                mid()
                P.op("act", lambda g: g.activation(out=junk[0:n], in_=bk[0:n, 0:KVL], func=AF.Square, accum_out=ssk[0:n]),
                     [bankB[bi]], [bjunk, bstk])
                rstd_from_ss(ssk, rstdk, n, KVL, bstk)
                P.op("dve", lambda g: g.scalar_tensor_tensor(out=ckv_f[0:n], in0=bk[0:n, 0:KVL], scalar=rstdk[0:n], in1=gkv_bc[0:n],
                                                             op0=ALU.mult, op1=ALU.mult), [bankB[bi], bstk, bg], [bckf])
                P.op("act", lambda g: g.activation(out=ckv_b[0:n], in_=ckv_f[0:n], func=AF.Copy), [bckf], [bckb])
                rp = rope2[p]
                rope_tok(lambda hf: bk[0:n, KVL + 32 * hf:KVL + 32 * hf + 32], lambda hf: kr_f[0:n, 32 * hf:32 * hf + 32],
                         rp[0:n, 0:32], rp[0:n, 32:64], n, (rt[0][0:n], rt[1][0:n]), [bankB[bi], brope2[p]], bkrf, brt)
                P.op("act", lambda g: g.activation(out=kr_b[0:n], in_=kr_f[0:n], func=AF.Copy), [bkrf], [bkrb])
                if t["outs"] is not None:
                    o_kd, o_vd, o_ckv, o_kr = t["outs"]
                    P.dma("pool", lambda g: g.dma_start(out=o_kd, in_=kd_f[0:n]), [bkdf], (), is_output=True)
                    P.dma("pool", lambda g: g.dma_start(out=o_vd, in_=vd_f[0:n]), [bvdf], (), is_output=True)
                    P.dma("pool", lambda g: g.dma_start(out=o_ckv, in_=ckv_f[0:n]), [bckf], (), is_output=True)
                    P.dma("pool", lambda g: g.dma_start(out=o_kr, in_=kr_f[0:n]), [bkrf], (), is_output=True)
                P.dma("pool", lambda g: g.dma_start(out=seq["Vd"][koff:koff + n, :], in_=vd_b[0:n]), [bvdb], ())
            if cut == 2:
                return
            transposes(lambda j: kd_b[0:n, j * 128:(j + 1) * 128], NH, n, 128, KdT_st, bKdT, bkdb)
            P.dma("pool", lambda g: g.dma_start(out=seq["KdT"][:, :, koff:koff + n].rearrange("h p k -> p h k"), in_=KdT_st[:, :, 0:n]),
                  [bKdT], ())
            transposes(lambda j: ckv_b[0:n, j * 128:(j + 1) * 128], 2, n, 128, ckvT, bckvT, bckb)
            transposes(lambda j: kr_b[0:n, 0:64], 1, n, 64, KrT_st, bKrT, bkrb)
            P.dma("pool", lambda g: g.dma_start(out=seq["KrT"][:, koff:koff + n], in_=KrT_st[0:64, 0, 0:n]), [bKrT], ())
            wk = Wukv.rearrange("p c (h x) -> p c h x", x=256)
            for hh in range(2):
                bi = next_bank()
                for h4 in range(4):
                    h = hh * 4 + h4
                    mm_group(bi, 128, n, lambda c, h=h: wk[:, c, h, 0:128], lambda c: ckvT[:, c, 0:n], 2, [bckvT, bW],
                             dst=bank(bi)[:, h4 * 128:h4 * 128 + n])
                copy_op(copy_eng(), KmT_st[:, hh * 4:hh * 4 + 4, 0:n],
                        bank(bi).rearrange("p (j t) -> p j t", t=128)[:, :, 0:n], [bankB[bi]], [bKmT])
            P.dma("pool", lambda g: g.dma_start(out=seq["KmT"][:, :, koff:koff + n].rearrange("h p k -> p h k"), in_=KmT_st[:, :, 0:n]),
                  [bKmT], ())
            for hh in range(2):
                bi = next_bank()
                mm_group(bi, n, 512, lambda c: ckvT[:, c, 0:n], lambda c, hh=hh: wk[:, c, hh * 4:hh * 4 + 4, 128:256], 2, [bckvT, bW])
                copy_op(copy_eng(), vm_b[0:n, hh * 512:(hh + 1) * 512], bank(bi)[0:n, 0:512], [bankB[bi]], [bvmb])
            P.dma("pool", lambda g: g.dma_start(out=seq["Vm"][koff:koff + n, :], in_=vm_b[0:n]), [bvmb], ())

        slot = 0
        for t in tiles:
            if "cache" not in t:
                t["slot"] = slot
                slot ^= 1
        nt = len(tiles)
        load(0)
        front(0)
        front_T(0)
        for i in range(nt):
            if i + 1 < nt:
                load(i + 1)
                front(i + 1)
            back(i, mid=lambda i=i: front_T(i + 1))
            for _ in range(3):
                if deferred:
                    deferred.pop(0)()
        while deferred:
            deferred.pop(0)()

    def passB():
        AR.off = persist_mark
        WinQ = AR.alloc([16, 1536], BF16); Wuq = AR.alloc([4, 1536], BF16)
        g_bc = AR.alloc([D], F32); gq_bc = AR.alloc([QL], F32)
        xs2 = [AR.alloc([D], F32) for _ in range(2)]
        hb = AR.alloc([D], BF16)
        hT2 = [AR.alloc([16, 128], BF16) for _ in range(2)]
        qd_b = AR.alloc([1024], BF16); cq_b = AR.alloc([QL], BF16); cqT = AR.alloc([4, 128], BF16)
        qn_b = AR.alloc([NH, 128], BF16); qr_f = AR.alloc([NH, 64], F32); qr_b = AR.alloc([NH, 64], BF16)
        rope2 = [AR.alloc([64], F32) for _ in range(2)]
        rt = [AR.alloc([2, 32], F32) for _ in range(2)]
        junk = AR.alloc([QL], F32)
        ss = AR.alloc([1], F32); rstd = AR.alloc([1], F32); ssq = AR.alloc([1], F32); rstdq = AR.alloc([1], F32)
        QdT_st = AR.alloc([NH, 128], BF16); QnT_st = AR.alloc([NH, 128], BF16); QrT_st = AR.alloc([NH, 128], BF16)
        bW = Buf(); bg = Buf(); bx2 = [Buf(), Buf()]; bhb = Buf(); bhT2 = [Buf(), Buf()]
        bqdb = Buf(); bcqb = Buf(); bcqT = Buf(); bqnb = Buf(); bqrf = Buf(); bqrb = Buf()
        brope2 = [Buf(), Buf()]; brt = Buf(); bjunk = Buf(); bst = Buf(); bstq = Buf()
        bQdT = Buf(); bQnT = Buf(); bQrT = Buf()
        wv = w_in_b.rearrange("(c p) n -> p c n", p=128)
        dma_mid("sp", WinQ[:, :, 0:1024], wv[:, :, 0:OFF_K], 4, [wbuf["w_in"]], [bW])
        dma_mid("sp", WinQ[:, :, 1024:1536], wv[:, :, OFF_CQ:OFF_CKV], 4, [wbuf["w_in"]], [bW])
        P.dma("sp", lambda g: g.dma_start(out=Wuq, in_=w_uq_b.rearrange("(c p) n -> p c n", p=128)), [wbuf["w_uq"]], [bW])
        P.dma("sp", lambda g: g.dma_start(out=g_bc, in_=g_attn.partition_broadcast(128)), (), [bg])
        P.dma("sp", lambda g: g.dma_start(out=gq_bc, in_=g_q.partition_broadcast(128)), (), [bg])
        tiles = [dict(x=xsam, n=DSEQ, seq=SS_, qoff=0, rope=rope_s)]
        for i in range(NOWN):
            r = slice(i * 128, (i + 1) * 128)
            tiles.append(dict(x=xo[r], n=128, seq=SP_, qoff=i * 128, rope=rope_o[r]))

        import os
        if os.environ.get("MK_B", "") == "s":
            tiles[:] = tiles[:1]

        def load(idx):
            t = tiles[idx]; p = idx % 2; n = t["n"]
            P.dma("sp", lambda g: g.dma_start(out=xs2[p][0:n], in_=t["x"]), (), [bx2[p]])
            P.dma("sp", lambda g: g.dma_start(out=rope2[p][0:n], in_=t["rope"]), (), [brope2[p]])

        def front(idx):
            t = tiles[idx]; p = idx % 2
            norm_T(xs2[p], bx2[p], t["n"], g_bc, bg, hb, bhb, hT2[p], bhT2[p], ss, rstd, bst)

        def front_T(idx):
            if idx >= len(tiles):
                return
            t = tiles[idx]; p = idx % 2
            norm_T2(t["n"], hb, bhb, hT2[p], bhT2[p])

        def back(idx, mid=lambda: None):
            t = tiles[idx]; p = idx % 2; n = t["n"]; seq = t["seq"]; qoff = t["qoff"]
            hT = hT2[p]; rB = [bhT2[p], bW]
            for c0 in (0, 512):
                bi = next_bank()
                mm_group(bi, n, 512, lambda c: hT[:, c, 0:n], lambda c, c0=c0: WinQ[:, c, c0:c0 + 512], 16, rB)
                copy_op(copy_eng(), qd_b[0:n, c0:c0 + 512], bank(bi)[0:n, 0:512], [bankB[bi]], [bqdb])
            transposes(lambda j: qd_b[0:n, j * 128:(j + 1) * 128], NH, n, 128, QdT_st, bQdT, bqdb)
            P.dma("pool", lambda g: g.dma_start(out=seq["QdT"][:, :, qoff:qoff + n].rearrange("h p k -> p h k"), in_=QdT_st[:, :, 0:n]),
                  [bQdT], ())
            bi = next_bank()
            mm_group(bi, n, 512, lambda c: hT[:, c, 0:n], lambda c: WinQ[:, c, 1024:1536], 16, rB)
            bk = bank(bi)
            mid()
            P.op("act", lambda g: g.activation(out=junk[0:n], in_=bk[0:n, 0:QL], func=AF.Square, accum_out=ssq[0:n]),
                 [bankB[bi]], [bjunk, bstq])
            rstd_from_ss(ssq, rstdq, n, QL, bstq)
            P.op("dve", lambda g: g.scalar_tensor_tensor(out=cq_b[0:n], in0=bk[0:n, 0:QL], scalar=rstdq[0:n], in1=gq_bc[0:n],
                                                         op0=ALU.mult, op1=ALU.mult), [bankB[bi], bstq, bg], [bcqb])
            transposes(lambda j: cq_b[0:n, j * 128:(j + 1) * 128], 4, n, 128, cqT, bcqT, bcqb)
            rp = rope2[p]
            for gq in range(4):
                bi = next_bank()
                mm_group(bi, n, 384, lambda c: cqT[:, c, 0:n], lambda c, gq=gq: Wuq[:, c, gq * 384:(gq + 1) * 384], 4, [bcqT, bW])
                bv = bank(bi)[:, 0:384].rearrange("p (h x) -> p h x", x=192)
                P.op("act", lambda g, bv=bv, gq=gq: g.activation(out=qn_b[0:n, 2 * gq:2 * gq + 2, :], in_=bv[0:n, :, 0:128], func=AF.Copy),
                     [bankB[bi]], [bqnb])
                cosb = rp[0:n, 0:32].unsqueeze(1).to_broadcast([n, 2, 32])
                sinb = rp[0:n, 32:64].unsqueeze(1).to_broadcast([n, 2, 32])
                rope_tok(lambda hf, bv=bv: bv[0:n, :, 128 + 32 * hf:128 + 32 * hf + 32],
                         lambda hf, gq=gq: qr_f[0:n, 2 * gq:2 * gq + 2, 32 * hf:32 * hf + 32],
                         cosb, sinb, n, (rt[0][0:n], rt[1][0:n]), [bankB[bi], brope2[p]], bqrf, brt)
            P.op("act", lambda g: g.activation(out=qr_b[0:n], in_=qr_f[0:n], func=AF.Copy), [bqrf], [bqrb])
            transposes(lambda j: qn_b[0:n, j, :], NH, n, 128, QnT_st, bQnT, bqnb)
            P.dma("pool", lambda g: g.dma_start(out=seq["QnT"][:, :, qoff:qoff + n].rearrange("h p k -> p h k"), in_=QnT_st[:, :, 0:n]),
                  [bQnT], ())
            transposes(lambda j: qr_b[0:n, j, :], NH, n, 64, QrT_st, bQrT, bqrb)
            P.dma("pool", lambda g: g.dma_start(out=seq["QrT"][:, :, qoff:qoff + n].rearrange("h p k -> p h k"), in_=QrT_st[0:64, :, 0:n]),
                  [bQrT], ())

        nt = len(tiles)
        load(0)
        front(0)
        front_T(0)
        for i in range(nt):
            if i + 1 < nt:
                load(i + 1)
                front(i + 1)
            back(i, mid=lambda i=i: front_T(i + 1))

    def attention():
        AR.off = persist_mark
        NT = NKEY // 128 + 1
        NKP = NKEY - NMETA + 128
        KT2 = [AR.alloc([NKP], BF16) for _ in range(2)]
        V2 = [AR.alloc([NT, 128], BF16) for _ in range(2)]
        Q2 = [[AR.alloc([NOWN * 128], BF16) for _ in range(2)] for _ in range(2)]
        Qn2 = [AR.alloc([NOWN * 128], BF16) for _ in range(2)]
        QR2 = [AR.alloc([NOWN * 128], BF16) for _ in range(2)]
        KrT = AR.alloc([NKP], BF16)
        PT = [AR.alloc([512], BF16) for _ in range(4)]
        onesM = AR.alloc([128], BF16); onesS = AR.alloc([128], BF16)
        bz = Buf()
        for par_ in range(2):
            P.op("pool", lambda g, par_=par_: g.memset(KT2[par_], 0.0), (), [bz])
            P.op("pool", lambda g, par_=par_: g.memset(V2[par_], 0.0), (), [bz])
            P.op("pool", lambda g, par_=par_: g.memset(Q2[par_][0], 0.0), (), [bz])
            P.op("pool", lambda g, par_=par_: g.memset(Q2[par_][1], 0.0), (), [bz])
            P.op("pool", lambda g, par_=par_: g.memset(QR2[par_], 0.0), (), [bz])
        P.op("pool", lambda g: g.memset(KrT, 0.0), (), [bz])
        for i_ in range(4):
            P.op("pool", lambda g, i_=i_: g.memset(PT[i_], 0.0), (), [bz])
        P.op("pool", lambda g: g.memset(onesM, 0.0), (), [bz])
        P.op("pool", lambda g: g.memset(onesS, 0.0), (), [bz])
        P.op("pool", lambda g: g.memset(onesM[0:NMETA], 1.0), [bz], [bz])
        P.op("pool", lambda g: g.memset(onesS[0:DSEQ], 1.0), [bz], [bz])
        P.barrier()
        rl = AR.alloc([512], F32); on0 = AR.alloc([512], F32); on1 = AR.alloc([512], F32)
        od = AR.alloc([512], F32); sq = AR.alloc([512], F32); rs = AR.alloc([512], F32)
        mix_st = [AR.alloc([512], BF16) for _ in range(2)]
        bKT2 = [Buf(), Buf()]; bV2 = [Buf(), Buf()]; bQ2 = [Buf(), Buf()]; bQR2 = [Buf(), Buf()]; bKrT = Buf()
        bPT = [Buf() for _ in range(4)]
        brl = Buf(); bon0 = Buf(); bon1 = Buf(); bod = Buf(); bsq = Buf(); brs = Buf(); bmix = [Buf(), Buf()]
        SETS = [dict(S=(0, 1), O=2, L=3, PT=(0, 1)), dict(S=(4, 5), O=6, L=7, PT=(2, 3))]

        def tile_list(kind, G, mla):
            tl = []
            if kind == "p":
                nq = 512
                tl.append(dict(koff=2 * NOWN * 128, nk=NMETA, c0=0,
                               bias=([(Bmeta, 0, 128, NMETA)] if (G == 0 and not mla) else [])))
                for j in range(4 * G + 4):
                    c0 = max(j - 4 * G, 0) * 128
                    b = []
                    if j >= 4 * G:
                        b.append((Mdiag if mla else Bdiag, c0, 128, 128))
                    tl.append(dict(koff=j * 128, nk=128, c0=c0, bias=b))
                    b = []
                    if j >= 4 * G:
                        b.append((Mo0 if mla else Bo0, c0, 128, 128))
                    if (not mla) and 4 * G <= j + 1 <= 4 * G + 3:
                        b.append((Bo1, (j + 1 - 4 * G) * 128, 128, 128))
                    tl.append(dict(koff=NOWN * 128 + j * 128, nk=128, c0=c0, bias=b))
                tl.sort(key=lambda d: d["c0"])
            else:
                nq = DSEQ
                for j in range(PAST // 128):
                    tl.append(dict(koff=j * 128, nk=128, c0=0, bias=([(Bs7, 0, 64, 128)] if (j == 7 and not mla) else [])))
                tl.append(dict(koff=PAST, nk=DSEQ, c0=0, bias=([(Bsn, 0, 64, 64)] if not mla else [])))
            return tl, nq

        steps = []
        maps = []
        mi = 0
        import os
        att_f = os.environ.get("MK_ATT", "sp")
        for kind, seq in (("s", SS_), ("p", SP_)):
            if kind not in att_f:
                continue
            nkey = seq["nkey"]; nqt = seq["nq"]
            ngroups = 1 if kind == "s" else 4
            for hi in range(2 * NH):
                mla = hi >= NH
                h = hi % NH
                par = hi % 2
                maps_head = []
                for G in range(ngroups):
                    for comp in ((0,) if mla else (0, 1)):
                        tl, nq = tile_list(kind, G, mla)
                        m = dict(kind=kind, seq=seq, mla=mla, h=h, par=par, G=G, comp=comp, tiles=tl, nq=nq,
                                 qbase=G * 512 if kind == "p" else 0, set=SETS[mi % 2], first_of_head=(G == 0 and comp == 0),
                                 nkey=nkey, nqt=nqt)
                        mi += 1
                        maps.append(m)
                        for ti in range(len(tl)):
                            steps.append((m, ti))

        def load_head(m):
            seq = m["seq"]; h = m["h"]; par = m["par"]; nkey = m["nkey"]; nqt = m["nqt"]; kind = m["kind"]
            nfull = (nkey // 128)
            rem = nkey - nfull * 128
            if m["mla"]:
                P.dma("sp", lambda g: g.dma_start(out=KT2[par][:, 0:nkey], in_=seq["KmT"][h]), (), [bKT2[par]])
                vsrc = seq["Vm"]
                P.dma("sp", lambda g: g.dma_start(out=Qn2[par][:, 0:nqt], in_=seq["QnT"][h]), (), [bQ2[par]])
                P.dma("sp", lambda g: g.dma_start(out=QR2[par][0:64, 0:nqt], in_=seq["QrT"][h]), (), [bQR2[par]])
                if h == 0:
                    P.dma("sp", lambda g: g.dma_start(out=KrT[0:64, 0:nkey], in_=seq["KrT"]), (), [bKrT])
            else:
                P.dma("sp", lambda g: g.dma_start(out=KT2[par][:, 0:nkey], in_=seq["KdT"][h]), (), [bKT2[par]])
                vsrc = seq["Vd"]
                P.dma("sp", lambda g: g.dma_start(out=Q2[par][0][0:64, 0:nqt], in_=seq["QdT"][h][0:64]), (), [bQ2[par]])
                P.dma("sp", lambda g: g.dma_start(out=Q2[par][1][64:128, 0:nqt], in_=seq["QdT"][h][64:128]), (), [bQ2[par]])
            dma_mid("sp", V2[par][:, 0:nfull, :], vsrc[0:nfull * 128, h * 128:(h + 1) * 128].rearrange("(t p) e -> p t e", p=128),
                    4, (), [bV2[par]])
            P.dma("sp", lambda g: g.dma_start(out=V2[par][0:rem, nfull, :], in_=vsrc[nfull * 128:nkey, h * 128:(h + 1) * 128]),
                  (), [bV2[par]])

        def stage1(m, ti):
            kt = m["tiles"][ti]; st = m["set"]; par = m["par"]
            nk = kt["nk"]; c0 = kt["c0"]; nq = m["nq"]; koff = kt["koff"]; qb = m["qbase"]
            Sb = st["S"][ti % 2]
            o = bank(Sb)[:, c0:nq]
            nb = len(kt["bias"])
            if m["mla"]:
                P.op("pe", lambda g: g.matmul(o, lhsT=KT2[par][:, koff:koff + 128], rhs=Qn2[par][:, qb + c0:qb + nq], start=True, stop=False),
                     [bKT2[par], bQ2[par]], [bankB[Sb]], signal=False)
                P.op("pe", lambda g: g.matmul(o, lhsT=KrT[:, koff:koff + 128], rhs=QR2[par][:, qb + c0:qb + nq], start=False, stop=(nb == 0)),
                     [bKrT, bQR2[par]], [bankB[Sb]], signal=(nb == 0))
            else:
                cc = m["comp"]
                P.op("pe", lambda g: g.matmul(o, lhsT=KT2[par][:, koff:koff + 128], rhs=Q2[par][cc][:, qb + c0:qb + nq], start=True, stop=(nb == 0)),
                     [bKT2[par], bQ2[par]], [bankB[Sb]], signal=(nb == 0))
            for bi_, (tab, coff, ncb, nkb) in enumerate(kt["bias"]):
                rhs = tab[:, 0:ncb] if m["mla"] else tab[:, m["h"], 0:ncb]
                P.op("pe", lambda g, rhs=rhs, coff=coff, ncb=ncb, last=(bi_ == nb - 1): g.matmul(
                    bank(Sb)[:, coff:coff + ncb], lhsT=ident, rhs=rhs, start=False, stop=last),
                    [b_const], [bankB[Sb]], signal=(bi_ == nb - 1))
            pt = st["PT"][ti % 2]
            scale = MLA_SCALE if m["mla"] else DIFF_SCALE
            P.op("act", lambda g: g.activation(out=PT[pt][:, c0:nq], in_=o, func=AF.Exp, scale=scale), [bankB[Sb]], [bPT[pt]])

        def stage2(m, ti):
            kt = m["tiles"][ti]; st = m["set"]; par = m["par"]
            nk = kt["nk"]; c0 = kt["c0"]; nq = m["nq"]; koff = kt["koff"]
            pt = st["PT"][ti % 2]
            nt_ = len(m["tiles"])
            first = (ti == 0); last = (ti == nt_ - 1)
            slot = koff // 128
            ones_t = ones_b if nk == 128 else (onesM if nk == NMETA else onesS)
            P.op("pe", lambda g: g.matmul(bank(st["O"])[:, c0:nq], lhsT=V2[par][:, slot, :], rhs=PT[pt][:, c0:nq], start=first, stop=last),
                 [bV2[par], bPT[pt]], [bankB[st["O"]]], signal=False)
            P.op("pe", lambda g: g.matmul(bank(st["L"])[:, c0:nq], lhsT=ones_t, rhs=PT[pt][:, c0:nq], start=first, stop=last),
                 [b_const, bPT[pt]], [bankB[st["L"]]], signal=True)
            if last:
                finish(m)

        def finish(m):
            st = m["set"]; nq = m["nq"]; seq = m["seq"]; qb = m["qbase"]; h = m["h"]
            O = bank(st["O"])[:, 0:nq]; L = bank(st["L"])[:, 0:nq]
            bO = bankB[st["O"]]; bL = bankB[st["L"]]
            P.op("dve", lambda g: g.reciprocal(out=rl[:, 0:nq], in_=L), [bL], [brl])
            if m["mla"]:
                ms = m["G"] % 2
                P.op("dve", lambda g: g.tensor_tensor(out=mix_st[ms][:, 0:nq], in0=O, in1=rl[:, 0:nq], op=ALU.mult), [bO, brl], [bmix[ms]])
                P.dma("pool", lambda g: g.dma_start(out=seq["MixT"][NH + h][:, qb:qb + nq], in_=mix_st[ms][:, 0:nq]), [bmix[ms]], ())
                return
            if m["comp"] == 0:
                P.op("dve", lambda g: g.tensor_tensor(out=on0[:, 0:nq], in0=O, in1=rl[:, 0:nq], op=ALU.mult), [bO, brl], [bon0])
                return
            P.op("dve", lambda g: g.tensor_tensor(out=on1[:, 0:nq], in0=O, in1=rl[:, 0:nq], op=ALU.mult), [bO, brl], [bon1])
            P.op("dve", lambda g: g.scalar_tensor_tensor(out=od[:, 0:nq], in0=on1[:, 0:nq], scalar=neg_lam, in1=on0[:, 0:nq],
                                                         op0=ALU.mult, op1=ALU.add), [bon0, bon1, b_const], [bod])
            P.op("act", lambda g: g.activation(out=sq[:, 0:nq], in_=od[:, 0:nq], func=AF.Square), [bod], [bsq])
            Sb = st["S"][0]
            P.op("pe", lambda g: g.matmul(bank(Sb)[:, 0:nq], lhsT=ones_f, rhs=sq[:, 0:nq], start=True, stop=True), [bsq, b_const], [bankB[Sb]])
            P.op("dve", lambda g: g.tensor_scalar(out=rs[:, 0:nq], in0=bank(Sb)[:, 0:nq], scalar1=1.0 / 128, scalar2=EPS,
                                                  op0=ALU.mult, op1=ALU.add), [bankB[Sb]], [brs])
            P.op("act", lambda g: g.activation(out=rs[:, 0:nq], in_=rs[:, 0:nq], func=AF.Sqrt), [brs], [brs])
            P.op("dve", lambda g: g.reciprocal(out=rs[:, 0:nq], in_=rs[:, 0:nq]), [brs], [brs])
            ms = m["G"] % 2
            P.op("dve", lambda g: g.scalar_tensor_tensor(out=mix_st[ms][:, 0:nq], in0=od[:, 0:nq], scalar=gsub8, in1=rs[:, 0:nq],
                                                         op0=ALU.mult, op1=ALU.mult), [bod, brs, b_const], [bmix[ms]])
            P.dma("pool", lambda g: g.dma_start(out=seq["MixT"][h][:, qb:qb + nq], in_=mix_st[ms][:, 0:nq]), [bmix[ms]], ())

        head_first = [i for i, mm in enumerate(maps) if mm["first_of_head"]]
        load_head(maps[head_first[0]])
        nxt = {head_first[k]: head_first[k + 1] for k in range(len(head_first) - 1)}
        seen = set()
        prev = None
        every = max(1, len(steps) // (len(deferred_late) + 1))
        for si_, (m, ti) in enumerate(steps):
            if deferred_late and si_ % every == every - 1:
                deferred_late.pop(0)()
            stage1(m, ti)
            if prev is not None:
                stage2(*prev)
            prev = (m, ti)
            idm = id(m)
            if idm not in seen:
                seen.add(idm)
                k = maps.index(m)
                if k in nxt:
                    load_head(maps[nxt[k]])
        stage2(*prev)

    def phase3():
        AR.off = persist_mark
        TMAX = 576
        g_bc = AR.alloc([D], F32)
        mixh = AR.alloc([16, TMAX], BF16)
        x1 = AR.alloc([5, D], F32)
        gated = AR.alloc([NFC, TMAX], BF16)
        _o = AR.off
        wo2 = [AR.alloc([16, 512], BF16) for _ in range(2)]
        AR.off = _o
        wgu = [[AR.alloc([16, 256], BF16) for _ in range(2)] for _ in range(2)]
        wd2 = [AR.alloc([NFC, 128], BF16) for _ in range(2)]
        hb = AR.alloc([D], BF16)
        sg = [AR.alloc([TMAX], BF16) for _ in range(2)]
        dT = [AR.alloc([TMAX], F32) for _ in range(2)]
        ss = AR.alloc([1], F32); rstd = AR.alloc([1], F32)
        bg = Buf(); bmixh = Buf(); bx1 = [Buf() for _ in range(5)]; bgated = Buf()
        bwgu = [[Buf(), Buf()], [Buf(), Buf()]]; bwd2 = [Buf(), Buf()]; bhb = Buf(); bsg = [Buf(), Buf()]; bdT = [Buf(), Buf()]
        bst = Buf()
        own = lambda i: dict(seq=SP_, q0=i * 128, n=128, x=xo[i * 128:(i + 1) * 128], y=y_o[i * 128:(i + 1) * 128])
        groups = [[dict(seq=SS_, q0=0, n=DSEQ, x=xsam, y=y_s)] + [own(i) for i in range(4)]]
        for G in range(1, 4):
            groups.append([own(i) for i in range(4 * G, 4 * G + 4)])
        cnt = {"wo": 0, "gu": 0, "wd": 0, "sg": 0, "dT": 0}

        def do_group(tiles):
            off = 0
            for t in tiles:
                t["off"] = off
                off += t["n"]
            T = off
            chunks = []
            cur = []
            for t in tiles:
                if cur and (t["off"] + t["n"] - cur[0]["off"] > 512 or t["seq"] is not cur[0]["seq"]):
                    chunks.append(cur); cur = []
                cur.append(t)
            chunks.append(cur)
            chunks = [(c[0]["off"], c[-1]["off"] + c[-1]["n"], c) for c in chunks]
            P.dma("sp", lambda g: g.dma_start(out=g_bc, in_=g_ffn.partition_broadcast(128)), (), [bg])
            for (c0, c1, ct) in chunks:
                seq = ct[0]["seq"]; q0 = ct[0]["q0"]
                dma_mid("sp", mixh[:, :, c0:c1], seq["MixT"][:, :, q0:q0 + (c1 - c0)].rearrange("c p t -> p c t"), 4, (), [bmixh])
            for k, t in enumerate(tiles):
                P.dma("sp", lambda g, k=k, t=t: g.dma_start(out=x1[0:t["n"], k, :], in_=t["x"]), (), [bx1[k]])
            for cg in range(4):
                p = cnt["wo"] % 2; cnt["wo"] += 1
                P.dma("sp", lambda g, p=p, cg=cg: g.dma_start(out=wo2[p], in_=w_out_b[cg]), [wbuf["w_out"]], bwgu[p])
                for k, t in enumerate(tiles):
                    n = t["n"]; o_ = t["off"]
                    bi = next_bank()
                    mm_group(bi, n, 512, lambda c, o_=o_, n=n: mixh[:, c, o_:o_ + n], lambda c, p=p: wo2[p][:, c, :], 16, [bmixh] + bwgu[p])
                    P.op("dve", lambda g, bi=bi, k=k, cg=cg, n=n: g.tensor_tensor(out=x1[0:n, k, cg * 512:(cg + 1) * 512], in0=bank(bi)[0:n, 0:512],
                                                                                   in1=x1[0:n, k, cg * 512:(cg + 1) * 512], op=ALU.add),
                         [bankB[bi], bx1[k]], [bx1[k]])
            for k, t in enumerate(tiles):
                n = t["n"]; o_ = t["off"]
                P.op("act", lambda g, k=k, n=n: g.activation(out=hb[0:n], in_=x1[0:n, k, :], func=AF.Square, accum_out=ss[0:n]), [bx1[k]], [bhb, bst])
                rstd_from_ss(ss, rstd, n, D, bst)
                P.op("dve", lambda g, k=k, n=n: g.scalar_tensor_tensor(out=hb[0:n], in0=x1[0:n, k, :], scalar=rstd[0:n], in1=g_bc[0:n],
                                                                        op0=ALU.mult, op1=ALU.mult), [bx1[k], bst, bg], [bhb])
                for j0 in (0, 8):
                    bi = next_bank()
                    bv = bank(bi, BF16).rearrange("p (j t) -> p j t", t=128)
                    for j in range(j0, j0 + 8):
                        P.op("pe", lambda g, j=j, j0=j0, bv=bv, n=n: g.transpose(out=bv[:, j - j0, 0:n], in_=hb[0:n, j * 128:(j + 1) * 128], identity=ident[0:n, 0:n]),
                             [bhb, b_const], [bankB[bi]], signal=(j == j0 + 7))
                    copy_op("dve", mixh[:, j0:j0 + 8, o_:o_ + n], bv[:, :, 0:n], [bankB[bi]], [bmixh])
            P.dma("sp", lambda g: g.dma_start(out=g_bc, in_=g_fin.partition_broadcast(128)), (), [bg])
            for f2 in range(NFC // 2):
                p = cnt["gu"] % 2; cnt["gu"] += 1
                P.dma("sp", lambda g, p=p, f2=f2: g.dma_start(out=wgu[0][p], in_=wg_b[f2]), [wbuf["wg"]], [bwgu[0][p]])
                P.dma("sp", lambda g, p=p, f2=f2: g.dma_start(out=wgu[1][p], in_=wu_b[f2]), [wbuf["wu"]], [bwgu[1][p]])
                for sub in range(2):
                    f = f2 * 2 + sub
                    s_ = cnt["sg"] % 2; cnt["sg"] += 1
                    for (c0, c1, ct) in chunks:
                        bg_ = next_bank(); bu_ = next_bank()
                        mm_group(bg_, 128, c1 - c0, lambda c, p=p, sub=sub: wgu[0][p][:, c, sub * 128:(sub + 1) * 128],
                                 lambda c, c0=c0, c1=c1: mixh[:, c, c0:c1], 16, [bmixh, bwgu[0][p]])
                        mm_group(bu_, 128, c1 - c0, lambda c, p=p, sub=sub: wgu[1][p][:, c, sub * 128:(sub + 1) * 128],
                                 lambda c, c0=c0, c1=c1: mixh[:, c, c0:c1], 16, [bmixh, bwgu[1][p]])
                        w = c1 - c0
                        P.op("act", lambda g, bg_=bg_, s_=s_, c0=c0, c1=c1, w=w: g.activation(out=sg[s_][:, c0:c1], in_=bank(bg_)[:, 0:w], func=AF.Silu),
                             [bankB[bg_]], [bsg[s_]])
                        P.op("dve", lambda g, bu_=bu_, s_=s_, f=f, c0=c0, c1=c1, w=w: g.tensor_tensor(out=gated[:, f, c0:c1], in0=bank(bu_)[:, 0:w],
                                                                                                     in1=sg[s_][:, c0:c1], op=ALU.mult),
                             [bankB[bu_], bsg[s_]], [bgated])
            for j in range(16):
                p = cnt["wd"] % 2; cnt["wd"] += 1
                P.dma("sp", lambda g, p=p, j=j: g.dma_start(out=wd2[p], in_=wd_b[j]), [wbuf["wd"]], [bwd2[p]])
                d_ = cnt["dT"] % 2; cnt["dT"] += 1
                for (c0, c1, ct) in chunks:
                    w = c1 - c0
                    bi = next_bank()
                    mm_group(bi, 128, w, lambda c, p=p: wd2[p][:, c, :], lambda c, c0=c0, c1=c1: gated[:, c, c0:c1], NFC, [bgated, bwd2[p]])
                    P.op("act", lambda g, bi=bi, d_=d_, c0=c0, c1=c1, w=w: g.activation(out=dT[d_][:, c0:c1], in_=bank(bi)[:, 0:w], func=AF.Copy),
                         [bankB[bi]], [bdT[d_]])
                    bt = next_bank()
                    for kk, t in enumerate(ct):
                        n = t["n"]; o_ = t["off"]
                        P.op("pe", lambda g, kk=kk, bt=bt, d_=d_, n=n, o_=o_: g.transpose(out=bank(bt)[0:n, kk * 128:(kk + 1) * 128],
                                                                                          in_=dT[d_][:, o_:o_ + n], identity=ident_f),
                             [bdT[d_], b_const], [bankB[bt]], signal=(kk == len(ct) - 1))
                    for kk, t in enumerate(ct):
                        n = t["n"]; k = tiles.index(t)
                        P.op("dve", lambda g, kk=kk, bt=bt, j=j, n=n, k=k: g.tensor_tensor(out=x1[0:n, k, j * 128:(j + 1) * 128],
                                                                                           in0=bank(bt)[0:n, kk * 128:(kk + 1) * 128],
                                                                                           in1=x1[0:n, k, j * 128:(j + 1) * 128], op=ALU.add),
                             [bankB[bt], bx1[k]], [bx1[k]])
            for k, t in enumerate(tiles):
                n = t["n"]
                P.op("act", lambda g, k=k, n=n: g.activation(out=hb[0:n], in_=x1[0:n, k, :], func=AF.Square, accum_out=ss[0:n]), [bx1[k]], [bhb, bst])
                rstd_from_ss(ss, rstd, n, D, bst)
                P.op("dve", lambda g, k=k, n=n: g.scalar_tensor_tensor(out=x1[0:n, k, :], in0=x1[0:n, k, :], scalar=rstd[0:n], in1=g_bc[0:n],
                                                                        op0=ALU.mult, op1=ALU.mult), [bx1[k], bst, bg], [bx1[k]])
                P.dma("pool", lambda g, k=k, t=t, n=n: g.dma_start(out=t["y"], in_=x1[0:n, k, :]), [bx1[k]], (), is_output=True)

        import os
        if os.environ.get("MK_P3", "") == "s":
            groups[:] = [groups[0][:1]]
        for tiles in groups:
            do_group(tiles)

    cast("w_ukv", w_ukv_b, w_ukv)
    cast("ck", ck_b, ck)
    cast("cckv", cckv_b, cckv)
    cast("ckr", ckr_b, ckr)
    cast("w_in", w_in_b, w_in)
    cast("cv", SS_["Vd"][0:PAST], cv)
    cast("w_uq", w_uq_b, w_uq)
    cast("w_out", w_out_b, w_out, defer=True, slab=512)
    cast("wg", wg_b, wg, defer=True, slab=256, late=True)
    cast("wu", wu_b, wu, defer=True, slab=256, late=True)
    cast("wd", wd_b, wd, defer=True, slab=128, late=True)
    import os
    stop = int(os.environ.get("MK_STOP", "9"))
    setup()
    P.barrier()
    if stop >= 1:
        passA()
        P.barrier()
    if stop >= 2:
        passB()
        P.barrier()
    if stop >= 3:
        attention()
    while deferred_late:
        deferred_late.pop(0)()
    if stop >= 3:
        P.barrier(final=True)
    if stop >= 4:
        phase3()
    while deferred:
        deferred.pop(0)()
    P.finalize()
    P.emit()
    return nc, P


def _t5_onehot(delta):
    rel = (np.arange(255, dtype=np.int32) - 127 + delta)
    try:
        import jax
        import jax.numpy as jnp
        with jax.default_device(jax.devices("cpu")[0]):
            r = jnp.asarray(rel)
            nb = 16
            max_exact = 8
            ret = jnp.where(r > 0, nb, 0)
            n = jnp.abs(r)
            large = max_exact + (jnp.log(jnp.maximum(n, 1).astype(jnp.float32) / max_exact)
                                 / math.log(128 / max_exact) * (nb - max_exact)).astype(jnp.int32)
            large = jnp.minimum(large, nb - 1)
            bucket = np.asarray(ret + jnp.where(n < max_exact, n, large))
    except Exception:
        nb, max_exact = 16, 8
        ret = np.where(rel > 0, nb, 0)
        n = np.abs(rel)
        large = max_exact + (np.log(np.maximum(n, 1).astype(np.float32) / np.float32(max_exact))
                             / np.float32(math.log(128 / max_exact)) * np.float32(nb - max_exact)).astype(np.int32)
        large = np.minimum(large, nb - 1)
        bucket = ret + np.where(n < max_exact, n, large)
    ohm = np.zeros((32, 255), np.float32)
    ohm[bucket, np.arange(255)] = 1.0
    ohm[15, :] -= 1.0
    return ohm


def _rope_table(pos):
    half = 32
    inv_freq = (np.float32(10000.0) ** (-np.arange(half, dtype=np.float32) / np.float32(half))).astype(np.float32)
    ang = pos.astype(np.float32)[:, None] * inv_freq[None, :]
    return np.concatenate([np.cos(ang), np.sin(ang)], axis=1).astype(np.float32)


_CACHE = {}


def kernel(x_prompt, x_sample, cache_diff_k, cache_diff_v, cache_mla_ckv, cache_mla_krope,
           meta_tokens, rel_bias, norm_attn_g, w_in, diff_lambda, diff_subln_g,
           mla_q_norm_g, mla_w_uq, mla_kv_norm_g, mla_w_ukv, w_out, norm_ffn_g,
           ffn_w_gate, ffn_w_up, ffn_w_down, final_norm_g):
    f = lambda a: np.ascontiguousarray(np.asarray(a, dtype=np.float32))
    x_prompt = f(x_prompt); x_sample = f(x_sample)
    if "nc" not in _CACHE:
        _CACHE["nc"] = build_program()[0]
    nc = _CACHE["nc"]
    ohs = np.stack([_t5_onehot(0), _t5_onehot(-128), _t5_onehot(-16)])
    kq = np.arange(128)
    md = np.where((kq[None, :] < 64) & (kq[:, None] >= 64), MASKV, 0.0).astype(np.float32)
    shared = dict(
        w_in=f(w_in[0]), w_uq=f(mla_w_uq[0]), w_ukv=f(mla_w_ukv[0]), w_out=f(w_out[0]),
        wg=f(ffn_w_gate[0]), wu=f(ffn_w_up[0]), wd=f(ffn_w_down[0]),
        g_attn=f(norm_attn_g[0]), g_ffn=f(norm_ffn_g[0]), g_fin=f(final_norm_g),
        g_q=f(mla_q_norm_g[0]), g_kv=f(mla_kv_norm_g[0]), g_sub=f(diff_subln_g[0]),
        lam4=f(diff_lambda[0]).reshape(256), relb=f(rel_bias), xmeta=f(meta_tokens),
        oh=ohs, mask_diag=md, rope_m=_rope_table(np.arange(NMETA)), rope_s=_rope_table(PAST + np.arange(DSEQ)),
    )
    in_maps = []
    for c in range(8):
        b, par = c // 2, c % 2
        own = np.arange(NOWN) * 2 + par
        oth = np.arange(NOWN) * 2 + (1 - par)
        xb = x_prompt[b].reshape(32, 128, D)
        pos_o = (NMETA + own[:, None] * 128 + np.arange(128)[None, :]).reshape(-1)
        pos_t = (NMETA + oth[:, None] * 128 + np.arange(128)[None, :]).reshape(-1)
        selv = np.zeros((128, 4), np.float32)
        if par == 1:
            selv[:, 0] = 1.0
            mo = np.zeros((128, 128), np.float32)
        else:
            selv[:, 1] = 1.0
            selv[:, 2] = 1.0
            mo = np.full((128, 128), MASKV, np.float32)
        m = dict(shared)
        m.update(
            xo=np.ascontiguousarray(xb[own].reshape(-1, D)), xt=np.ascontiguousarray(xb[oth].reshape(-1, D)),
            xsam=x_sample[c],
            ck=f(cache_diff_k[0, c]).reshape(PAST, 1024), cv=f(cache_diff_v[0, c]).reshape(PAST, 1024),
            cckv=f(cache_mla_ckv[0, c]), ckr=f(cache_mla_krope[0, c]),
            rope_o=_rope_table(pos_o), rope_t=_rope_table(pos_t), mask_o0=mo, sel=selv,
        )
        in_maps.append(m)
    if _CACHE.get("prep_only"):
        return in_maps
    res = run_bass_kernel_spmd(nc, in_maps, core_ids=list(range(8))).results
    B = 4
    L = NMETA + SEQ
    y_prompt = np.zeros((B, SEQ, D), np.float32)
    nk = np.zeros((1, B, L, 1024), np.float32); nv = np.zeros((1, B, L, 1024), np.float32)
    nckv = np.zeros((1, B, L, KVL), np.float32); nkr = np.zeros((1, B, L, ROPE), np.float32)
    y_sample = np.zeros((8, DSEQ, D), np.float32)
    sk = np.zeros((1, 8, DSEQ, 1024), np.float32); sv = np.zeros((1, 8, DSEQ, 1024), np.float32)
    sckv = np.zeros((1, 8, DSEQ, KVL), np.float32); skr = np.zeros((1, 8, DSEQ, ROPE), np.float32)
    for c in range(8):
        b, par = c // 2, c % 2
        r = res[c]
        for i in range(NOWN):
            gt = 2 * i + par
            sl = slice(i * 128, (i + 1) * 128)
            y_prompt[b, gt * 128:(gt + 1) * 128] = r["y_o"][sl]
            dst = slice(NMETA + gt * 128, NMETA + (gt + 1) * 128)
            nk[0, b, dst] = r["kd_o"][sl]; nv[0, b, dst] = r["vd_o"][sl]
            nckv[0, b, dst] = r["ckv_o"][sl]; nkr[0, b, dst] = r["kr_o"][sl]
        if par == 0:
            nk[0, b, 0:NMETA] = r["kd_m"]; nv[0, b, 0:NMETA] = r["vd_m"]
            nckv[0, b, 0:NMETA] = r["ckv_m"]; nkr[0, b, 0:NMETA] = r["kr_m"]
        y_sample[c] = r["y_s"]
        sk[0, c] = r["kd_s"]; sv[0, c] = r["vd_s"]; sckv[0, c] = r["ckv_s"]; skr[0, c] = r["kr_s"]
    return (y_prompt, y_sample,
            nk.reshape(1, B, L, NH, 2, DH), nv.reshape(1, B, L, NH, 128), nckv, nkr,
            sk.reshape(1, 8, DSEQ, NH, 2, DH), sv.reshape(1, 8, DSEQ, NH, 128), sckv, skr)
```

```python
import contextlib
import math
import numpy as np
import concourse.bass as bass
import concourse.mybir as mybir
from concourse.bass_utils import run_bass_kernel_spmd

F32 = mybir.dt.float32
BF16 = mybir.dt.bfloat16
U8 = mybir.dt.uint8
AF = mybir.ActivationFunctionType
ALU = mybir.AluOpType
AX = mybir.AxisListType

D = 2048
SEQ = 4096
NMETA = 16
NH = 8
DH = 64
DIN = 3904
OFF_K, OFF_V, OFF_CQ, OFF_CKV, OFF_KR = 1024, 2048, 3072, 3584, 3840
QL, KVL, ROPE = 512, 256, 64
DFF = 5632
NFC = DFF // 128
EPS = 1e-6
DIFF_SCALE = DH ** -0.5
MLA_SCALE = 192 ** -0.5
LAM_INIT = 0.8 - 0.6 * math.exp(0.0)
PAST = 1024
DSEQ = 64
NOWN = 16
NKEY = 2 * NOWN * 128 + NMETA
SKEY = PAST + DSEQ
MASKV = -30000.0

ENGINES = ("pe", "act", "dve", "pool", "sp")
EPOCH = 12000
N_EPOCH = {"pe": 8, "act": 10, "dve": 10, "pool": 3, "sp": 1}


class Buf:
    __slots__ = ("name", "w", "r", "excl")

    def __init__(self, name="", excl=False):
        self.name = name
        self.w = None
        self.r = {}
        self.excl = excl


class Prog:
    def __init__(self, nc, n_dma_sp=30, n_dma_pool=14, n_dma_act=2):
        self.nc = nc
        self.ops = {e: [] for e in ENGINES}
        self.sem_handles = []
        self.sem_names = []
        self.sem_owner = {}
        self.eng_sems = {}
        for e in ENGINES:
            self.eng_sems[e] = [self._new_sem(f"s_{e}{i}", e) for i in range(N_EPOCH[e])]
        self.cnt = {e: 0 for e in ENGINES}
        self.know = {e: {} for e in ENGINES}
        self.tok_know = {}
        self.dma_sems = {
            "sp": [[self._new_sem(f"d_sp{i}", None), 0] for i in range(n_dma_sp)],
            "pool": [[self._new_sem(f"d_pool{i}", None), 0] for i in range(n_dma_pool)],
            "act": [[self._new_sem(f"d_act{i}", None), 0] for i in range(n_dma_act)],
        }
        self.dma_rr = {"sp": 0, "pool": 0, "act": 0}
        self.out_tokens = []
        self.pending_dma = []
        self.late_dma = []
        self.own_done = {e: {} for e in ENGINES}
        self.n_waits = 0
        self.n_ops = {e: 0 for e in ENGINES}

    def _new_sem(self, name, owner):
        self.sem_names.append(name)
        i = len(self.sem_names) - 1
        self.sem_owner[i] = owner
        return i

    def _next_tok(self, e):
        c = self.cnt[e]
        return (self.eng_sems[e][c // EPOCH], c % EPOCH + 1)

    def _resolve(self, e, reads, writes, extra=()):
        deps = {}

        def add(tok, raw):
            if tok is None:
                return
            s, v = tok
            if (not raw) and self.sem_owner[s] == e and e == "pe":
                return
            if deps.get(s, -1) < v:
                deps[s] = v

        for b in reads:
            add(b.w, True)
            if b.excl:
                for s, v in b.r.items():
                    if self.sem_owner[s] != e:
                        add((s, v), True)
        for b in writes:
            add(b.w, False)
            for s, v in b.r.items():
                add((s, v), False)
        for tok in extra:
            add(tok, True)
        know = self.know[e]
        waits = [(s, v) for s, v in deps.items() if know.get(s, -1) < v]
        for s, v in waits:
            tk = self.tok_know.get((s, v))
            if tk:
                for s2, v2 in tk.items():
                    if know.get(s2, -1) < v2:
                        know[s2] = v2
            if know.get(s, -1) < v:
                know[s] = v
        self.n_waits += len(waits)
        return waits

    def _finish(self, tok, reads, writes, e):
        if tok not in self.tok_know:
            self.tok_know[tok] = dict(self.know[e])
        else:
            self.tok_know[tok].update(self.know[e])
        if self.own_done[e]:
            self.tok_know[tok].update(self.own_done[e])
        s, v = tok
        for b in reads:
            if b.r.get(s, -1) < v:
                b.r[s] = v
        for b in writes:
            b.w = tok
            b.r = {}

    def op(self, e, fn, reads=(), writes=(), signal=True, extra=()):
        waits = self._resolve(e, reads, writes, extra)
        tok = self._next_tok(e)
        self.n_ops[e] += 1
        if signal:
            self.cnt[e] += 1
            self.ops[e].append((waits, fn, (tok[0], 1)))
            if self.cnt[e] % EPOCH == 0:
                self.own_done[e][tok[0]] = EPOCH
        else:
            self.ops[e].append((waits, fn, None))
        self._finish(tok, reads, writes, e)
        return tok

    def dma(self, q, fn, reads=(), writes=(), is_output=False, extra=(), late=False):
        pool = self.dma_sems[q]
        i = self.dma_rr[q]
        self.dma_rr[q] = (i + 1) % len(pool)
        s, v = pool[i]
        prev = [(s, v)] if v > 0 else []
        waits = self._resolve(q, reads, writes, tuple(extra) + tuple(prev))
        pool[i][1] = v + 16
        tok = (s, v + 16)
        self.n_ops[q] += 1
        self.ops[q].append((waits, fn, (s, 16)))
        self._finish(tok, reads, writes, q)
        (self.late_dma if late else self.pending_dma).append(tok)
        if is_output:
            self.out_tokens.append(tok)
        return tok

    def barrier(self, final=False):
        toks = list(self.pending_dma)
        if final:
            toks += self.late_dma
            self.late_dma = []
        for e in ENGINES:
            c = self.cnt[e]
            if c > 0:
                cc = c - 1
                toks.append((self.eng_sems[e][cc // EPOCH], cc % EPOCH + 1))
        for e in ENGINES:
            waits = self._resolve(e, (), (), tuple(toks))
            self.ops[e].append((waits, None, None))
        self.pending_dma = []

    def finalize(self):
        waits = self._resolve("sp", (), (), tuple(self.out_tokens))
        self.ops["sp"].append((waits, None, None))

    def emit(self):
        nc = self.nc
        with contextlib.ExitStack() as st:
            for nm in self.sem_names:
                self.sem_handles.append(st.enter_context(nc.semaphore(nm)))
            block = st.enter_context(nc.Block())
            H = self.sem_handles

            def run(eng, ops):
                for waits, fn, inc in ops:
                    for s, v in waits:
                        eng.wait_ge(H[s], v)
                    if fn is None:
                        continue
                    ins = fn(eng)
                    if inc is not None:
                        ins.then_inc(H[inc[0]], inc[1])

            @block.sync
            def _(eng):
                run(eng, self.ops["sp"])

            @block.tensor
            def _(eng):
                run(eng, self.ops["pe"])

            @block.scalar
            def _(eng):
                run(eng, self.ops["act"])

            @block.vector
            def _(eng):
                run(eng, self.ops["dve"])

            @block.gpsimd
            def _(eng):
                run(eng, self.ops["pool"])


class Arena:
    def __init__(self, ap):
        self.ap = ap
        self.off = 0
        self.size = ap.shape[1]

    def alloc(self, shape, dt):
        es = 4 if dt == F32 else 2
        n = int(np.prod(shape)) * es
        assert self.off + n <= self.size, f"arena overflow {self.off}+{n}>{self.size}"
        v = self.ap[:, self.off:self.off + n].bitcast(dt)
        self.off += (n + 63) // 64 * 64
        if len(shape) == 2:
            v = v.rearrange("p (a b) -> p a b", b=shape[1])
        elif len(shape) == 3:
            v = v.rearrange("p (a b c) -> p a b c", b=shape[1], c=shape[2])
        return v


def build_program():
    nc = bass.Bass("TRN2", target_bir_lowering=False)
    P = Prog(nc)

    def din(name, shape):
        return nc.dram_tensor(name, list(shape), F32, kind="ExternalInput").ap()

    def dout(name, shape):
        return nc.dram_tensor(name, list(shape), F32, kind="ExternalOutput").ap()

    def dscr(name, shape, dt=BF16):
        return nc.dram_tensor(name, list(shape), dt).ap()

    xo = din("xo", [NOWN * 128, D]); xt = din("xt", [NOWN * 128, D])
    xmeta = din("xmeta", [NMETA, D]); xsam = din("xsam", [DSEQ, D])
    ck = din("ck", [PAST, 1024]); cv = din("cv", [PAST, 1024])
    cckv = din("cckv", [PAST, KVL]); ckr = din("ckr", [PAST, ROPE])
    w_in = din("w_in", [D, DIN]); w_uq = din("w_uq", [QL, 1536]); w_ukv = din("w_ukv", [KVL, 2048])
    w_out = din("w_out", [D, D]); wg = din("wg", [D, DFF]); wu = din("wu", [D, DFF]); wd = din("wd", [DFF, D])
    g_attn = din("g_attn", [D]); g_ffn = din("g_ffn", [D]); g_fin = din("g_fin", [D])
    g_q = din("g_q", [QL]); g_kv = din("g_kv", [KVL]); g_sub = din("g_sub", [128])
    lam4 = din("lam4", [256]); relb = din("relb", [32, 8])
    rope_o = din("rope_o", [NOWN * 128, 64]); rope_t = din("rope_t", [NOWN * 128, 64])
    rope_m = din("rope_m", [NMETA, 64]); rope_s = din("rope_s", [DSEQ, 64])
    oh = din("oh", [3, 32, 255])
    mask_diag = din("mask_diag", [128, 128]); mask_o0 = din("mask_o0", [128, 128])
    sel = din("sel", [128, 4])
    y_o = dout("y_o", [NOWN * 128, D]); kd_o = dout("kd_o", [NOWN * 128, 1024]); vd_o = dout("vd_o", [NOWN * 128, 1024])
    ckv_o = dout("ckv_o", [NOWN * 128, KVL]); kr_o = dout("kr_o", [NOWN * 128, ROPE])
    kd_m = dout("kd_m", [NMETA, 1024]); vd_m = dout("vd_m", [NMETA, 1024])
    ckv_m = dout("ckv_m", [NMETA, KVL]); kr_m = dout("kr_m", [NMETA, ROPE])
    y_s = dout("y_s", [DSEQ, D]); kd_s = dout("kd_s", [DSEQ, 1024]); vd_s = dout("vd_s", [DSEQ, 1024])
    ckv_s = dout("ckv_s", [DSEQ, KVL]); kr_s = dout("kr_s", [DSEQ, ROPE])
    w_in_b = dscr("w_in_b", [D, DIN]); w_uq_b = dscr("w_uq_b", [QL, 1536]); w_ukv_b = dscr("w_ukv_b", [KVL, 2048])
    w_out_b = dscr("w_out_b", [4, 128, 16, 512]); wg_b = dscr("wg_b", [NFC // 2, 128, 16, 256])
    wu_b = dscr("wu_b", [NFC // 2, 128, 16, 256]); wd_b = dscr("wd_b", [16, 128, NFC, 128])
    ck_b = dscr("ck_b", [PAST, 1024]); cckv_b = dscr("cckv_b", [PAST, KVL]); ckr_b = dscr("ckr_b", [PAST, ROPE])

    def seq_scratch(pfx, nkey, nq):
        return dict(
            KdT=dscr(pfx + "KdT", [NH, 128, nkey]), Vd=dscr(pfx + "Vd", [nkey, 1024]),
            KmT=dscr(pfx + "KmT", [NH, 128, nkey]), Vm=dscr(pfx + "Vm", [nkey, 1024]),
            KrT=dscr(pfx + "KrT", [64, nkey]),
            QdT=dscr(pfx + "QdT", [NH, 128, nq]), QnT=dscr(pfx + "QnT", [NH, 128, nq]),
            QrT=dscr(pfx + "QrT", [NH, 64, nq]), MixT=dscr(pfx + "MixT", [16, 128, nq]),
            nkey=nkey, nq=nq)

    SP_ = seq_scratch("p", NKEY, NOWN * 128)
    SS_ = seq_scratch("s", SKEY, DSEQ)

    arena_ap = nc.alloc_sbuf_tensor("arena", [128, 206 * 1024], U8).ap()
    AR = Arena(arena_ap)
    PS = nc.alloc_psum_tensor("ps", [128, 4096], F32).ap()
    bankB = [Buf(f"bank{i}", excl=True) for i in range(8)]

    def bank(i, dt=F32):
        v = PS[:, i * 512:(i + 1) * 512]
        return v.bitcast(dt) if dt != F32 else v

    rr = {"b": 0, "cp": 0}

    def next_bank(lo=0, hi=8):
        i = lo + rr["b"] % (hi - lo)
        rr["b"] += 1
        return i

    def copy_eng():
        rr["cp"] += 1
        return "dve"

    def copy_op(e, out, in_, r, w):
        if e == "act":
            P.op("act", lambda g: g.activation(out=out, in_=in_, func=AF.Copy), r, w)
        else:
            P.op(e, lambda g: g.tensor_copy(out, in_), r, w)

    ident_f = AR.alloc([128], F32); ident = AR.alloc([128], BF16)
    ones_b = AR.alloc([128], BF16); ones_f = AR.alloc([128], F32)
    lamb = AR.alloc([256], F32); lprod = AR.alloc([2, 64], F32); lsum = AR.alloc([2], F32)
    lexp = AR.alloc([2], F32); lam_t = AR.alloc([1], F32); neg_lam = AR.alloc([1], F32)
    gsub = AR.alloc([1], F32); gsub8 = AR.alloc([1], F32)
    sel_s = AR.alloc([4], F32)
    Bdiag = AR.alloc([NH, 128], BF16); Bo0 = AR.alloc([NH, 128], BF16); Bo1 = AR.alloc([NH, 128], BF16)
    Bmeta = AR.alloc([NH, 128], BF16); Bs7 = AR.alloc([NH, 64], BF16); Bsn = AR.alloc([NH, 64], BF16)
    Mdiag = AR.alloc([128], BF16); Mo0 = AR.alloc([128], BF16)
    b_const = Buf("const")
    persist_mark = AR.off

    def setup():
        R = AR.alloc([8], F32); R8 = AR.alloc([8], F32)
        oh_s = AR.alloc([3, 255], F32)
        md_f = AR.alloc([128], F32); mo_f = AR.alloc([128], F32)
        bR = Buf(); bT = Buf()
        P.op("pool", lambda g: g.memset(ident_f, 0.0), (), [bT])
        P.op("pool", lambda g: g.affine_select(ident_f, ident_f, pattern=[[-1, 128]], compare_op=ALU.not_equal,
                                               fill=1.0, base=0, channel_multiplier=1), [bT], [bT])
        P.op("pool", lambda g: g.memset(ones_f, 1.0), (), [bT])
        P.op("dve", lambda g: g.tensor_copy(ident, ident_f), [bT], [b_const])
        P.op("dve", lambda g: g.tensor_copy(ones_b, ones_f), [bT], [b_const])
        P.dma("sp", lambda g: g.dma_start(out=lamb, in_=lam4.partition_broadcast(128)), (), [bR])
        P.dma("sp", lambda g: g.dma_start(out=gsub, in_=g_sub.rearrange("(p o) -> p o", o=1)), (), [bR])
        P.dma("sp", lambda g: g.dma_start(out=sel_s, in_=sel), (), [bR])
        P.dma("sp", lambda g: g.dma_start(out=R[0:32], in_=relb), (), [bR])
        P.dma("sp", lambda g: g.dma_start(out=oh_s[0:32], in_=oh.rearrange("t b r -> b t r")), (), [bR])
        P.dma("sp", lambda g: g.dma_start(out=md_f, in_=mask_diag), (), [bR])
        P.dma("sp", lambda g: g.dma_start(out=mo_f, in_=mask_o0), (), [bR])
        lv = lamb.rearrange("p (a b c) -> p a b c", a=2, b=2, c=64)
        P.op("dve", lambda g: g.tensor_tensor(out=lprod, in0=lv[:, :, 0, :], in1=lv[:, :, 1, :], op=ALU.mult), [bR], [bT])
        P.op("dve", lambda g: g.reduce_sum(out=lsum, in_=lprod, axis=AX.X), [bT], [bT])
        P.op("act", lambda g: g.activation(out=lexp, in_=lsum, func=AF.Exp), [bT], [bT])
        P.op("dve", lambda g: g.tensor_tensor(out=lam_t, in0=lexp[:, 0:1], in1=lexp[:, 1:2], op=ALU.subtract), [bT], [bT])
        P.op("dve", lambda g: g.tensor_scalar(out=neg_lam, in0=lam_t, scalar1=LAM_INIT, scalar2=-1.0,
                                              op0=ALU.add, op1=ALU.mult), [bT], [b_const])
        P.op("dve", lambda g: g.tensor_scalar(out=gsub8, in0=gsub, scalar1=1.0 - LAM_INIT, scalar2=None,
                                              op0=ALU.mult), [bR], [b_const])
        P.op("dve", lambda g: g.tensor_scalar(out=R8[0:32], in0=R[0:32], scalar1=1.0 / DIFF_SCALE, scalar2=None,
                                              op0=ALU.mult), [bR], [bT])
        views = []
        for t in range(3):
            bA, bB = 2 * t, 2 * t + 1
            for q in range(128):
                bi = bA if q < 64 else bB
                o = bank(bi)[:, (q % 64) * 8:(q % 64) * 8 + 8]
                P.op("pe", lambda g, o=o, t=t, q=q: g.matmul(o, lhsT=oh_s[0:32, t, 127 - q:255 - q], rhs=R8[0:32],
                                                             start=True, stop=True),
                     [bT, bR], [bankB[bi]], signal=(q % 64 == 63))
            views.append((bA, bB))

        def tv(t, h, k0, k1, q0, q1):
            res = []
            for half in range(2):
                a, b = max(q0, half * 64), min(q1, half * 64 + 64)
                if a >= b:
                    continue
                bi = views[t][half]
                v = bank(bi).rearrange("p (q h) -> p q h", h=8)[k0:k1, a - half * 64:b - half * 64, h]
                res.append((v, bi, a, b))
            return res

        P.op("dve", lambda g: g.memset(Bmeta, 0.0), (), [b_const])
        P.op("dve", lambda g: g.memset(Bsn, 0.0), (), [b_const])
        for h in range(NH):
            for v, bi, a, b in tv(0, h, 0, 128, 0, 128):
                P.op("dve", lambda g, v=v, a=a, b=b, h=h: g.tensor_tensor(out=Bdiag[:, h, a:b], in0=v, in1=md_f[:, a:b], op=ALU.add),
                     [bankB[bi], bR], [b_const])
            for v, bi, a, b in tv(1, h, 0, 128, 0, 128):
                P.op("dve", lambda g, v=v, a=a, b=b, h=h: g.scalar_tensor_tensor(out=Bo0[:, h, a:b], in0=v, scalar=sel_s[:, 0:1],
                                                                                 in1=mo_f[:, a:b], op0=ALU.mult, op1=ALU.add),
                     [bankB[bi], bR], [b_const])
                P.op("act", lambda g, v=v, a=a, b=b, h=h: g.activation(out=Bo1[:, h, a:b], in_=v, func=AF.Copy, scale=sel_s[:, 1:2]),
                     [bankB[bi], bR], [b_const])
            for v, bi, a, b in tv(2, h, 0, 16, 0, 128):
                P.op("act", lambda g, v=v, a=a, b=b, h=h: g.activation(out=Bmeta[0:16, h, a:b], in_=v, func=AF.Copy, scale=sel_s[0:16, 2:3]),
                     [bankB[bi], bR], [b_const])
            for v, bi, a, b in tv(1, h, 0, 128, 0, 64):
                P.op("dve", lambda g, v=v, a=a, b=b, h=h: g.tensor_copy(Bs7[:, h, a:b], v), [bankB[bi]], [b_const])
            for v, bi, a, b in tv(0, h, 0, 64, 0, 64):
                P.op("dve", lambda g, v=v, a=a, b=b, h=h: g.tensor_copy(Bsn[0:64, h, a:b], v), [bankB[bi]], [b_const])
        P.op("dve", lambda g: g.tensor_copy(Mdiag, md_f), [bR], [b_const])
        P.op("dve", lambda g: g.tensor_copy(Mo0, mo_f), [bR], [b_const])

    def dma_mid(q, out3, in3, step, r, w):
        m = out3.shape[1]
        for a in range(0, m, step):
            b = min(m, a + step)
            P.dma(q, lambda g, a=a, b=b: g.dma_start(out=out3[:, a:b], in_=in3[:, a:b]), r, w)

    wbuf = {}

    deferred = []
    deferred_late = []

    def cast(name, dst, src, rows_per=128, defer=False, slab=None, late=False, cols=None):
        b = wbuf.get(name) or Buf(name)
        wbuf[name] = b
        n = src.shape[0]
        for r in range(0, n, rows_per):
            def go(r=r):
                if cols is not None:
                    for (a_, b_) in cols:
                        P.dma("pool", lambda g, a_=a_, b_=b_: g.dma_start(out=dst[r:r + rows_per, a_:b_], in_=src[r:r + rows_per, a_:b_]), (), [b])
                elif slab is None:
                    P.dma("pool", lambda g: g.dma_start(out=dst[r:r + rows_per], in_=src[r:r + rows_per]), (), [b])
                else:
                    w = slab
                    dv = dst.rearrange("a p c w -> p a c w")[:, :, r // 128, :]
                    sv = src[r:r + 128].rearrange("p (a w) -> p a w", w=w)
                    P.dma("pool", lambda g: g.dma_start(out=dv, in_=sv), (), [b], late=late)
            if late:
                deferred_late.append(go)
            elif defer:
                deferred.append(go)
            else:
                go()
        return b

    def rstd_from_ss(ss, rstd, n, nfeat, rb):
        P.op("dve", lambda g: g.tensor_scalar(out=rstd[0:n], in0=ss[0:n], scalar1=1.0 / nfeat, scalar2=EPS,
                                              op0=ALU.mult, op1=ALU.add), [rb], [rb])
        P.op("act", lambda g: g.activation(out=rstd[0:n], in_=rstd[0:n], func=AF.Sqrt), [rb], [rb])
        P.op("dve", lambda g: g.reciprocal(out=rstd[0:n], in_=rstd[0:n]), [rb], [rb])

    def transposes(src_fn, nblk, n, cols, dst, dstB, srcB, dt=BF16, idn=None):
        idn = ident if dt == BF16 else ident_f
        per = 8 if dt == BF16 else 4
        for j0 in range(0, nblk, per):
            bi = next_bank()
            bv = bank(bi, dt).rearrange("p (j t) -> p j t", t=128)
            m = min(per, nblk - j0)
            for j in range(j0, j0 + m):
                P.op("pe", lambda g, j=j, j0=j0, bv=bv: g.transpose(out=bv[0:cols, j - j0, 0:n], in_=src_fn(j), identity=idn[0:n, 0:n]),
                     [srcB, b_const], [bankB[bi]], signal=(j == j0 + m - 1))
            copy_op(copy_eng(), dst[0:cols, j0:j0 + m, 0:n], bv[0:cols, 0:m, 0:n], [bankB[bi]], [dstB])

    def norm_T(xs, bx, n, gbc, bg, hb, bhb, hT, bhT, ss, rstd, bst):
        import os
        mf = int(os.environ.get("MK_F", "9"))
        if mf == 0:
            return
        P.op("act", lambda g: g.activation(out=hb[0:n], in_=xs[0:n], func=AF.Square, accum_out=ss[0:n]), [bx], [bhb, bst])
        if mf == 1:
            return
        rstd_from_ss(ss, rstd, n, D, bst)
        if mf == 2:
            return
        P.op("dve", lambda g: g.scalar_tensor_tensor(out=hb[0:n], in0=xs[0:n], scalar=rstd[0:n], in1=gbc[0:n],
                                                     op0=ALU.mult, op1=ALU.mult), [bx, bst, bg], [bhb])

    def norm_T2(n, hb, bhb, hT, bhT):
        transposes(lambda j: hb[0:n, j * 128:(j + 1) * 128], 16, n, 128, hT, bhT, bhb)

    def mm_group(bi, n, ncols, lhs_fn, rhs_fn, nk, rB, dst=None):
        o = bank(bi)[0:n, 0:ncols] if dst is None else dst
        for c in range(nk):
            P.op("pe", lambda g, c=c: g.matmul(o, lhsT=lhs_fn(c), rhs=rhs_fn(c), start=(c == 0), stop=(c == nk - 1)),
                 rB, [bankB[bi]], signal=(c == nk - 1))

    def rope_tok(src_fn, dst_fn, cos, sin, n, tmp, rB, wB, tB):
        x1, x2 = src_fn(0), src_fn(1)
        t1, t2 = tmp
        P.op("dve", lambda g: g.tensor_tensor(out=t1, in0=x1, in1=cos, op=ALU.mult), rB, [tB])
        P.op("dve", lambda g: g.tensor_tensor(out=t2, in0=x2, in1=sin, op=ALU.mult), rB, [tB])
        P.op("dve", lambda g: g.tensor_tensor(out=dst_fn(0), in0=t1, in1=t2, op=ALU.subtract), [tB], [wB])
        P.op("dve", lambda g: g.tensor_tensor(out=t1, in0=x1, in1=sin, op=ALU.mult), rB + [wB], [tB])
        P.op("dve", lambda g: g.tensor_tensor(out=t2, in0=x2, in1=cos, op=ALU.mult), rB, [tB])
        P.op("dve", lambda g: g.tensor_tensor(out=dst_fn(1), in0=t1, in1=t2, op=ALU.add), [tB], [wB])

    def passA():
        AR.off = persist_mark
        WinK = AR.alloc([16, 2368], BF16); Wukv = AR.alloc([2, 2048], BF16)
        g_bc = AR.alloc([D], F32); gkv_bc = AR.alloc([KVL], F32)
        xs2 = [AR.alloc([D], F32) for _ in range(2)]
        hb = AR.alloc([D], BF16)
        hT2 = [AR.alloc([16, 128], BF16) for _ in range(2)]
        kd_f = AR.alloc([1024], F32); vd_f = AR.alloc([1024], F32)
        kd_b = AR.alloc([1024], BF16); vd_b = AR.alloc([1024], BF16)
        ckv_f = AR.alloc([KVL], F32); ckv_b = AR.alloc([KVL], BF16)
        kr_f = AR.alloc([64], F32); kr_b = AR.alloc([64], BF16)
        rope2 = [AR.alloc([64], F32) for _ in range(2)]
        rt = [AR.alloc([32], F32) for _ in range(2)]
        junk = AR.alloc([KVL], F32)
        ss = AR.alloc([1], F32); rstd = AR.alloc([1], F32); ssk = AR.alloc([1], F32); rstdk = AR.alloc([1], F32)
        KdT_st = AR.alloc([NH, 128], BF16); ckvT = AR.alloc([2, 128], BF16); KrT_st = AR.alloc([1, 128], BF16)
        KmT_st = AR.alloc([NH, 128], BF16); vm_b = AR.alloc([1024], BF16)
        bW = Buf(); bg = Buf()
        bx2 = [Buf(), Buf()]; bhb = Buf(); bhT2 = [Buf(), Buf()]
        bkdf = Buf(); bvdf = Buf(); bkdb = Buf(); bvdb = Buf(); bckf = Buf(); bckb = Buf(); bkrf = Buf(); bkrb = Buf()
        brope2 = [Buf(), Buf()]; brt = Buf(); bjunk = Buf(); bst = Buf(); bstk = Buf()
        bKdT = Buf(); bckvT = Buf(); bKrT = Buf(); bKmT = Buf(); bvmb = Buf()
        wv = w_in_b.rearrange("(c p) n -> p c n", p=128)
        dma_mid("sp", WinK[:, :, 0:2048], wv[:, :, OFF_K:OFF_CQ], 2, [wbuf["w_in"]], [bW])
        dma_mid("sp", WinK[:, :, 2048:2368], wv[:, :, OFF_CKV:DIN], 4, [wbuf["w_in"]], [bW])
        P.dma("sp", lambda g: g.dma_start(out=Wukv, in_=w_ukv_b.rearrange("(c p) n -> p c n", p=128)), [wbuf["w_ukv"]], [bW])
        P.dma("sp", lambda g: g.dma_start(out=g_bc, in_=g_attn.partition_broadcast(128)), (), [bg])
        P.dma("sp", lambda g: g.dma_start(out=gkv_bc, in_=g_kv.partition_broadcast(128)), (), [bg])

        tiles = []
        for j in range(PAST // 128):
            tiles.append(dict(cache=j, n=128, seq=SS_, koff=j * 128))
        tiles.append(dict(x=xmeta, n=NMETA, seq=SP_, koff=2 * NOWN * 128, rope=rope_m, outs=(kd_m, vd_m, ckv_m, kr_m)))
        tiles.append(dict(x=xsam, n=DSEQ, seq=SS_, koff=PAST, rope=rope_s, outs=(kd_s, vd_s, ckv_s, kr_s)))
        for i in range(NOWN):
            r = slice(i * 128, (i + 1) * 128)
            tiles.append(dict(x=xo[r], n=128, seq=SP_, koff=i * 128, rope=rope_o[r],
                              outs=(kd_o[r], vd_o[r], ckv_o[r], kr_o[r])))
            tiles.append(dict(x=xt[r], n=128, seq=SP_, koff=NOWN * 128 + i * 128, rope=rope_t[r], outs=None))

        import os
        flt = os.environ.get("MK_A", "")
        if flt:
            keep = []
            for t in tiles:
                kind = "cache" if "cache" in t else ("meta" if t["n"] == NMETA else ("sam" if t["n"] == DSEQ else "own"))
                if kind in flt.split(","):
                    keep.append(t)
            tiles[:] = keep[:int(os.environ.get("MK_AN", "100"))]

        def load(idx):
            t = tiles[idx]
            if "cache" in t:
                return
            p = t["slot"]
            n = t["n"]
            P.dma("sp", lambda g: g.dma_start(out=xs2[p][0:n], in_=t["x"]), (), [bx2[p]])
            P.dma("sp", lambda g: g.dma_start(out=rope2[p][0:n], in_=t["rope"]), (), [brope2[p]])

        def front(idx):
            t = tiles[idx]
            if "cache" in t:
                return
            p = t["slot"]
            norm_T(xs2[p], bx2[p], t["n"], g_bc, bg, hb, bhb, hT2[p], bhT2[p], ss, rstd, bst)

        def front_T(idx):
            if idx >= len(tiles):
                return
            t = tiles[idx]
            if "cache" in t:
                return
            p = t["slot"]
            norm_T2(t["n"], hb, bhb, hT2[p], bhT2[p])

        cut = int(os.environ.get("MK_CUT", "9"))

        def back(idx, mid=lambda: None):
            t = tiles[idx]
            n = t["n"]; seq = t["seq"]; koff = t["koff"]
            if cut == 0:
                return
            if "cache" in t:
                mid()
                j = t["cache"]
                r = slice(j * 128, (j + 1) * 128)
                P.dma("sp", lambda g: g.dma_start(out=kd_b[0:n], in_=ck_b[r]), [wbuf["ck"]], [bkdb])
                P.dma("sp", lambda g: g.dma_start(out=ckv_b[0:n], in_=cckv_b[r]), [wbuf["cckv"]], [bckb])
                P.dma("sp", lambda g: g.dma_start(out=kr_b[0:n], in_=ckr_b[r]), [wbuf["ckr"]], [bkrb])
            else:
                p = t["slot"]
                hT = hT2[p]
                rB = [bhT2[p], bW]
                for gi, (c0, dstf, dstb, bf_, bb_) in enumerate([(0, kd_f, kd_b, bkdf, bkdb), (512, kd_f, kd_b, bkdf, bkdb),
                                                                  (1024, vd_f, vd_b, bvdf, bvdb), (1536, vd_f, vd_b, bvdf, bvdb)]):
                    bi = next_bank()
                    mm_group(bi, n, 512, lambda c: hT[:, c, 0:n], lambda c, c0=c0: WinK[:, c, c0:c0 + 512], 16, rB)
                    lc = c0 % 1024
                    P.op("act", lambda g, bi=bi, dstf=dstf, lc=lc: g.activation(out=dstf[0:n, lc:lc + 512], in_=bank(bi)[0:n, 0:512], func=AF.Copy),
                         [bankB[bi]], [bf_])
                    P.op("dve", lambda g, bi=bi, dstb=dstb, lc=lc: g.tensor_copy(dstb[0:n, lc:lc + 512], bank(bi)[0:n, 0:512]),
                         [bankB[bi]], [bb_])
                if cut == 1:
                    return
                bi = next_bank()
                mm_group(bi, n, 320, lambda c: hT[:, c, 0:n], lambda c: WinK[:, c, 2048:2368], 16, rB)
                bk = bank(bi)
                mid()
                P.op("act", lambda g: g.activation(out=junk[0:n], in_=bk[0:n, 0:KVL], func=AF.Square, accum_out=ssk[0:n]),
                     [bankB[bi]], [bjunk, bstk])
                rstd_from_ss(ssk, rstdk, n, KVL, bstk)
                P.op("dve", lambda g: g.scalar_tensor_tensor(out=ckv_f[0:n], in0=bk[0:n, 0:KVL], scalar=rstdk[0:n], in1=gkv_bc[0:n],
                                                             op0=ALU.mult, op1=ALU.mult), [bankB[bi], bstk, bg], [bckf])
                P.op("act", lambda g: g.activation(out=ckv_b[0:n], in_=ckv_f[0:n], func=AF.Copy), [bckf], [bckb])
                rp = rope2[p]
                rope_tok(lambda hf: bk[0:n, KVL + 32 * hf:KVL + 32 * hf + 32], lambda hf: kr_f[0:n, 32 * hf:32 * hf + 32],
                         rp[0:n, 0:32], rp[0:n, 32:64], n, (rt[0][0:n], rt[1][0:n]), [bankB[bi], brope2[p]], bkrf, brt)
                P.op("act", lambda g: g.activation(out=kr_b[0:n], in_=kr_f[0:n], func=AF.Copy), [bkrf], [bkrb])
                if t["outs"] is not None:
                    o_kd, o_vd, o_ckv, o_kr = t["outs"]
                    P.dma("pool", lambda g: g.dma_start(out=o_kd, in_=kd_f[0:n]), [bkdf], (), is_output=True)
                    P.dma("pool", lambda g: g.dma_start(out=o_vd, in_=vd_f[0:n]), [bvdf], (), is_output=True)
                    P.dma("pool", lambda g: g.dma_start(out=o_ckv, in_=ckv_f[0:n]), [bckf], (), is_output=True)
                    P.dma("pool", lambda g: g.dma_start(out=o_kr, in_=kr_f[0:n]), [bkrf], (), is_output=True)
                P.dma("pool", lambda g: g.dma_start(out=seq["Vd"][koff:koff + n, :], in_=vd_b[0:n]), [bvdb], ())
            if cut == 2:
                return
            transposes(lambda j: kd_b[0:n, j * 128:(j + 1) * 128], NH, n, 128, KdT_st, bKdT, bkdb)
            P.dma("pool", lambda g: g.dma_start(out=seq["KdT"][:, :, koff:koff + n].rearrange("h p k -> p h k"), in_=KdT_st[:, :, 0:n]),
                  [bKdT], ())
            transposes(lambda j: ckv_b[0:n, j * 128:(j + 1) * 128], 2, n, 128, ckvT, bckvT, bckb)
            transposes(lambda j: kr_b[0:n, 0:64], 1, n, 64, KrT_st, bKrT, bkrb)
            P.dma("pool", lambda g: g.dma_start(out=seq["KrT"][:, koff:koff + n], in_=KrT_st[0:64, 0, 0:n]), [bKrT], ())
            wk = Wukv.rearrange("p c (h x) -> p c h x", x=256)
            for hh in range(2):
                bi = next_bank()
                for h4 in range(4):
                    h = hh * 4 + h4
                    mm_group(bi, 128, n, lambda c, h=h: wk[:, c, h, 0:128], lambda c: ckvT[:, c, 0:n], 2, [bckvT, bW],
                             dst=bank(bi)[:, h4 * 128:h4 * 128 + n])
                copy_op(copy_eng(), KmT_st[:, hh * 4:hh * 4 + 4, 0:n],
                        bank(bi).rearrange("p (j t) -> p j t", t=128)[:, :, 0:n], [bankB[bi]], [bKmT])
            P.dma("pool", lambda g: g.dma_start(out=seq["KmT"][:, :, koff:koff + n].rearrange("h p k -> p h k"), in_=KmT_st[:, :, 0:n]),
                  [bKmT], ())
            for hh in range(2):
                bi = next_bank()
                mm_group(bi, n, 512, lambda c: ckvT[:, c, 0:n], lambda c, hh=hh: wk[:, c, hh * 4:hh * 4 + 4, 128:256], 2, [bckvT, bW])
                copy_op(copy_eng(), vm_b[0:n, hh * 512:(hh + 1) * 512], bank(bi)[0:n, 0:512], [bankB[bi]], [bvmb])
            P.dma("pool", lambda g: g.dma_start(out=seq["Vm"][koff:koff + n, :], in_=vm_b[0:n]), [bvmb], ())

        slot = 0
        for t in tiles:
            if "cache" not in t:
                t["slot"] = slot
                slot ^= 1
        nt = len(tiles)
        load(0)
        front(0)
        front_T(0)
        for i in range(nt):
            if i + 1 < nt:
                load(i + 1)
                front(i + 1)
            back(i, mid=lambda i=i: front_T(i + 1))
            for _ in range(3):
                if deferred:
                    deferred.pop(0)()
        while deferred:
            deferred.pop(0)()

    def passB():
        AR.off = persist_mark
        WinQ = AR.alloc([16, 1536], BF16); Wuq = AR.alloc([4, 1536], BF16)
        g_bc = AR.alloc([D], F32); gq_bc = AR.alloc([QL], F32)
        xs2 = [AR.alloc([D], F32) for _ in range(2)]
        hb = AR.alloc([D], BF16)
        hT2 = [AR.alloc([16, 128], BF16) for _ in range(2)]
        qd_b = AR.alloc([1024], BF16); cq_b = AR.alloc([QL], BF16); cqT = AR.alloc([4, 128], BF16)
        qn_b = AR.alloc([NH, 128], BF16); qr_f = AR.alloc([NH, 64], F32); qr_b = AR.alloc([NH, 64], BF16)
        rope2 = [AR.alloc([64], F32) for _ in range(2)]
        rt = [AR.alloc([2, 32], F32) for _ in range(2)]
        junk = AR.alloc([QL], F32)
        ss = AR.alloc([1], F32); rstd = AR.alloc([1], F32); ssq = AR.alloc([1], F32); rstdq = AR.alloc([1], F32)
        QdT_st = AR.alloc([NH, 128], BF16); QnT_st = AR.alloc([NH, 128], BF16); QrT_st = AR.alloc([NH, 128], BF16)
        bW = Buf(); bg = Buf(); bx2 = [Buf(), Buf()]; bhb = Buf(); bhT2 = [Buf(), Buf()]
        bqdb = Buf(); bcqb = Buf(); bcqT = Buf(); bqnb = Buf(); bqrf = Buf(); bqrb = Buf()
        brope2 = [Buf(), Buf()]; brt = Buf(); bjunk = Buf(); bst = Buf(); bstq = Buf()
        bQdT = Buf(); bQnT = Buf(); bQrT = Buf()
        wv = w_in_b.rearrange("(c p) n -> p c n", p=128)
        dma_mid("sp", WinQ[:, :, 0:1024], wv[:, :, 0:OFF_K], 4, [wbuf["w_in"]], [bW])
        dma_mid("sp", WinQ[:, :, 1024:1536], wv[:, :, OFF_CQ:OFF_CKV], 4, [wbuf["w_in"]], [bW])
        P.dma("sp", lambda g: g.dma_start(out=Wuq, in_=w_uq_b.rearrange("(c p) n -> p c n", p=128)), [wbuf["w_uq"]], [bW])
        P.dma("sp", lambda g: g.dma_start(out=g_bc, in_=g_attn.partition_broadcast(128)), (), [bg])
        P.dma("sp", lambda g: g.dma_start(out=gq_bc, in_=g_q.partition_broadcast(128)), (), [bg])
        tiles = [dict(x=xsam, n=DSEQ, seq=SS_, qoff=0, rope=rope_s)]
        for i in range(NOWN):
            r = slice(i * 128, (i + 1) * 128)
            tiles.append(dict(x=xo[r], n=128, seq=SP_, qoff=i * 128, rope=rope_o[r]))

        import os
        if os.environ.get("MK_B", "") == "s":
            tiles[:] = tiles[:1]

        def load(idx):
            t = tiles[idx]; p = idx % 2; n = t["n"]
            P.dma("sp", lambda g: g.dma_start(out=xs2[p][0:n], in_=t["x"]), (), [bx2[p]])
            P.dma("sp", lambda g: g.dma_start(out=rope2[p][0:n], in_=t["rope"]), (), [brope2[p]])

        def front(idx):
            t = tiles[idx]; p = idx % 2
            norm_T(xs2[p], bx2[p], t["n"], g_bc, bg, hb, bhb, hT2[p], bhT2[p], ss, rstd, bst)

        def front_T(idx):
            if idx >= len(tiles):
                return
            t = tiles[idx]; p = idx % 2
            norm_T2(t["n"], hb, bhb, hT2[p], bhT2[p])

        def back(idx, mid=lambda: None):
            t = tiles[idx]; p = idx % 2; n = t["n"]; seq = t["seq"]; qoff = t["qoff"]
            hT = hT2[p]; rB = [bhT2[p], bW]
            for c0 in (0, 512):
                bi = next_bank()
                mm_group(bi, n, 512, lambda c: hT[:, c, 0:n], lambda c, c0=c0: WinQ[:, c, c0:c0 + 512], 16, rB)
                copy_op(copy_eng(), qd_b[0:n, c0:c0 + 512], bank(bi)[0:n, 0:512], [bankB[bi]], [bqdb])
            transposes(lambda j: qd_b[0:n, j * 128:(j + 1) * 128], NH, n, 128, QdT_st, bQdT, bqdb)
            P.dma("pool", lambda g: g.dma_start(out=seq["QdT"][:, :, qoff:qoff + n].rearrange("h p k -> p h k"), in_=QdT_st[:, :, 0:n]),
                  [bQdT], ())
            bi = next_bank()
            mm_group(bi, n, 512, lambda c: hT[:, c, 0:n], lambda c: WinQ[:, c, 1024:1536], 16, rB)
            bk = bank(bi)
            mid()
            P.op("act", lambda g: g.activation(out=junk[0:n], in_=bk[0:n, 0:QL], func=AF.Square, accum_out=ssq[0:n]),
                 [bankB[bi]], [bjunk, bstq])
            rstd_from_ss(ssq, rstdq, n, QL, bstq)
            P.op("dve", lambda g: g.scalar_tensor_tensor(out=cq_b[0:n], in0=bk[0:n, 0:QL], scalar=rstdq[0:n], in1=gq_bc[0:n],
                                                         op0=ALU.mult, op1=ALU.mult), [bankB[bi], bstq, bg], [bcqb])
            transposes(lambda j: cq_b[0:n, j * 128:(j + 1) * 128], 4, n, 128, cqT, bcqT, bcqb)
            rp = rope2[p]
            for gq in range(4):
                bi = next_bank()
                mm_group(bi, n, 384, lambda c: cqT[:, c, 0:n], lambda c, gq=gq: Wuq[:, c, gq * 384:(gq + 1) * 384], 4, [bcqT, bW])
                bv = bank(bi)[:, 0:384].rearrange("p (h x) -> p h x", x=192)
                P.op("act", lambda g, bv=bv, gq=gq: g.activation(out=qn_b[0:n, 2 * gq:2 * gq + 2, :], in_=bv[0:n, :, 0:128], func=AF.Copy),
                     [bankB[bi]], [bqnb])
                cosb = rp[0:n, 0:32].unsqueeze(1).to_broadcast([n, 2, 32])
                sinb = rp[0:n, 32:64].unsqueeze(1).to_broadcast([n, 2, 32])
                rope_tok(lambda hf, bv=bv: bv[0:n, :, 128 + 32 * hf:128 + 32 * hf + 32],
                         lambda hf, gq=gq: qr_f[0:n, 2 * gq:2 * gq + 2, 32 * hf:32 * hf + 32],
                         cosb, sinb, n, (rt[0][0:n], rt[1][0:n]), [bankB[bi], brope2[p]], bqrf, brt)
            P.op("act", lambda g: g.activation(out=qr_b[0:n], in_=qr_f[0:n], func=AF.Copy), [bqrf], [bqrb])
            transposes(lambda j: qn_b[0:n, j, :], NH, n, 128, QnT_st, bQnT, bqnb)
            P.dma("pool", lambda g: g.dma_start(out=seq["QnT"][:, :, qoff:qoff + n].rearrange("h p k -> p h k"), in_=QnT_st[:, :, 0:n]),
                  [bQnT], ())
            transposes(lambda j: qr_b[0:n, j, :], NH, n, 64, QrT_st, bQrT, bqrb)
            P.dma("pool", lambda g: g.dma_start(out=seq["QrT"][:, :, qoff:qoff + n].rearrange("h p k -> p h k"), in_=QrT_st[0:64, :, 0:n]),
                  [bQrT], ())

        nt = len(tiles)
        load(0)
        front(0)
        front_T(0)
        for i in range(nt):
            if i + 1 < nt:
                load(i + 1)
                front(i + 1)
            back(i, mid=lambda i=i: front_T(i + 1))

    def attention():
        AR.off = persist_mark
        NT = NKEY // 128 + 1
        NKP = NKEY - NMETA + 128
        KT2 = [AR.alloc([NKP], BF16) for _ in range(2)]
        V2 = [AR.alloc([NT, 128], BF16) for _ in range(2)]
        Q2 = [[AR.alloc([NOWN * 128], BF16) for _ in range(2)] for _ in range(2)]
        Qn2 = [AR.alloc([NOWN * 128], BF16) for _ in range(2)]
        QR2 = [AR.alloc([NOWN * 128], BF16) for _ in range(2)]
        KrT = AR.alloc([NKP], BF16)
        PT = [AR.alloc([512], BF16) for _ in range(4)]
        onesM = AR.alloc([128], BF16); onesS = AR.alloc([128], BF16)
        bz = Buf()
        for par_ in range(2):
            P.op("pool", lambda g, par_=par_: g.memset(KT2[par_], 0.0), (), [bz])
            P.op("pool", lambda g, par_=par_: g.memset(V2[par_], 0.0), (), [bz])
            P.op("pool", lambda g, par_=par_: g.memset(Q2[par_][0], 0.0), (), [bz])
            P.op("pool", lambda g, par_=par_: g.memset(Q2[par_][1], 0.0), (), [bz])
            P.op("pool", lambda g, par_=par_: g.memset(QR2[par_], 0.0), (), [bz])
        P.op("pool", lambda g: g.memset(KrT, 0.0), (), [bz])
        for i_ in range(4):
            P.op("pool", lambda g, i_=i_: g.memset(PT[i_], 0.0), (), [bz])
        P.op("pool", lambda g: g.memset(onesM, 0.0), (), [bz])
        P.op("pool", lambda g: g.memset(onesS, 0.0), (), [bz])
        P.op("pool", lambda g: g.memset(onesM[0:NMETA], 1.0), [bz], [bz])
        P.op("pool", lambda g: g.memset(onesS[0:DSEQ], 1.0), [bz], [bz])
        P.barrier()
        rl = AR.alloc([512], F32); on0 = AR.alloc([512], F32); on1 = AR.alloc([512], F32)
        od = AR.alloc([512], F32); sq = AR.alloc([512], F32); rs = AR.alloc([512], F32)
        mix_st = [AR.alloc([512], BF16) for _ in range(2)]
        bKT2 = [Buf(), Buf()]; bV2 = [Buf(), Buf()]; bQ2 = [Buf(), Buf()]; bQR2 = [Buf(), Buf()]; bKrT = Buf()
        bPT = [Buf() for _ in range(4)]
        brl = Buf(); bon0 = Buf(); bon1 = Buf(); bod = Buf(); bsq = Buf(); brs = Buf(); bmix = [Buf(), Buf()]
        SETS = [dict(S=(0, 1), O=2, L=3, PT=(0, 1)), dict(S=(4, 5), O=6, L=7, PT=(2, 3))]

        def tile_list(kind, G, mla):
            tl = []
            if kind == "p":
                nq = 512
                tl.append(dict(koff=2 * NOWN * 128, nk=NMETA, c0=0,
                               bias=([(Bmeta, 0, 128, NMETA)] if (G == 0 and not mla) else [])))
                for j in range(4 * G + 4):
                    c0 = max(j - 4 * G, 0) * 128
                    b = []
                    if j >= 4 * G:
                        b.append((Mdiag if mla else Bdiag, c0, 128, 128))
                    tl.append(dict(koff=j * 128, nk=128, c0=c0, bias=b))
                    b = []
                    if j >= 4 * G:
                        b.append((Mo0 if mla else Bo0, c0, 128, 128))
                    if (not mla) and 4 * G <= j + 1 <= 4 * G + 3:
                        b.append((Bo1, (j + 1 - 4 * G) * 128, 128, 128))
                    tl.append(dict(koff=NOWN * 128 + j * 128, nk=128, c0=c0, bias=b))
                tl.sort(key=lambda d: d["c0"])
            else:
                nq = DSEQ
                for j in range(PAST // 128):
                    tl.append(dict(koff=j * 128, nk=128, c0=0, bias=([(Bs7, 0, 64, 128)] if (j == 7 and not mla) else [])))
                tl.append(dict(koff=PAST, nk=DSEQ, c0=0, bias=([(Bsn, 0, 64, 64)] if not mla else [])))
            return tl, nq

        steps = []
        maps = []
        mi = 0
        import os
        att_f = os.environ.get("MK_ATT", "sp")
        for kind, seq in (("s", SS_), ("p", SP_)):
            if kind not in att_f:
                continue
            nkey = seq["nkey"]; nqt = seq["nq"]
            ngroups = 1 if kind == "s" else 4
            for hi in range(2 * NH):
                mla = hi >= NH
                h = hi % NH
                par = hi % 2
                maps_head = []
                for G in range(ngroups):
                    for comp in ((0,) if mla else (0, 1)):
                        tl, nq = tile_list(kind, G, mla)
                        m = dict(kind=kind, seq=seq, mla=mla, h=h, par=par, G=G, comp=comp, tiles=tl, nq=nq,
                                 qbase=G * 512 if kind == "p" else 0, set=SETS[mi % 2], first_of_head=(G == 0 and comp == 0),
                                 nkey=nkey, nqt=nqt)
                        mi += 1
                        maps.append(m)
                        for ti in range(len(tl)):
                            steps.append((m, ti))

        def load_head(m):
            seq = m["seq"]; h = m["h"]; par = m["par"]; nkey = m["nkey"]; nqt = m["nqt"]; kind = m["kind"]
            nfull = (nkey // 128)
            rem = nkey - nfull * 128
            if m["mla"]:
                P.dma("sp", lambda g: g.dma_start(out=KT2[par][:, 0:nkey], in_=seq["KmT"][h]), (), [bKT2[par]])
                vsrc = seq["Vm"]
                P.dma("sp", lambda g: g.dma_start(out=Qn2[par][:, 0:nqt], in_=seq["QnT"][h]), (), [bQ2[par]])
                P.dma("sp", lambda g: g.dma_start(out=QR2[par][0:64, 0:nqt], in_=seq["QrT"][h]), (), [bQR2[par]])
                if h == 0:
                    P.dma("sp", lambda g: g.dma_start(out=KrT[0:64, 0:nkey], in_=seq["KrT"]), (), [bKrT])
            else:
                P.dma("sp", lambda g: g.dma_start(out=KT2[par][:, 0:nkey], in_=seq["KdT"][h]), (), [bKT2[par]])
                vsrc = seq["Vd"]
                P.dma("sp", lambda g: g.dma_start(out=Q2[par][0][0:64, 0:nqt], in_=seq["QdT"][h][0:64]), (), [bQ2[par]])
                P.dma("sp", lambda g: g.dma_start(out=Q2[par][1][64:128, 0:nqt], in_=seq["QdT"][h][64:128]), (), [bQ2[par]])
            dma_mid("sp", V2[par][:, 0:nfull, :], vsrc[0:nfull * 128, h * 128:(h + 1) * 128].rearrange("(t p) e -> p t e", p=128),
                    4, (), [bV2[par]])
            P.dma("sp", lambda g: g.dma_start(out=V2[par][0:rem, nfull, :], in_=vsrc[nfull * 128:nkey, h * 128:(h + 1) * 128]),
                  (), [bV2[par]])

        def stage1(m, ti):
            kt = m["tiles"][ti]; st = m["set"]; par = m["par"]
            nk = kt["nk"]; c0 = kt["c0"]; nq = m["nq"]; koff = kt["koff"]; qb = m["qbase"]
            Sb = st["S"][ti % 2]
            o = bank(Sb)[:, c0:nq]
            nb = len(kt["bias"])
            if m["mla"]:
                P.op("pe", lambda g: g.matmul(o, lhsT=KT2[par][:, koff:koff + 128], rhs=Qn2[par][:, qb + c0:qb + nq], start=True, stop=False),
                     [bKT2[par], bQ2[par]], [bankB[Sb]], signal=False)
                P.op("pe", lambda g: g.matmul(o, lhsT=KrT[:, koff:koff + 128], rhs=QR2[par][:, qb + c0:qb + nq], start=False, stop=(nb == 0)),
                     [bKrT, bQR2[par]], [bankB[Sb]], signal=(nb == 0))
            else:
                cc = m["comp"]
                P.op("pe", lambda g: g.matmul(o, lhsT=KT2[par][:, koff:koff + 128], rhs=Q2[par][cc][:, qb + c0:qb + nq], start=True, stop=(nb == 0)),
                     [bKT2[par], bQ2[par]], [bankB[Sb]], signal=(nb == 0))
            for bi_, (tab, coff, ncb, nkb) in enumerate(kt["bias"]):
                rhs = tab[:, 0:ncb] if m["mla"] else tab[:, m["h"], 0:ncb]
                P.op("pe", lambda g, rhs=rhs, coff=coff, ncb=ncb, last=(bi_ == nb - 1): g.matmul(
                    bank(Sb)[:, coff:coff + ncb], lhsT=ident, rhs=rhs, start=False, stop=last),
                    [b_const], [bankB[Sb]], signal=(bi_ == nb - 1))
            pt = st["PT"][ti % 2]
            scale = MLA_SCALE if m["mla"] else DIFF_SCALE
            P.op("act", lambda g: g.activation(out=PT[pt][:, c0:nq], in_=o, func=AF.Exp, scale=scale), [bankB[Sb]], [bPT[pt]])

        def stage2(m, ti):
            kt = m["tiles"][ti]; st = m["set"]; par = m["par"]
            nk = kt["nk"]; c0 = kt["c0"]; nq = m["nq"]; koff = kt["koff"]
            pt = st["PT"][ti % 2]
            nt_ = len(m["tiles"])
            first = (ti == 0); last = (ti == nt_ - 1)
            slot = koff // 128
            ones_t = ones_b if nk == 128 else (onesM if nk == NMETA else onesS)
            P.op("pe", lambda g: g.matmul(bank(st["O"])[:, c0:nq], lhsT=V2[par][:, slot, :], rhs=PT[pt][:, c0:nq], start=first, stop=last),
                 [bV2[par], bPT[pt]], [bankB[st["O"]]], signal=False)
            P.op("pe", lambda g: g.matmul(bank(st["L"])[:, c0:nq], lhsT=ones_t, rhs=PT[pt][:, c0:nq], start=first, stop=last),
                 [b_const, bPT[pt]], [bankB[st["L"]]], signal=True)
            if last:
                finish(m)

        def finish(m):
            st = m["set"]; nq = m["nq"]; seq = m["seq"]; qb = m["qbase"]; h = m["h"]
            O = bank(st["O"])[:, 0:nq]; L = bank(st["L"])[:, 0:nq]
            bO = bankB[st["O"]]; bL = bankB[st["L"]]
            P.op("dve", lambda g: g.reciprocal(out=rl[:, 0:nq], in_=L), [bL], [brl])
            if m["mla"]:
                ms = m["G"] % 2
                P.op("dve", lambda g: g.tensor_tensor(out=mix_st[ms][:, 0:nq], in0=O, in1=rl[:, 0:nq], op=ALU.mult), [bO, brl], [bmix[ms]])
                P.dma("pool", lambda g: g.dma_start(out=seq["MixT"][NH + h][:, qb:qb + nq], in_=mix_st[ms][:, 0:nq]), [bmix[ms]], ())
                return
            if m["comp"] == 0:
                P.op("dve", lambda g: g.tensor_tensor(out=on0[:, 0:nq], in0=O, in1=rl[:, 0:nq], op=ALU.mult), [bO, brl], [bon0])
                return
            P.op("dve", lambda g: g.tensor_tensor(out=on1[:, 0:nq], in0=O, in1=rl[:, 0:nq], op=ALU.mult), [bO, brl], [bon1])
            P.op("dve", lambda g: g.scalar_tensor_tensor(out=od[:, 0:nq], in0=on1[:, 0:nq], scalar=neg_lam, in1=on0[:, 0:nq],
                                                         op0=ALU.mult, op1=ALU.add), [bon0, bon1, b_const], [bod])
            P.op("act", lambda g: g.activation(out=sq[:, 0:nq], in_=od[:, 0:nq], func=AF.Square), [bod], [bsq])
            Sb = st["S"][0]
            P.op("pe", lambda g: g.matmul(bank(Sb)[:, 0:nq], lhsT=ones_f, rhs=sq[:, 0:nq], start=True, stop=True), [bsq, b_const], [bankB[Sb]])
            P.op("dve", lambda g: g.tensor_scalar(out=rs[:, 0:nq], in0=bank(Sb)[:, 0:nq], scalar1=1.0 / 128, scalar2=EPS,
                                                  op0=ALU.mult, op1=ALU.add), [bankB[Sb]], [brs])
            P.op("act", lambda g: g.activation(out=rs[:, 0:nq], in_=rs[:, 0:nq], func=AF.Sqrt), [brs], [brs])
            P.op("dve", lambda g: g.reciprocal(out=rs[:, 0:nq], in_=rs[:, 0:nq]), [brs], [brs])
            ms = m["G"] % 2
            P.op("dve", lambda g: g.scalar_tensor_tensor(out=mix_st[ms][:, 0:nq], in0=od[:, 0:nq], scalar=gsub8, in1=rs[:, 0:nq],
                                                         op0=ALU.mult, op1=ALU.mult), [bod, brs, b_const], [bmix[ms]])
            P.dma("pool", lambda g: g.dma_start(out=seq["MixT"][h][:, qb:qb + nq], in_=mix_st[ms][:, 0:nq]), [bmix[ms]], ())

        head_first = [i for i, mm in enumerate(maps) if mm["first_of_head"]]
        load_head(maps[head_first[0]])
        nxt = {head_first[k]: head_first[k + 1] for k in range(len(head_first) - 1)}
        seen = set()
        prev = None
        every = max(1, len(steps) // (len(deferred_late) + 1))
        for si_, (m, ti) in enumerate(steps):
            if deferred_late and si_ % every == every - 1:
                deferred_late.pop(0)()
            stage1(m, ti)
            if prev is not None:
                stage2(*prev)
            prev = (m, ti)
            idm = id(m)
            if idm not in seen:
                seen.add(idm)
                k = maps.index(m)
                if k in nxt:
                    load_head(maps[nxt[k]])
        stage2(*prev)

    def phase3():
        AR.off = persist_mark
        TMAX = 576
        g_bc = AR.alloc([D], F32)
        mixh = AR.alloc([16, TMAX], BF16)
        x1 = AR.alloc([5, D], F32)
        gated = AR.alloc([NFC, TMAX], BF16)
        _o = AR.off
        wo2 = [AR.alloc([16, 512], BF16) for _ in range(2)]
        AR.off = _o
        wgu = [[AR.alloc([16, 256], BF16) for _ in range(2)] for _ in range(2)]
        wd2 = [AR.alloc([NFC, 128], BF16) for _ in range(2)]
        hb2 = [AR.alloc([D], BF16) for _ in range(2)]
        bhb2 = [Buf(), Buf()]
        sg = [AR.alloc([TMAX], BF16) for _ in range(2)]
        dT = [AR.alloc([TMAX], F32) for _ in range(2)]
        ss = AR.alloc([1], F32); rstd = AR.alloc([1], F32)
        bg = Buf(); bmixh = Buf(); bx1 = [Buf() for _ in range(5)]; bgated = Buf()
        bwgu = [[Buf(), Buf()], [Buf(), Buf()]]; bwd2 = [Buf(), Buf()]; bsg = [Buf(), Buf()]; bdT = [Buf(), Buf()]
        bst = Buf()
        own = lambda i: dict(seq=SP_, q0=i * 128, n=128, x=xo[i * 128:(i + 1) * 128], y=y_o[i * 128:(i + 1) * 128])
        groups = [[dict(seq=SS_, q0=0, n=DSEQ, x=xsam, y=y_s)] + [own(i) for i in range(4)]]
        for G in range(1, 4):
            groups.append([own(i) for i in range(4 * G, 4 * G + 4)])
        cnt = {"wo": 0, "gu": 0, "wd": 0, "sg": 0, "dT": 0}

        def do_group(tiles):
            off = 0
            for t in tiles:
                t["off"] = off
                off += t["n"]
            T = off
            chunks = []
            cur = []
            for t in tiles:
                if cur and (t["off"] + t["n"] - cur[0]["off"] > 512 or t["seq"] is not cur[0]["seq"]):
                    chunks.append(cur); cur = []
                cur.append(t)
            chunks.append(cur)
            chunks = [(c[0]["off"], c[-1]["off"] + c[-1]["n"], c) for c in chunks]
            P.dma("sp", lambda g: g.dma_start(out=g_bc, in_=g_ffn.partition_broadcast(128)), (), [bg])
            for (c0, c1, ct) in chunks:
                seq = ct[0]["seq"]; q0 = ct[0]["q0"]
                dma_mid("sp", mixh[:, :, c0:c1], seq["MixT"][:, :, q0:q0 + (c1 - c0)].rearrange("c p t -> p c t"), 4, (), [bmixh])
            for k, t in enumerate(tiles):
                P.dma("sp", lambda g, k=k, t=t: g.dma_start(out=x1[0:t["n"], k, :], in_=t["x"]), (), [bx1[k]])
            for cg in range(4):
                p = cnt["wo"] % 2; cnt["wo"] += 1
                P.dma("sp", lambda g, p=p, cg=cg: g.dma_start(out=wo2[p], in_=w_out_b[cg]), [wbuf["w_out"]], bwgu[p])
                for k, t in enumerate(tiles):
                    n = t["n"]; o_ = t["off"]
                    bi = next_bank()
                    mm_group(bi, n, 512, lambda c, o_=o_, n=n: mixh[:, c, o_:o_ + n], lambda c, p=p: wo2[p][:, c, :], 16, [bmixh] + bwgu[p])
                    P.op("dve", lambda g, bi=bi, k=k, cg=cg, n=n: g.tensor_tensor(out=x1[0:n, k, cg * 512:(cg + 1) * 512], in0=bank(bi)[0:n, 0:512],
                                                                                   in1=x1[0:n, k, cg * 512:(cg + 1) * 512], op=ALU.add),
                         [bankB[bi], bx1[k]], [bx1[k]])
            for k, t in enumerate(tiles):
                n = t["n"]; o_ = t["off"]
                hb = hb2[k % 2]; bhb = bhb2[k % 2]
                P.op("act", lambda g, k=k, n=n, hb=hb: g.activation(out=hb[0:n], in_=x1[0:n, k, :], func=AF.Square, accum_out=ss[0:n]), [bx1[k]], [bhb, bst])
                rstd_from_ss(ss, rstd, n, D, bst)
                P.op("dve", lambda g, k=k, n=n, hb=hb: g.scalar_tensor_tensor(out=hb[0:n], in0=x1[0:n, k, :], scalar=rstd[0:n], in1=g_bc[0:n],
                                                                               op0=ALU.mult, op1=ALU.mult), [bx1[k], bst, bg], [bhb])
                for j0 in (0, 8):
                    bi = next_bank()
                    bv = bank(bi, BF16).rearrange("p (j t) -> p j t", t=128)
                    for j in range(j0, j0 + 8):
                        P.op("pe", lambda g, j=j, j0=j0, bv=bv, n=n, hb=hb: g.transpose(out=bv[:, j - j0, 0:n], in_=hb[0:n, j * 128:(j + 1) * 128], identity=ident[0:n, 0:n]),
                             [bhb, b_const], [bankB[bi]], signal=(j == j0 + 7))
                    copy_op("dve", mixh[:, j0:j0 + 8, o_:o_ + n], bv[:, :, 0:n], [bankB[bi]], [bmixh])
            P.dma("sp", lambda g: g.dma_start(out=g_bc, in_=g_fin.partition_broadcast(128)), (), [bg])
            for f2 in range(NFC // 2):
                p = cnt["gu"] % 2; cnt["gu"] += 1
                P.dma("sp", lambda g, p=p, f2=f2: g.dma_start(out=wgu[0][p], in_=wg_b[f2]), [wbuf["wg"]], [bwgu[0][p]])
                P.dma("sp", lambda g, p=p, f2=f2: g.dma_start(out=wgu[1][p], in_=wu_b[f2]), [wbuf["wu"]], [bwgu[1][p]])
                for sub in range(2):
                    f = f2 * 2 + sub
                    s_ = cnt["sg"] % 2; cnt["sg"] += 1
                    for (c0, c1, ct) in chunks:
                        bg_ = next_bank(); bu_ = next_bank()
                        mm_group(bg_, 128, c1 - c0, lambda c, p=p, sub=sub: wgu[0][p][:, c, sub * 128:(sub + 1) * 128],
                                 lambda c, c0=c0, c1=c1: mixh[:, c, c0:c1], 16, [bmixh, bwgu[0][p]])
                        mm_group(bu_, 128, c1 - c0, lambda c, p=p, sub=sub: wgu[1][p][:, c, sub * 128:(sub + 1) * 128],
                                 lambda c, c0=c0, c1=c1: mixh[:, c, c0:c1], 16, [bmixh, bwgu[1][p]])
                        w = c1 - c0
                        P.op("act", lambda g, bg_=bg_, s_=s_, c0=c0, c1=c1, w=w: g.activation(out=sg[s_][:, c0:c1], in_=bank(bg_)[:, 0:w], func=AF.Silu),
                             [bankB[bg_]], [bsg[s_]])
                        P.op("dve", lambda g, bu_=bu_, s_=s_, f=f, c0=c0, c1=c1, w=w: g.tensor_tensor(out=gated[:, f, c0:c1], in0=bank(bu_)[:, 0:w],
                                                                                                     in1=sg[s_][:, c0:c1], op=ALU.mult),
                             [bankB[bu_], bsg[s_]], [bgated])
            for j in range(16):
                p = cnt["wd"] % 2; cnt["wd"] += 1
                P.dma("sp", lambda g, p=p, j=j: g.dma_start(out=wd2[p], in_=wd_b[j]), [wbuf["wd"]], [bwd2[p]])
                d_ = cnt["dT"] % 2; cnt["dT"] += 1
                for (c0, c1, ct) in chunks:
                    w = c1 - c0
                    bi = next_bank()
                    mm_group(bi, 128, w, lambda c, p=p: wd2[p][:, c, :], lambda c, c0=c0, c1=c1: gated[:, c, c0:c1], NFC, [bgated, bwd2[p]])
                    P.op("act", lambda g, bi=bi, d_=d_, c0=c0, c1=c1, w=w: g.activation(out=dT[d_][:, c0:c1], in_=bank(bi)[:, 0:w], func=AF.Copy),
                         [bankB[bi]], [bdT[d_]])
                    bt = next_bank()
                    for kk, t in enumerate(ct):
                        n = t["n"]; o_ = t["off"]
                        P.op("pe", lambda g, kk=kk, bt=bt, d_=d_, n=n, o_=o_: g.transpose(out=bank(bt)[0:n, kk * 128:(kk + 1) * 128],
                                                                                          in_=dT[d_][:, o_:o_ + n], identity=ident_f),
                             [bdT[d_], b_const], [bankB[bt]], signal=(kk == len(ct) - 1))
                    for kk, t in enumerate(ct):
                        n = t["n"]; k = tiles.index(t)
                        P.op("dve", lambda g, kk=kk, bt=bt, j=j, n=n, k=k: g.tensor_tensor(out=x1[0:n, k, j * 128:(j + 1) * 128],
                                                                                           in0=bank(bt)[0:n, kk * 128:(kk + 1) * 128],
                                                                                           in1=x1[0:n, k, j * 128:(j + 1) * 128], op=ALU.add),
                             [bankB[bt], bx1[k]], [bx1[k]])
            for k, t in enumerate(tiles):
                n = t["n"]
                P.op("act", lambda g, k=k, n=n: g.activation(out=hb2[0][0:n], in_=x1[0:n, k, :], func=AF.Square, accum_out=ss[0:n]), [bx1[k]], [bhb2[0], bst])
                rstd_from_ss(ss, rstd, n, D, bst)
                P.op("dve", lambda g, k=k, n=n: g.scalar_tensor_tensor(out=x1[0:n, k, :], in0=x1[0:n, k, :], scalar=rstd[0:n], in1=g_bc[0:n],
                                                                        op0=ALU.mult, op1=ALU.mult), [bx1[k], bst, bg], [bx1[k]])
                P.dma("pool", lambda g, k=k, t=t, n=n: g.dma_start(out=t["y"], in_=x1[0:n, k, :]), [bx1[k]], (), is_output=True)

        import os
        if os.environ.get("MK_P3", "") == "s":
            groups[:] = [groups[0][:1]]
        for tiles in groups:
            do_group(tiles)

    cast("w_ukv", w_ukv_b, w_ukv)
    cast("ck", ck_b, ck)
    cast("cckv", cckv_b, cckv)
    cast("ckr", ckr_b, ckr)
    cast("w_in", w_in_b, w_in, cols=[(OFF_K, OFF_CQ), (OFF_CKV, DIN)])
    cast("w_in", w_in_b, w_in, cols=[(0, OFF_K), (OFF_CQ, OFF_CKV)], defer=True)
    cast("w_uq", w_uq_b, w_uq, defer=True)
    cast("cv", SS_["Vd"][0:PAST], cv, defer=True)
    cast("w_out", w_out_b, w_out, defer=True, slab=512)
    cast("wg", wg_b, wg, defer=True, slab=256, late=True)
    cast("wu", wu_b, wu, defer=True, slab=256, late=True)
    cast("wd", wd_b, wd, defer=True, slab=128, late=True)
    import os
    stop = int(os.environ.get("MK_STOP", "9"))
    setup()
    P.barrier()
    if stop >= 1:
        passA()
        P.barrier()
    if stop >= 2:
        passB()
        P.barrier()
    if stop >= 3:
        attention()
    while deferred_late:
        deferred_late.pop(0)()
    if stop >= 3:
        P.barrier(final=True)
    if stop >= 4:
        phase3()
    while deferred:
        deferred.pop(0)()
    P.finalize()
    P.emit()
    return nc, P


def _t5_onehot(delta):
    rel = (np.arange(255, dtype=np.int32) - 127 + delta)
    try:
        import jax
        import jax.numpy as jnp
        with jax.default_device(jax.devices("cpu")[0]):
            r = jnp.asarray(rel)
            nb = 16
            max_exact = 8
            ret = jnp.where(r > 0, nb, 0)
            n = jnp.abs(r)
            large = max_exact + (jnp.log(jnp.maximum(n, 1).astype(jnp.float32) / max_exact)
                                 / math.log(128 / max_exact) * (nb - max_exact)).astype(jnp.int32)
            large = jnp.minimum(large, nb - 1)
            bucket = np.asarray(ret + jnp.where(n < max_exact, n, large))
    except Exception:
        nb, max_exact = 16, 8
        ret = np.where(rel > 0, nb, 0)
        n = np.abs(rel)
        large = max_exact + (np.log(np.maximum(n, 1).astype(np.float32) / np.float32(max_exact))
                             / np.float32(math.log(128 / max_exact)) * np.float32(nb - max_exact)).astype(np.int32)
        large = np.minimum(large, nb - 1)
        bucket = ret + np.where(n < max_exact, n, large)
    ohm = np.zeros((32, 255), np.float32)
    ohm[bucket, np.arange(255)] = 1.0
    ohm[15, :] -= 1.0
    return ohm


def _rope_table(pos):
    half = 32
    inv_freq = (np.float32(10000.0) ** (-np.arange(half, dtype=np.float32) / np.float32(half))).astype(np.float32)
    ang = pos.astype(np.float32)[:, None] * inv_freq[None, :]
    return np.concatenate([np.cos(ang), np.sin(ang)], axis=1).astype(np.float32)


_CACHE = {}


def kernel(x_prompt, x_sample, cache_diff_k, cache_diff_v, cache_mla_ckv, cache_mla_krope,
           meta_tokens, rel_bias, norm_attn_g, w_in, diff_lambda, diff_subln_g,
           mla_q_norm_g, mla_w_uq, mla_kv_norm_g, mla_w_ukv, w_out, norm_ffn_g,
           ffn_w_gate, ffn_w_up, ffn_w_down, final_norm_g):
    f = lambda a: np.ascontiguousarray(np.asarray(a, dtype=np.float32))
    x_prompt = f(x_prompt); x_sample = f(x_sample)
    if "nc" not in _CACHE:
        _CACHE["nc"] = build_program()[0]
    nc = _CACHE["nc"]
    ohs = np.stack([_t5_onehot(0), _t5_onehot(-128), _t5_onehot(-16)])
    kq = np.arange(128)
    md = np.where((kq[None, :] < 64) & (kq[:, None] >= 64), MASKV, 0.0).astype(np.float32)
    shared = dict(
        w_in=f(w_in[0]), w_uq=f(mla_w_uq[0]), w_ukv=f(mla_w_ukv[0]), w_out=f(w_out[0]),
        wg=f(ffn_w_gate[0]), wu=f(ffn_w_up[0]), wd=f(ffn_w_down[0]),
        g_attn=f(norm_attn_g[0]), g_ffn=f(norm_ffn_g[0]), g_fin=f(final_norm_g),
        g_q=f(mla_q_norm_g[0]), g_kv=f(mla_kv_norm_g[0]), g_sub=f(diff_subln_g[0]),
        lam4=f(diff_lambda[0]).reshape(256), relb=f(rel_bias), xmeta=f(meta_tokens),
        oh=ohs, mask_diag=md, rope_m=_rope_table(np.arange(NMETA)), rope_s=_rope_table(PAST + np.arange(DSEQ)),
    )
    in_maps = []
    for c in range(8):
        b, par = c // 2, c % 2
        own = np.arange(NOWN) * 2 + par
        oth = np.arange(NOWN) * 2 + (1 - par)
        xb = x_prompt[b].reshape(32, 128, D)
        pos_o = (NMETA + own[:, None] * 128 + np.arange(128)[None, :]).reshape(-1)
        pos_t = (NMETA + oth[:, None] * 128 + np.arange(128)[None, :]).reshape(-1)
        selv = np.zeros((128, 4), np.float32)
        if par == 1:
            selv[:, 0] = 1.0
            mo = np.zeros((128, 128), np.float32)
        else:
            selv[:, 1] = 1.0
            selv[:, 2] = 1.0
            mo = np.full((128, 128), MASKV, np.float32)
        m = dict(shared)
        m.update(
            xo=np.ascontiguousarray(xb[own].reshape(-1, D)), xt=np.ascontiguousarray(xb[oth].reshape(-1, D)),
            xsam=x_sample[c],
            ck=f(cache_diff_k[0, c]).reshape(PAST, 1024), cv=f(cache_diff_v[0, c]).reshape(PAST, 1024),
            cckv=f(cache_mla_ckv[0, c]), ckr=f(cache_mla_krope[0, c]),
            rope_o=_rope_table(pos_o), rope_t=_rope_table(pos_t), mask_o0=mo, sel=selv,
        )
        in_maps.append(m)
    if _CACHE.get("prep_only"):
        return in_maps
    res = run_bass_kernel_spmd(nc, in_maps, core_ids=list(range(8))).results
    B = 4
    L = NMETA + SEQ
    y_prompt = np.zeros((B, SEQ, D), np.float32)
    nk = np.zeros((1, B, L, 1024), np.float32); nv = np.zeros((1, B, L, 1024), np.float32)
    nckv = np.zeros((1, B, L, KVL), np.float32); nkr = np.zeros((1, B, L, ROPE), np.float32)
    y_sample = np.zeros((8, DSEQ, D), np.float32)
    sk = np.zeros((1, 8, DSEQ, 1024), np.float32); sv = np.zeros((1, 8, DSEQ, 1024), np.float32)
    sckv = np.zeros((1, 8, DSEQ, KVL), np.float32); skr = np.zeros((1, 8, DSEQ, ROPE), np.float32)
    for c in range(8):
        b, par = c // 2, c % 2
        r = res[c]
        for i in range(NOWN):
            gt = 2 * i + par
            sl = slice(i * 128, (i + 1) * 128)
            y_prompt[b, gt * 128:(gt + 1) * 128] = r["y_o"][sl]
            dst = slice(NMETA + gt * 128, NMETA + (gt + 1) * 128)
            nk[0, b, dst] = r["kd_o"][sl]; nv[0, b, dst] = r["vd_o"][sl]
            nckv[0, b, dst] = r["ckv_o"][sl]; nkr[0, b, dst] = r["kr_o"][sl]
        if par == 0:
            nk[0, b, 0:NMETA] = r["kd_m"]; nv[0, b, 0:NMETA] = r["vd_m"]
            nckv[0, b, 0:NMETA] = r["ckv_m"]; nkr[0, b, 0:NMETA] = r["kr_m"]
        y_sample[c] = r["y_s"]
        sk[0, c] = r["kd_s"]; sv[0, c] = r["vd_s"]; sckv[0, c] = r["ckv_s"]; skr[0, c] = r["kr_s"]
    return (y_prompt, y_sample,
            nk.reshape(1, B, L, NH, 2, DH), nv.reshape(1, B, L, NH, 128), nckv, nkr,
            sk.reshape(1, 8, DSEQ, NH, 2, DH), sv.reshape(1, 8, DSEQ, NH, 128), sckv, skr)
```

```python
import contextlib
import math
import numpy as np
import concourse.bass as bass
import concourse.mybir as mybir
from concourse.bass_utils import run_bass_kernel_spmd

F32 = mybir.dt.float32
BF16 = mybir.dt.bfloat16
U8 = mybir.dt.uint8
AF = mybir.ActivationFunctionType
ALU = mybir.AluOpType
AX = mybir.AxisListType

D = 2048
SEQ = 4096
NMETA = 16
NH = 8
DH = 64
DIN = 3904
OFF_K, OFF_V, OFF_CQ, OFF_CKV, OFF_KR = 1024, 2048, 3072, 3584, 3840
QL, KVL, ROPE = 512, 256, 64
DFF = 5632
NFC = DFF // 128
EPS = 1e-6
DIFF_SCALE = DH ** -0.5
MLA_SCALE = 192 ** -0.5
LAM_INIT = 0.8 - 0.6 * math.exp(0.0)
PAST = 1024
DSEQ = 64
NOWN = 16
NKEY = 2 * NOWN * 128 + NMETA
SKEY = PAST + DSEQ
MASKV = -30000.0

ENGINES = ("pe", "act", "dve", "pool", "sp")
EPOCH = 12000
N_EPOCH = {"pe": 8, "act": 10, "dve": 10, "pool": 3, "sp": 1}


class Buf:
    __slots__ = ("name", "w", "r", "excl")

    def __init__(self, name="", excl=False):
        self.name = name
        self.w = None
        self.r = {}
        self.excl = excl


class Prog:
    def __init__(self, nc, n_dma_sp=30, n_dma_pool=14, n_dma_act=2):
        self.nc = nc
        self.ops = {e: [] for e in ENGINES}
        self.sem_handles = []
        self.sem_names = []
        self.sem_owner = {}
        self.eng_sems = {}
        for e in ENGINES:
            self.eng_sems[e] = [self._new_sem(f"s_{e}{i}", e) for i in range(N_EPOCH[e])]
        self.cnt = {e: 0 for e in ENGINES}
        self.know = {e: {} for e in ENGINES}
        self.tok_know = {}
        self.dma_sems = {
            "sp": [[self._new_sem(f"d_sp{i}", None), 0] for i in range(n_dma_sp)],
            "pool": [[self._new_sem(f"d_pool{i}", None), 0] for i in range(n_dma_pool)],
            "act": [[self._new_sem(f"d_act{i}", None), 0] for i in range(n_dma_act)],
        }
        self.dma_rr = {"sp": 0, "pool": 0, "act": 0}
        self.out_tokens = []
        self.pending_dma = []
        self.late_dma = []
        self.own_done = {e: {} for e in ENGINES}
        self.n_waits = 0
        self.n_ops = {e: 0 for e in ENGINES}

    def _new_sem(self, name, owner):
        self.sem_names.append(name)
        i = len(self.sem_names) - 1
        self.sem_owner[i] = owner
        return i

    def _next_tok(self, e):
        c = self.cnt[e]
        return (self.eng_sems[e][c // EPOCH], c % EPOCH + 1)

    def _resolve(self, e, reads, writes, extra=()):
        deps = {}

        def add(tok, raw):
            if tok is None:
                return
            s, v = tok
            if (not raw) and self.sem_owner[s] == e and e == "pe":
                return
            if deps.get(s, -1) < v:
                deps[s] = v

        for b in reads:
            add(b.w, True)
            if b.excl:
                for s, v in b.r.items():
                    if self.sem_owner[s] != e:
                        add((s, v), True)
        for b in writes:
            add(b.w, False)
            for s, v in b.r.items():
                add((s, v), False)
        for tok in extra:
            add(tok, True)
        know = self.know[e]
        waits = [(s, v) for s, v in deps.items() if know.get(s, -1) < v]
        for s, v in waits:
            tk = self.tok_know.get((s, v))
            if tk:
                for s2, v2 in tk.items():
                    if know.get(s2, -1) < v2:
                        know[s2] = v2
            if know.get(s, -1) < v:
                know[s] = v
        self.n_waits += len(waits)
        return waits

    def _finish(self, tok, reads, writes, e):
        if tok not in self.tok_know:
            self.tok_know[tok] = dict(self.know[e])
        else:
            self.tok_know[tok].update(self.know[e])
        if self.own_done[e]:
            self.tok_know[tok].update(self.own_done[e])
        s, v = tok
        for b in reads:
            if b.r.get(s, -1) < v:
                b.r[s] = v
        for b in writes:
            b.w = tok
            b.r = {}

    def op(self, e, fn, reads=(), writes=(), signal=True, extra=()):
        waits = self._resolve(e, reads, writes, extra)
        tok = self._next_tok(e)
        self.n_ops[e] += 1
        if signal:
            self.cnt[e] += 1
            self.ops[e].append((waits, fn, (tok[0], 1)))
            if self.cnt[e] % EPOCH == 0:
                self.own_done[e][tok[0]] = EPOCH
        else:
            self.ops[e].append((waits, fn, None))
        self._finish(tok, reads, writes, e)
        return tok

    def dma(self, q, fn, reads=(), writes=(), is_output=False, extra=(), late=False):
        pool = self.dma_sems[q]
        i = self.dma_rr[q]
        self.dma_rr[q] = (i + 1) % len(pool)
        s, v = pool[i]
        prev = [(s, v)] if v > 0 else []
        waits = self._resolve(q, reads, writes, tuple(extra) + tuple(prev))
        pool[i][1] = v + 16
        tok = (s, v + 16)
        self.n_ops[q] += 1
        self.ops[q].append((waits, fn, (s, 16)))
        self._finish(tok, reads, writes, q)
        (self.late_dma if late else self.pending_dma).append(tok)
        if is_output:
            self.out_tokens.append(tok)
        return tok

    def barrier(self, final=False):
        toks = list(self.pending_dma)
        if final:
            toks += self.late_dma
            self.late_dma = []
        for e in ENGINES:
            c = self.cnt[e]
            if c > 0:
                cc = c - 1
                toks.append((self.eng_sems[e][cc // EPOCH], cc % EPOCH + 1))
        for e in ENGINES:
            waits = self._resolve(e, (), (), tuple(toks))
            self.ops[e].append((waits, None, None))
        self.pending_dma = []

    def finalize(self):
        waits = self._resolve("sp", (), (), tuple(self.out_tokens))
        self.ops["sp"].append((waits, None, None))

    def emit(self):
        nc = self.nc
        with contextlib.ExitStack() as st:
            for nm in self.sem_names:
                self.sem_handles.append(st.enter_context(nc.semaphore(nm)))
            block = st.enter_context(nc.Block())
            H = self.sem_handles

            def run(eng, ops):
                for waits, fn, inc in ops:
                    for s, v in waits:
                        eng.wait_ge(H[s], v)
                    if fn is None:
                        continue
                    ins = fn(eng)
                    if inc is not None:
                        ins.then_inc(H[inc[0]], inc[1])

            @block.sync
            def _(eng):
                run(eng, self.ops["sp"])

            @block.tensor
            def _(eng):
                run(eng, self.ops["pe"])

            @block.scalar
            def _(eng):
                run(eng, self.ops["act"])

            @block.vector
            def _(eng):
                run(eng, self.ops["dve"])

            @block.gpsimd
            def _(eng):
                run(eng, self.ops["pool"])


class Arena:
    def __init__(self, ap):
        self.ap = ap
        self.off = 0
        self.size = ap.shape[1]

    def alloc(self, shape, dt):
        es = 4 if dt == F32 else 2
        n = int(np.prod(shape)) * es
        assert self.off + n <= self.size, f"arena overflow {self.off}+{n}>{self.size}"
        v = self.ap[:, self.off:self.off + n].bitcast(dt)
        self.off += (n + 63) // 64 * 64
        if len(shape) == 2:
            v = v.rearrange("p (a b) -> p a b", b=shape[1])
        elif len(shape) == 3:
            v = v.rearrange("p (a b c) -> p a b c", b=shape[1], c=shape[2])
        return v


def build_program():
    nc = bass.Bass("TRN2", target_bir_lowering=False)
    P = Prog(nc)

    def din(name, shape):
        return nc.dram_tensor(name, list(shape), F32, kind="ExternalInput").ap()

    def dout(name, shape):
        return nc.dram_tensor(name, list(shape), F32, kind="ExternalOutput").ap()

    def dscr(name, shape, dt=BF16):
        return nc.dram_tensor(name, list(shape), dt).ap()

    xo = din("xo", [NOWN * 128, D]); xt = din("xt", [NOWN * 128, D])
    xmeta = din("xmeta", [NMETA, D]); xsam = din("xsam", [DSEQ, D])
    ck = din("ck", [PAST, 1024]); cv = din("cv", [PAST, 1024])
    cckv = din("cckv", [PAST, KVL]); ckr = din("ckr", [PAST, ROPE])
    w_in = din("w_in", [D, DIN]); w_uq = din("w_uq", [QL, 1536]); w_ukv = din("w_ukv", [KVL, 2048])
    w_out = din("w_out", [D, D]); wg = din("wg", [D, DFF]); wu = din("wu", [D, DFF]); wd = din("wd", [DFF, D])
    g_attn = din("g_attn", [D]); g_ffn = din("g_ffn", [D]); g_fin = din("g_fin", [D])
    g_q = din("g_q", [QL]); g_kv = din("g_kv", [KVL]); g_sub = din("g_sub", [128])
    lam4 = din("lam4", [256]); relb = din("relb", [32, 8])
    rope_o = din("rope_o", [NOWN * 128, 64]); rope_t = din("rope_t", [NOWN * 128, 64])
    rope_m = din("rope_m", [NMETA, 64]); rope_s = din("rope_s", [DSEQ, 64])
    oh = din("oh", [3, 32, 255])
    mask_diag = din("mask_diag", [128, 128]); mask_o0 = din("mask_o0", [128, 128])
    sel = din("sel", [128, 4])
    y_o = dout("y_o", [NOWN * 128, D]); kd_o = dout("kd_o", [NOWN * 128, 1024]); vd_o = dout("vd_o", [NOWN * 128, 1024])
    ckv_o = dout("ckv_o", [NOWN * 128, KVL]); kr_o = dout("kr_o", [NOWN * 128, ROPE])
    kd_m = dout("kd_m", [NMETA, 1024]); vd_m = dout("vd_m", [NMETA, 1024])
    ckv_m = dout("ckv_m", [NMETA, KVL]); kr_m = dout("kr_m", [NMETA, ROPE])
    y_s = dout("y_s", [DSEQ, D]); kd_s = dout("kd_s", [DSEQ, 1024]); vd_s = dout("vd_s", [DSEQ, 1024])
    ckv_s = dout("ckv_s", [DSEQ, KVL]); kr_s = dout("kr_s", [DSEQ, ROPE])
    w_in_b = dscr("w_in_b", [D, DIN]); w_uq_b = dscr("w_uq_b", [QL, 1536]); w_ukv_b = dscr("w_ukv_b", [KVL, 2048])
    w_out_b = dscr("w_out_b", [4, 128, 16, 512]); wg_b = dscr("wg_b", [NFC // 2, 128, 16, 256])
    wu_b = dscr("wu_b", [NFC // 2, 128, 16, 256]); wd_b = dscr("wd_b", [16, 128, NFC, 128])
    ck_b = dscr("ck_b", [PAST, 1024]); cckv_b = dscr("cckv_b", [PAST, KVL]); ckr_b = dscr("ckr_b", [PAST, ROPE])

    def seq_scratch(pfx, nkey, nq):
        return dict(
            KdT=dscr(pfx + "KdT", [NH, 128, nkey]), Vd=dscr(pfx + "Vd", [nkey, 1024]),
            KmT=dscr(pfx + "KmT", [NH, 128, nkey]), Vm=dscr(pfx + "Vm", [nkey, 1024]),
            KrT=dscr(pfx + "KrT", [64, nkey]),
            QdT=dscr(pfx + "QdT", [NH, 128, nq]), QnT=dscr(pfx + "QnT", [NH, 128, nq]),
            QrT=dscr(pfx + "QrT", [NH, 64, nq]), MixT=dscr(pfx + "MixT", [16, 128, nq]),
            nkey=nkey, nq=nq)

    SP_ = seq_scratch("p", NKEY, NOWN * 128)
    SS_ = seq_scratch("s", SKEY, DSEQ)

    arena_ap = nc.alloc_sbuf_tensor("arena", [128, 206 * 1024], U8).ap()
    AR = Arena(arena_ap)
    PS = nc.alloc_psum_tensor("ps", [128, 4096], F32).ap()
    bankB = [Buf(f"bank{i}", excl=True) for i in range(8)]

    def bank(i, dt=F32):
        v = PS[:, i * 512:(i + 1) * 512]
        return v.bitcast(dt) if dt != F32 else v

    rr = {"b": 0, "cp": 0}

    def next_bank(lo=0, hi=8):
        i = lo + rr["b"] % (hi - lo)
        rr["b"] += 1
        return i

    def copy_eng():
        rr["cp"] += 1
        return "dve"

    def copy_op(e, out, in_, r, w):
        if e == "act":
            P.op("act", lambda g: g.activation(out=out, in_=in_, func=AF.Copy), r, w)
        else:
            P.op(e, lambda g: g.tensor_copy(out, in_), r, w)

    ident_f = AR.alloc([128], F32); ident = AR.alloc([128], BF16)
    ones_b = AR.alloc([128], BF16); ones_f = AR.alloc([128], F32)
    lamb = AR.alloc([256], F32); lprod = AR.alloc([2, 64], F32); lsum = AR.alloc([2], F32)
    lexp = AR.alloc([2], F32); lam_t = AR.alloc([1], F32); neg_lam = AR.alloc([1], F32)
    gsub = AR.alloc([1], F32); gsub8 = AR.alloc([1], F32)
    sel_s = AR.alloc([4], F32)
    Bdiag = AR.alloc([NH, 128], BF16); Bo0 = AR.alloc([NH, 128], BF16); Bo1 = AR.alloc([NH, 128], BF16)
    Bmeta = AR.alloc([NH, 128], BF16); Bs7 = AR.alloc([NH, 64], BF16); Bsn = AR.alloc([NH, 64], BF16)
    Mdiag = AR.alloc([128], BF16); Mo0 = AR.alloc([128], BF16)
    b_const = Buf("const")
    persist_mark = AR.off

    def setup():
        R = AR.alloc([8], F32); R8 = AR.alloc([8], F32)
        oh_s = AR.alloc([3, 255], F32)
        md_f = AR.alloc([128], F32); mo_f = AR.alloc([128], F32)
        bR = Buf(); bT = Buf()
        P.op("pool", lambda g: g.memset(ident_f, 0.0), (), [bT])
        P.op("pool", lambda g: g.affine_select(ident_f, ident_f, pattern=[[-1, 128]], compare_op=ALU.not_equal,
                                               fill=1.0, base=0, channel_multiplier=1), [bT], [bT])
        P.op("pool", lambda g: g.memset(ones_f, 1.0), (), [bT])
        P.op("dve", lambda g: g.tensor_copy(ident, ident_f), [bT], [b_const])
        P.op("dve", lambda g: g.tensor_copy(ones_b, ones_f), [bT], [b_const])
        P.dma("sp", lambda g: g.dma_start(out=lamb, in_=lam4.partition_broadcast(128)), (), [bR])
        P.dma("sp", lambda g: g.dma_start(out=gsub, in_=g_sub.rearrange("(p o) -> p o", o=1)), (), [bR])
        P.dma("sp", lambda g: g.dma_start(out=sel_s, in_=sel), (), [bR])
        P.dma("sp", lambda g: g.dma_start(out=R[0:32], in_=relb), (), [bR])
        P.dma("sp", lambda g: g.dma_start(out=oh_s[0:32], in_=oh.rearrange("t b r -> b t r")), (), [bR])
        P.dma("sp", lambda g: g.dma_start(out=md_f, in_=mask_diag), (), [bR])
        P.dma("sp", lambda g: g.dma_start(out=mo_f, in_=mask_o0), (), [bR])
        lv = lamb.rearrange("p (a b c) -> p a b c", a=2, b=2, c=64)
        P.op("dve", lambda g: g.tensor_tensor(out=lprod, in0=lv[:, :, 0, :], in1=lv[:, :, 1, :], op=ALU.mult), [bR], [bT])
        P.op("dve", lambda g: g.reduce_sum(out=lsum, in_=lprod, axis=AX.X), [bT], [bT])
        P.op("act", lambda g: g.activation(out=lexp, in_=lsum, func=AF.Exp), [bT], [bT])
        P.op("dve", lambda g: g.tensor_tensor(out=lam_t, in0=lexp[:, 0:1], in1=lexp[:, 1:2], op=ALU.subtract), [bT], [bT])
        P.op("dve", lambda g: g.tensor_scalar(out=neg_lam, in0=lam_t, scalar1=LAM_INIT, scalar2=-1.0,
                                              op0=ALU.add, op1=ALU.mult), [bT], [b_const])
        P.op("dve", lambda g: g.tensor_scalar(out=gsub8, in0=gsub, scalar1=1.0 - LAM_INIT, scalar2=None,
                                              op0=ALU.mult), [bR], [b_const])
        P.op("dve", lambda g: g.tensor_scalar(out=R8[0:32], in0=R[0:32], scalar1=1.0 / DIFF_SCALE, scalar2=None,
                                              op0=ALU.mult), [bR], [bT])
        views = []
        for t in range(3):
            bA, bB = 2 * t, 2 * t + 1
            for q in range(128):
                bi = bA if q < 64 else bB
                o = bank(bi)[:, (q % 64) * 8:(q % 64) * 8 + 8]
                P.op("pe", lambda g, o=o, t=t, q=q: g.matmul(o, lhsT=oh_s[0:32, t, 127 - q:255 - q], rhs=R8[0:32],
                                                             start=True, stop=True),
                     [bT, bR], [bankB[bi]], signal=(q % 64 == 63))
            views.append((bA, bB))

        def tv(t, h, k0, k1, q0, q1):
            res = []
            for half in range(2):
                a, b = max(q0, half * 64), min(q1, half * 64 + 64)
                if a >= b:
                    continue
                bi = views[t][half]
                v = bank(bi).rearrange("p (q h) -> p q h", h=8)[k0:k1, a - half * 64:b - half * 64, h]
                res.append((v, bi, a, b))
            return res

        P.op("dve", lambda g: g.memset(Bmeta, 0.0), (), [b_const])
        P.op("dve", lambda g: g.memset(Bsn, 0.0), (), [b_const])
        for h in range(NH):
            for v, bi, a, b in tv(0, h, 0, 128, 0, 128):
                P.op("dve", lambda g, v=v, a=a, b=b, h=h: g.tensor_tensor(out=Bdiag[:, h, a:b], in0=v, in1=md_f[:, a:b], op=ALU.add),
                     [bankB[bi], bR], [b_const])
            for v, bi, a, b in tv(1, h, 0, 128, 0, 128):
                P.op("dve", lambda g, v=v, a=a, b=b, h=h: g.scalar_tensor_tensor(out=Bo0[:, h, a:b], in0=v, scalar=sel_s[:, 0:1],
                                                                                 in1=mo_f[:, a:b], op0=ALU.mult, op1=ALU.add),
                     [bankB[bi], bR], [b_const])
                P.op("act", lambda g, v=v, a=a, b=b, h=h: g.activation(out=Bo1[:, h, a:b], in_=v, func=AF.Copy, scale=sel_s[:, 1:2]),
                     [bankB[bi], bR], [b_const])
            for v, bi, a, b in tv(2, h, 0, 16, 0, 128):
                P.op("act", lambda g, v=v, a=a, b=b, h=h: g.activation(out=Bmeta[0:16, h, a:b], in_=v, func=AF.Copy, scale=sel_s[0:16, 2:3]),
                     [bankB[bi], bR], [b_const])
            for v, bi, a, b in tv(1, h, 0, 128, 0, 64):
                P.op("dve", lambda g, v=v, a=a, b=b, h=h: g.tensor_copy(Bs7[:, h, a:b], v), [bankB[bi]], [b_const])
            for v, bi, a, b in tv(0, h, 0, 64, 0, 64):
                P.op("dve", lambda g, v=v, a=a, b=b, h=h: g.tensor_copy(Bsn[0:64, h, a:b], v), [bankB[bi]], [b_const])
        P.op("dve", lambda g: g.tensor_copy(Mdiag, md_f), [bR], [b_const])
        P.op("dve", lambda g: g.tensor_copy(Mo0, mo_f), [bR], [b_const])

    def dma_mid(q, out3, in3, step, r, w, extra=()):
        m = out3.shape[1]
        for a in range(0, m, step):
            b = min(m, a + step)
            P.dma(q, lambda g, a=a, b=b: g.dma_start(out=out3[:, a:b], in_=in3[:, a:b]), r, w, extra=extra)

    wbuf = {}

    deferred = []
    deferred_late = []

    def cast(name, dst, src, rows_per=128, defer=False, slab=None, late=False, cols=None, tokens=None):
        b = wbuf.get(name) or Buf(name)
        wbuf[name] = b
        n = src.shape[0]
        for r in range(0, n, rows_per):
            def go(r=r):
                if cols is not None:
                    for (a_, b_) in cols:
                        tk_ = P.dma("pool", lambda g, a_=a_, b_=b_: g.dma_start(out=dst[r:r + rows_per, a_:b_], in_=src[r:r + rows_per, a_:b_]), (), [b],
                                    late=(tokens is not None))
                        if tokens is not None:
                            tokens.append(tk_)
                elif slab is None:
                    P.dma("pool", lambda g: g.dma_start(out=dst[r:r + rows_per], in_=src[r:r + rows_per]), (), [b])
                else:
                    w = slab
                    dv = dst.rearrange("a p c w -> p a c w")[:, :, r // 128, :]
                    sv = src[r:r + 128].rearrange("p (a w) -> p a w", w=w)
                    P.dma("pool", lambda g: g.dma_start(out=dv, in_=sv), (), [b], late=late)
            if late:
                deferred_late.append(go)
            elif defer:
                deferred.append(go)
            else:
                go()
        return b

    def rstd_from_ss(ss, rstd, n, nfeat, rb):
        P.op("dve", lambda g: g.tensor_scalar(out=rstd[0:n], in0=ss[0:n], scalar1=1.0 / nfeat, scalar2=EPS,
                                              op0=ALU.mult, op1=ALU.add), [rb], [rb])
        P.op("act", lambda g: g.activation(out=rstd[0:n], in_=rstd[0:n], func=AF.Sqrt), [rb], [rb])
        P.op("dve", lambda g: g.reciprocal(out=rstd[0:n], in_=rstd[0:n]), [rb], [rb])

    def transposes(src_fn, nblk, n, cols, dst, dstB, srcB, dt=BF16, idn=None):
        idn = ident if dt == BF16 else ident_f
        per = 8 if dt == BF16 else 4
        for j0 in range(0, nblk, per):
            bi = next_bank()
            bv = bank(bi, dt).rearrange("p (j t) -> p j t", t=128)
            m = min(per, nblk - j0)
            for j in range(j0, j0 + m):
                P.op("pe", lambda g, j=j, j0=j0, bv=bv: g.transpose(out=bv[0:cols, j - j0, 0:n], in_=src_fn(j), identity=idn[0:n, 0:n]),
                     [srcB, b_const], [bankB[bi]], signal=(j == j0 + m - 1))
            copy_op(copy_eng(), dst[0:cols, j0:j0 + m, 0:n], bv[0:cols, 0:m, 0:n], [bankB[bi]], [dstB])

    def norm_T(xs, bx, n, gbc, bg, hb, bhb, hT, bhT, ss, rstd, bst):
        import os
        mf = int(os.environ.get("MK_F", "9"))
        if mf == 0:
            return
        P.op("act", lambda g: g.activation(out=hb[0:n], in_=xs[0:n], func=AF.Square, accum_out=ss[0:n]), [bx], [bhb, bst])
        if mf == 1:
            return
        rstd_from_ss(ss, rstd, n, D, bst)
        if mf == 2:
            return
        P.op("dve", lambda g: g.scalar_tensor_tensor(out=hb[0:n], in0=xs[0:n], scalar=rstd[0:n], in1=gbc[0:n],
                                                     op0=ALU.mult, op1=ALU.mult), [bx, bst, bg], [bhb])

    def norm_T2(n, hb, bhb, hT, bhT):
        transposes(lambda j: hb[0:n, j * 128:(j + 1) * 128], 16, n, 128, hT, bhT, bhb)

    def mm_group(bi, n, ncols, lhs_fn, rhs_fn, nk, rB, dst=None):
        o = bank(bi)[0:n, 0:ncols] if dst is None else dst
        for c in range(nk):
            P.op("pe", lambda g, c=c: g.matmul(o, lhsT=lhs_fn(c), rhs=rhs_fn(c), start=(c == 0), stop=(c == nk - 1)),
                 rB, [bankB[bi]], signal=(c == nk - 1))

    def rope_tok(src_fn, dst_fn, cos, sin, n, tmp, rB, wB, tB):
        x1, x2 = src_fn(0), src_fn(1)
        t1, t2 = tmp
        P.op("dve", lambda g: g.tensor_tensor(out=t1, in0=x1, in1=cos, op=ALU.mult), rB, [tB])
        P.op("dve", lambda g: g.tensor_tensor(out=t2, in0=x2, in1=sin, op=ALU.mult), rB, [tB])
        P.op("dve", lambda g: g.tensor_tensor(out=dst_fn(0), in0=t1, in1=t2, op=ALU.subtract), [tB], [wB])
        P.op("dve", lambda g: g.tensor_tensor(out=t1, in0=x1, in1=sin, op=ALU.mult), rB + [wB], [tB])
        P.op("dve", lambda g: g.tensor_tensor(out=t2, in0=x2, in1=cos, op=ALU.mult), rB, [tB])
        P.op("dve", lambda g: g.tensor_tensor(out=dst_fn(1), in0=t1, in1=t2, op=ALU.add), [tB], [wB])

    def passA():
        AR.off = persist_mark
        WinK = AR.alloc([16, 2368], BF16); Wukv = AR.alloc([2, 2048], BF16)
        g_bc = AR.alloc([D], F32); gkv_bc = AR.alloc([KVL], F32)
        xs2 = [AR.alloc([D], F32) for _ in range(2)]
        hb = AR.alloc([D], BF16)
        hT2 = [AR.alloc([16, 128], BF16) for _ in range(2)]
        kd_f = AR.alloc([1024], F32); vd_f = AR.alloc([1024], F32)
        kd_b = AR.alloc([1024], BF16); vd_b = AR.alloc([1024], BF16)
        ckv_f = AR.alloc([KVL], F32); ckv_b = AR.alloc([KVL], BF16)
        kr_f = AR.alloc([64], F32); kr_b = AR.alloc([64], BF16)
        rope2 = [AR.alloc([64], F32) for _ in range(2)]
        rt = [AR.alloc([32], F32) for _ in range(2)]
        junk = AR.alloc([KVL], F32)
        ss = AR.alloc([1], F32); rstd = AR.alloc([1], F32); ssk = AR.alloc([1], F32); rstdk = AR.alloc([1], F32)
        KdT_st = AR.alloc([NH, 128], BF16); ckvT = AR.alloc([2, 128], BF16); KrT_st = AR.alloc([1, 128], BF16)
        KmT_st = AR.alloc([NH, 128], BF16); vm_b = AR.alloc([1024], BF16)
        bW = Buf(); bg = Buf()
        bx2 = [Buf(), Buf()]; bhb = Buf(); bhT2 = [Buf(), Buf()]
        bkdf = Buf(); bvdf = Buf(); bkdb = Buf(); bvdb = Buf(); bckf = Buf(); bckb = Buf(); bkrf = Buf(); bkrb = Buf()
        brope2 = [Buf(), Buf()]; brt = Buf(); bjunk = Buf(); bst = Buf(); bstk = Buf()
        bKdT = Buf(); bckvT = Buf(); bKrT = Buf(); bKmT = Buf(); bvmb = Buf()
        wv = w_in_b.rearrange("(c p) n -> p c n", p=128)
        bWu = Buf()

        def load_WinK():
            dma_mid("sp", WinK[:, :, 2048:2368], wv[:, :, OFF_CKV:DIN], 4, (), [bW], extra=tuple(winK_toks))
            dma_mid("sp", WinK[:, :, 0:2048], wv[:, :, OFF_K:OFF_CQ], 2, (), [bW], extra=tuple(winK_toks))
        P.dma("sp", lambda g: g.dma_start(out=Wukv, in_=w_ukv_b.rearrange("(c p) n -> p c n", p=128)), [wbuf["w_ukv"]], [bWu])
        P.dma("sp", lambda g: g.dma_start(out=g_bc, in_=g_attn.partition_broadcast(128)), (), [bg])
        P.dma("sp", lambda g: g.dma_start(out=gkv_bc, in_=g_kv.partition_broadcast(128)), (), [bg])

        tiles = []
        for j in range(PAST // 128):
            tiles.append(dict(cache=j, n=128, seq=SS_, koff=j * 128))
        tiles.append(dict(x=xmeta, n=NMETA, seq=SP_, koff=2 * NOWN * 128, rope=rope_m, outs=(kd_m, vd_m, ckv_m, kr_m)))
        tiles.append(dict(x=xsam, n=DSEQ, seq=SS_, koff=PAST, rope=rope_s, outs=(kd_s, vd_s, ckv_s, kr_s)))
        for i in range(NOWN):
            r = slice(i * 128, (i + 1) * 128)
            tiles.append(dict(x=xo[r], n=128, seq=SP_, koff=i * 128, rope=rope_o[r],
                              outs=(kd_o[r], vd_o[r], ckv_o[r], kr_o[r])))
            tiles.append(dict(x=xt[r], n=128, seq=SP_, koff=NOWN * 128 + i * 128, rope=rope_t[r], outs=None))

        import os
        flt = os.environ.get("MK_A", "")
        if flt:
            keep = []
            for t in tiles:
                kind = "cache" if "cache" in t else ("meta" if t["n"] == NMETA else ("sam" if t["n"] == DSEQ else "own"))
                if kind in flt.split(","):
                    keep.append(t)
            tiles[:] = keep[:int(os.environ.get("MK_AN", "100"))]

        def load(idx):
            t = tiles[idx]
            if "cache" in t:
                return
            p = t["slot"]
            n = t["n"]
            P.dma("sp", lambda g: g.dma_start(out=xs2[p][0:n], in_=t["x"]), (), [bx2[p]])
            P.dma("sp", lambda g: g.dma_start(out=rope2[p][0:n], in_=t["rope"]), (), [brope2[p]])

        def front(idx):
            t = tiles[idx]
            if "cache" in t:
                return
            p = t["slot"]
            norm_T(xs2[p], bx2[p], t["n"], g_bc, bg, hb, bhb, hT2[p], bhT2[p], ss, rstd, bst)

        def front_T(idx):
            if idx >= len(tiles):
                return
            t = tiles[idx]
            if "cache" in t:
                return
            p = t["slot"]
            norm_T2(t["n"], hb, bhb, hT2[p], bhT2[p])

        cut = int(os.environ.get("MK_CUT", "9"))

        def back(idx, mid=lambda: None):
            t = tiles[idx]
            n = t["n"]; seq = t["seq"]; koff = t["koff"]
            if cut == 0:
                return
            if "cache" in t:
                mid()
                j = t["cache"]
                r = slice(j * 128, (j + 1) * 128)
                P.dma("sp", lambda g: g.dma_start(out=kd_b[0:n], in_=ck_b[r]), [wbuf["ck"]], [bkdb])
                P.dma("sp", lambda g: g.dma_start(out=ckv_b[0:n], in_=cckv_b[r]), [wbuf["cckv"]], [bckb])
                P.dma("sp", lambda g: g.dma_start(out=kr_b[0:n], in_=ckr_b[r]), [wbuf["ckr"]], [bkrb])
            else:
                p = t["slot"]
                hT = hT2[p]
                rB = [bhT2[p], bW]
                for gi, (c0, dstf, dstb, bf_, bb_) in enumerate([(0, kd_f, kd_b, bkdf, bkdb), (512, kd_f, kd_b, bkdf, bkdb),
                                                                  (1024, vd_f, vd_b, bvdf, bvdb), (1536, vd_f, vd_b, bvdf, bvdb)]):
                    bi = next_bank()
                    mm_group(bi, n, 512, lambda c: hT[:, c, 0:n], lambda c, c0=c0: WinK[:, c, c0:c0 + 512], 16, rB)
                    lc = c0 % 1024
                    P.op("act", lambda g, bi=bi, dstf=dstf, lc=lc: g.activation(out=dstf[0:n, lc:lc + 512], in_=bank(bi)[0:n, 0:512], func=AF.Copy),
                         [bankB[bi]], [bf_])
                    P.op("dve", lambda g, bi=bi, dstb=dstb, lc=lc: g.tensor_copy(dstb[0:n, lc:lc + 512], bank(bi)[0:n, 0:512]),
                         [bankB[bi]], [bb_])
                if cut == 1:
                    return
                bi = next_bank()
                mm_group(bi, n, 320, lambda c: hT[:, c, 0:n], lambda c: WinK[:, c, 2048:2368], 16, rB)
                bk = bank(bi)
                mid()
                P.op("act", lambda g: g.activation(out=junk[0:n], in_=bk[0:n, 0:KVL], func=AF.Square, accum_out=ssk[0:n]),
                     [bankB[bi]], [bjunk, bstk])
                rstd_from_ss(ssk, rstdk, n, KVL, bstk)
                P.op("dve", lambda g: g.scalar_tensor_tensor(out=ckv_f[0:n], in0=bk[0:n, 0:KVL], scalar=rstdk[0:n], in1=gkv_bc[0:n],
                                                             op0=ALU.mult, op1=ALU.mult), [bankB[bi], bstk, bg], [bckf])
                P.op("act", lambda g: g.activation(out=ckv_b[0:n], in_=ckv_f[0:n], func=AF.Copy), [bckf], [bckb])
                rp = rope2[p]
                rope_tok(lambda hf: bk[0:n, KVL + 32 * hf:KVL + 32 * hf + 32], lambda hf: kr_f[0:n, 32 * hf:32 * hf + 32],
                         rp[0:n, 0:32], rp[0:n, 32:64], n, (rt[0][0:n], rt[1][0:n]), [bankB[bi], brope2[p]], bkrf, brt)
                P.op("act", lambda g: g.activation(out=kr_b[0:n], in_=kr_f[0:n], func=AF.Copy), [bkrf], [bkrb])
                if t["outs"] is not None:
                    o_kd, o_vd, o_ckv, o_kr = t["outs"]
                    P.dma("pool", lambda g: g.dma_start(out=o_kd, in_=kd_f[0:n]), [bkdf], (), is_output=True)
                    P.dma("pool", lambda g: g.dma_start(out=o_vd, in_=vd_f[0:n]), [bvdf], (), is_output=True)
                    P.dma("pool", lambda g: g.dma_start(out=o_ckv, in_=ckv_f[0:n]), [bckf], (), is_output=True)
                    P.dma("pool", lambda g: g.dma_start(out=o_kr, in_=kr_f[0:n]), [bkrf], (), is_output=True)
                P.dma("pool", lambda g: g.dma_start(out=seq["Vd"][koff:koff + n, :], in_=vd_b[0:n]), [bvdb], ())
            if cut == 2:
                return
            transposes(lambda j: kd_b[0:n, j * 128:(j + 1) * 128], NH, n, 128, KdT_st, bKdT, bkdb)
            P.dma("pool", lambda g: g.dma_start(out=seq["KdT"][:, :, koff:koff + n].rearrange("h p k -> p h k"), in_=KdT_st[:, :, 0:n]),
                  [bKdT], ())
            transposes(lambda j: ckv_b[0:n, j * 128:(j + 1) * 128], 2, n, 128, ckvT, bckvT, bckb)
            transposes(lambda j: kr_b[0:n, 0:64], 1, n, 64, KrT_st, bKrT, bkrb)
            P.dma("pool", lambda g: g.dma_start(out=seq["KrT"][:, koff:koff + n], in_=KrT_st[0:64, 0, 0:n]), [bKrT], ())
            wk = Wukv.rearrange("p c (h x) -> p c h x", x=256)
            for hh in range(2):
                bi = next_bank()
                for h4 in range(4):
                    h = hh * 4 + h4
                    mm_group(bi, 128, n, lambda c, h=h: wk[:, c, h, 0:128], lambda c: ckvT[:, c, 0:n], 2, [bckvT, bWu],
                             dst=bank(bi)[:, h4 * 128:h4 * 128 + n])
                copy_op(copy_eng(), KmT_st[:, hh * 4:hh * 4 + 4, 0:n],
                        bank(bi).rearrange("p (j t) -> p j t", t=128)[:, :, 0:n], [bankB[bi]], [bKmT])
            P.dma("pool", lambda g: g.dma_start(out=seq["KmT"][:, :, koff:koff + n].rearrange("h p k -> p h k"), in_=KmT_st[:, :, 0:n]),
                  [bKmT], ())
            for hh in range(2):
                bi = next_bank()
                mm_group(bi, n, 512, lambda c: ckvT[:, c, 0:n], lambda c, hh=hh: wk[:, c, hh * 4:hh * 4 + 4, 128:256], 2, [bckvT, bWu])
                copy_op(copy_eng(), vm_b[0:n, hh * 512:(hh + 1) * 512], bank(bi)[0:n, 0:512], [bankB[bi]], [bvmb])
            P.dma("pool", lambda g: g.dma_start(out=seq["Vm"][koff:koff + n, :], in_=vm_b[0:n]), [bvmb], ())

        slot = 0
        for t in tiles:
            if "cache" not in t:
                t["slot"] = slot
                slot ^= 1
        nt = len(tiles)
        wk_loaded = [False]

        def maybe_WinK(idx):
            if not wk_loaded[0] and idx < nt and "cache" not in tiles[idx]:
                load_WinK()
                wk_loaded[0] = True
        load(0)
        front(0)
        maybe_WinK(0)
        front_T(0)
        for i in range(nt):
            if i + 1 < nt:
                load(i + 1)
                front(i + 1)
                maybe_WinK(i + 1)
            back(i, mid=lambda i=i: front_T(i + 1))
            for _ in range(3):
                if deferred:
                    deferred.pop(0)()
        while deferred:
            deferred.pop(0)()

    def passB():
        AR.off = persist_mark
        WinQ = AR.alloc([16, 1536], BF16); Wuq = AR.alloc([4, 1536], BF16)
        g_bc = AR.alloc([D], F32); gq_bc = AR.alloc([QL], F32)
        xs2 = [AR.alloc([D], F32) for _ in range(2)]
        hb = AR.alloc([D], BF16)
        hT2 = [AR.alloc([16, 128], BF16) for _ in range(2)]
        qd_b = AR.alloc([1024], BF16); cq_b = AR.alloc([QL], BF16); cqT = AR.alloc([4, 128], BF16)
        qn_b = AR.alloc([NH, 128], BF16); qr_f = AR.alloc([NH, 64], F32); qr_b = AR.alloc([NH, 64], BF16)
        rope2 = [AR.alloc([64], F32) for _ in range(2)]
        rt = [AR.alloc([2, 32], F32) for _ in range(2)]
        junk = AR.alloc([QL], F32)
        ss = AR.alloc([1], F32); rstd = AR.alloc([1], F32); ssq = AR.alloc([1], F32); rstdq = AR.alloc([1], F32)
        QdT_st = AR.alloc([NH, 128], BF16); QnT_st = AR.alloc([NH, 128], BF16); QrT_st = AR.alloc([NH, 128], BF16)
        bW = Buf(); bg = Buf(); bx2 = [Buf(), Buf()]; bhb = Buf(); bhT2 = [Buf(), Buf()]
        bqdb = Buf(); bcqb = Buf(); bcqT = Buf(); bqnb = Buf(); bqrf = Buf(); bqrb = Buf()
        brope2 = [Buf(), Buf()]; brt = Buf(); bjunk = Buf(); bst = Buf(); bstq = Buf()
        bQdT = Buf(); bQnT = Buf(); bQrT = Buf()
        wv = w_in_b.rearrange("(c p) n -> p c n", p=128)
        dma_mid("sp", WinQ[:, :, 0:1024], wv[:, :, 0:OFF_K], 4, [wbuf["w_in"]], [bW])
        dma_mid("sp", WinQ[:, :, 1024:1536], wv[:, :, OFF_CQ:OFF_CKV], 4, [wbuf["w_in"]], [bW])
        P.dma("sp", lambda g: g.dma_start(out=Wuq, in_=w_uq_b.rearrange("(c p) n -> p c n", p=128)), [wbuf["w_uq"]], [bW])
        P.dma("sp", lambda g: g.dma_start(out=g_bc, in_=g_attn.partition_broadcast(128)), (), [bg])
        P.dma("sp", lambda g: g.dma_start(out=gq_bc, in_=g_q.partition_broadcast(128)), (), [bg])
        tiles = [dict(x=xsam, n=DSEQ, seq=SS_, qoff=0, rope=rope_s)]
        for i in range(NOWN):
            r = slice(i * 128, (i + 1) * 128)
            tiles.append(dict(x=xo[r], n=128, seq=SP_, qoff=i * 128, rope=rope_o[r]))

        import os
        if os.environ.get("MK_B", "") == "s":
            tiles[:] = tiles[:1]

        def load(idx):
            t = tiles[idx]; p = idx % 2; n = t["n"]
            P.dma("sp", lambda g: g.dma_start(out=xs2[p][0:n], in_=t["x"]), (), [bx2[p]])
            P.dma("sp", lambda g: g.dma_start(out=rope2[p][0:n], in_=t["rope"]), (), [brope2[p]])

        def front(idx):
            t = tiles[idx]; p = idx % 2
            norm_T(xs2[p], bx2[p], t["n"], g_bc, bg, hb, bhb, hT2[p], bhT2[p], ss, rstd, bst)

        def front_T(idx):
            if idx >= len(tiles):
                return
            t = tiles[idx]; p = idx % 2
            norm_T2(t["n"], hb, bhb, hT2[p], bhT2[p])

        def back(idx, mid=lambda: None):
            t = tiles[idx]; p = idx % 2; n = t["n"]; seq = t["seq"]; qoff = t["qoff"]
            hT = hT2[p]; rB = [bhT2[p], bW]
            for c0 in (0, 512):
                bi = next_bank()
                mm_group(bi, n, 512, lambda c: hT[:, c, 0:n], lambda c, c0=c0: WinQ[:, c, c0:c0 + 512], 16, rB)
                copy_op(copy_eng(), qd_b[0:n, c0:c0 + 512], bank(bi)[0:n, 0:512], [bankB[bi]], [bqdb])
            transposes(lambda j: qd_b[0:n, j * 128:(j + 1) * 128], NH, n, 128, QdT_st, bQdT, bqdb)
            P.dma("pool", lambda g: g.dma_start(out=seq["QdT"][:, :, qoff:qoff + n].rearrange("h p k -> p h k"), in_=QdT_st[:, :, 0:n]),
                  [bQdT], ())
            bi = next_bank()
            mm_group(bi, n, 512, lambda c: hT[:, c, 0:n], lambda c: WinQ[:, c, 1024:1536], 16, rB)
            bk = bank(bi)
            mid()
            P.op("act", lambda g: g.activation(out=junk[0:n], in_=bk[0:n, 0:QL], func=AF.Square, accum_out=ssq[0:n]),
                 [bankB[bi]], [bjunk, bstq])
            rstd_from_ss(ssq, rstdq, n, QL, bstq)
            P.op("dve", lambda g: g.scalar_tensor_tensor(out=cq_b[0:n], in0=bk[0:n, 0:QL], scalar=rstdq[0:n], in1=gq_bc[0:n],
                                                         op0=ALU.mult, op1=ALU.mult), [bankB[bi], bstq, bg], [bcqb])
            transposes(lambda j: cq_b[0:n, j * 128:(j + 1) * 128], 4, n, 128, cqT, bcqT, bcqb)
            rp = rope2[p]
            for gq in range(4):
                bi = next_bank()
                mm_group(bi, n, 384, lambda c: cqT[:, c, 0:n], lambda c, gq=gq: Wuq[:, c, gq * 384:(gq + 1) * 384], 4, [bcqT, bW])
                bv = bank(bi)[:, 0:384].rearrange("p (h x) -> p h x", x=192)
                P.op("act", lambda g, bv=bv, gq=gq: g.activation(out=qn_b[0:n, 2 * gq:2 * gq + 2, :], in_=bv[0:n, :, 0:128], func=AF.Copy),
                     [bankB[bi]], [bqnb])
                cosb = rp[0:n, 0:32].unsqueeze(1).to_broadcast([n, 2, 32])
                sinb = rp[0:n, 32:64].unsqueeze(1).to_broadcast([n, 2, 32])
                rope_tok(lambda hf, bv=bv: bv[0:n, :, 128 + 32 * hf:128 + 32 * hf + 32],
                         lambda hf, gq=gq: qr_f[0:n, 2 * gq:2 * gq + 2, 32 * hf:32 * hf + 32],
                         cosb, sinb, n, (rt[0][0:n], rt[1][0:n]), [bankB[bi], brope2[p]], bqrf, brt)
            P.op("act", lambda g: g.activation(out=qr_b[0:n], in_=qr_f[0:n], func=AF.Copy), [bqrf], [bqrb])
            transposes(lambda j: qn_b[0:n, j, :], NH, n, 128, QnT_st, bQnT, bqnb)
            P.dma("pool", lambda g: g.dma_start(out=seq["QnT"][:, :, qoff:qoff + n].rearrange("h p k -> p h k"), in_=QnT_st[:, :, 0:n]),
                  [bQnT], ())
            transposes(lambda j: qr_b[0:n, j, :], NH, n, 64, QrT_st, bQrT, bqrb)
            P.dma("pool", lambda g: g.dma_start(out=seq["QrT"][:, :, qoff:qoff + n].rearrange("h p k -> p h k"), in_=QrT_st[0:64, :, 0:n]),
                  [bQrT], ())

        nt = len(tiles)
        load(0)
        front(0)
        front_T(0)
        for i in range(nt):
            if i + 1 < nt:
                load(i + 1)
                front(i + 1)
            back(i, mid=lambda i=i: front_T(i + 1))

    def attention():
        AR.off = persist_mark
        NT = NKEY // 128 + 1
        NKP = NKEY - NMETA + 128
        KT2 = [AR.alloc([NKP], BF16) for _ in range(2)]
        V2 = [AR.alloc([NT, 128], BF16) for _ in range(2)]
        Q2 = [[AR.alloc([NOWN * 128], BF16) for _ in range(2)] for _ in range(2)]
        Qn2 = [AR.alloc([NOWN * 128], BF16) for _ in range(2)]
        QR2 = [AR.alloc([NOWN * 128], BF16) for _ in range(2)]
        KrT = AR.alloc([NKP], BF16)
        PT = [AR.alloc([512], BF16) for _ in range(4)]
        onesM = AR.alloc([128], BF16); onesS = AR.alloc([128], BF16)
        bz = Buf()
        for par_ in range(2):
            P.op("pool", lambda g, par_=par_: g.memset(KT2[par_], 0.0), (), [bz])
            P.op("pool", lambda g, par_=par_: g.memset(V2[par_], 0.0), (), [bz])
            P.op("pool", lambda g, par_=par_: g.memset(Q2[par_][0], 0.0), (), [bz])
            P.op("pool", lambda g, par_=par_: g.memset(Q2[par_][1], 0.0), (), [bz])
            P.op("pool", lambda g, par_=par_: g.memset(QR2[par_], 0.0), (), [bz])
        P.op("pool", lambda g: g.memset(KrT, 0.0), (), [bz])
        for i_ in range(4):
            P.op("pool", lambda g, i_=i_: g.memset(PT[i_], 0.0), (), [bz])
        P.op("pool", lambda g: g.memset(onesM, 0.0), (), [bz])
        P.op("pool", lambda g: g.memset(onesS, 0.0), (), [bz])
        P.op("pool", lambda g: g.memset(onesM[0:NMETA], 1.0), [bz], [bz])
        P.op("pool", lambda g: g.memset(onesS[0:DSEQ], 1.0), [bz], [bz])
        P.barrier()
        rl = AR.alloc([512], F32); on0 = AR.alloc([512], F32); on1 = AR.alloc([512], F32)
        od = AR.alloc([512], F32); sq = AR.alloc([512], F32); rs = AR.alloc([512], F32)
        mix_st = [AR.alloc([512], BF16) for _ in range(2)]
        bKT2 = [Buf(), Buf()]; bV2 = [Buf(), Buf()]; bQ2 = [Buf(), Buf()]; bQR2 = [Buf(), Buf()]; bKrT = Buf()
        bPT = [Buf() for _ in range(4)]
        brl = Buf(); bon0 = Buf(); bon1 = Buf(); bod = Buf(); bsq = Buf(); brs = Buf(); bmix = [Buf(), Buf()]
        SETS = [dict(S=(0, 1), O=2, L=3, PT=(0, 1)), dict(S=(4, 5), O=6, L=7, PT=(2, 3))]

        def tile_list(kind, G, mla):
            tl = []
            if kind == "p":
                nq = 512
                tl.append(dict(koff=2 * NOWN * 128, nk=NMETA, c0=0,
                               bias=([(Bmeta, 0, 128, NMETA)] if (G == 0 and not mla) else [])))
                for j in range(4 * G + 4):
                    c0 = max(j - 4 * G, 0) * 128
                    b = []
                    if j >= 4 * G:
                        b.append((Mdiag if mla else Bdiag, c0, 128, 128))
                    tl.append(dict(koff=j * 128, nk=128, c0=c0, bias=b))
                    b = []
                    if j >= 4 * G:
                        b.append((Mo0 if mla else Bo0, c0, 128, 128))
                    if (not mla) and 4 * G <= j + 1 <= 4 * G + 3:
                        b.append((Bo1, (j + 1 - 4 * G) * 128, 128, 128))
                    tl.append(dict(koff=NOWN * 128 + j * 128, nk=128, c0=c0, bias=b))
                tl.sort(key=lambda d: d["c0"])
            else:
                nq = DSEQ
                for j in range(PAST // 128):
                    tl.append(dict(koff=j * 128, nk=128, c0=0, bias=([(Bs7, 0, 64, 128)] if (j == 7 and not mla) else [])))
                tl.append(dict(koff=PAST, nk=DSEQ, c0=0, bias=([(Bsn, 0, 64, 64)] if not mla else [])))
            return tl, nq

        steps = []
        maps = []
        mi = 0
        import os
        att_f = os.environ.get("MK_ATT", "sp")
        for kind, seq in (("s", SS_), ("p", SP_)):
            if kind not in att_f:
                continue
            nkey = seq["nkey"]; nqt = seq["nq"]
            ngroups = 1 if kind == "s" else 4
            for hi in range(2 * NH):
                mla = hi >= NH
                h = hi % NH
                par = hi % 2
                maps_head = []
                for G in range(ngroups):
                    for comp in ((0,) if mla else (0, 1)):
                        tl, nq = tile_list(kind, G, mla)
                        m = dict(kind=kind, seq=seq, mla=mla, h=h, par=par, G=G, comp=comp, tiles=tl, nq=nq,
                                 qbase=G * 512 if kind == "p" else 0, set=SETS[mi % 2], first_of_head=(G == 0 and comp == 0),
                                 nkey=nkey, nqt=nqt)
                        mi += 1
                        maps.append(m)
                        for ti in range(len(tl)):
                            steps.append((m, ti))

        def load_head(m):
            seq = m["seq"]; h = m["h"]; par = m["par"]; nkey = m["nkey"]; nqt = m["nqt"]; kind = m["kind"]
            nfull = (nkey // 128)
            rem = nkey - nfull * 128
            if m["mla"]:
                P.dma("sp", lambda g: g.dma_start(out=KT2[par][:, 0:nkey], in_=seq["KmT"][h]), (), [bKT2[par]])
                vsrc = seq["Vm"]
                P.dma("sp", lambda g: g.dma_start(out=Qn2[par][:, 0:nqt], in_=seq["QnT"][h]), (), [bQ2[par]])
                P.dma("sp", lambda g: g.dma_start(out=QR2[par][0:64, 0:nqt], in_=seq["QrT"][h]), (), [bQR2[par]])
                if h == 0:
                    P.dma("sp", lambda g: g.dma_start(out=KrT[0:64, 0:nkey], in_=seq["KrT"]), (), [bKrT])
            else:
                P.dma("sp", lambda g: g.dma_start(out=KT2[par][:, 0:nkey], in_=seq["KdT"][h]), (), [bKT2[par]])
                vsrc = seq["Vd"]
                P.dma("sp", lambda g: g.dma_start(out=Q2[par][0][0:64, 0:nqt], in_=seq["QdT"][h][0:64]), (), [bQ2[par]])
                P.dma("sp", lambda g: g.dma_start(out=Q2[par][1][64:128, 0:nqt], in_=seq["QdT"][h][64:128]), (), [bQ2[par]])
            dma_mid("sp", V2[par][:, 0:nfull, :], vsrc[0:nfull * 128, h * 128:(h + 1) * 128].rearrange("(t p) e -> p t e", p=128),
                    4, (), [bV2[par]])
            P.dma("sp", lambda g: g.dma_start(out=V2[par][0:rem, nfull, :], in_=vsrc[nfull * 128:nkey, h * 128:(h + 1) * 128]),
                  (), [bV2[par]])

        def stage1(m, ti):
            kt = m["tiles"][ti]; st = m["set"]; par = m["par"]
            nk = kt["nk"]; c0 = kt["c0"]; nq = m["nq"]; koff = kt["koff"]; qb = m["qbase"]
            Sb = st["S"][ti % 2]
            o = bank(Sb)[:, c0:nq]
            nb = len(kt["bias"])
            if m["mla"]:
                P.op("pe", lambda g: g.matmul(o, lhsT=KT2[par][:, koff:koff + 128], rhs=Qn2[par][:, qb + c0:qb + nq], start=True, stop=False),
                     [bKT2[par], bQ2[par]], [bankB[Sb]], signal=False)
                P.op("pe", lambda g: g.matmul(o, lhsT=KrT[:, koff:koff + 128], rhs=QR2[par][:, qb + c0:qb + nq], start=False, stop=(nb == 0)),
                     [bKrT, bQR2[par]], [bankB[Sb]], signal=(nb == 0))
            else:
                cc = m["comp"]
                P.op("pe", lambda g: g.matmul(o, lhsT=KT2[par][:, koff:koff + 128], rhs=Q2[par][cc][:, qb + c0:qb + nq], start=True, stop=(nb == 0)),
                     [bKT2[par], bQ2[par]], [bankB[Sb]], signal=(nb == 0))
            for bi_, (tab, coff, ncb, nkb) in enumerate(kt["bias"]):
                rhs = tab[:, 0:ncb] if m["mla"] else tab[:, m["h"], 0:ncb]
                P.op("pe", lambda g, rhs=rhs, coff=coff, ncb=ncb, last=(bi_ == nb - 1): g.matmul(
                    bank(Sb)[:, coff:coff + ncb], lhsT=ident, rhs=rhs, start=False, stop=last),
                    [b_const], [bankB[Sb]], signal=(bi_ == nb - 1))
            pt = st["PT"][ti % 2]
            scale = MLA_SCALE if m["mla"] else DIFF_SCALE
            P.op("act", lambda g: g.activation(out=PT[pt][:, c0:nq], in_=o, func=AF.Exp, scale=scale), [bankB[Sb]], [bPT[pt]])

        def stage2(m, ti):
            kt = m["tiles"][ti]; st = m["set"]; par = m["par"]
            nk = kt["nk"]; c0 = kt["c0"]; nq = m["nq"]; koff = kt["koff"]
            pt = st["PT"][ti % 2]
            nt_ = len(m["tiles"])
            first = (ti == 0); last = (ti == nt_ - 1)
            slot = koff // 128
            ones_t = ones_b if nk == 128 else (onesM if nk == NMETA else onesS)
            P.op("pe", lambda g: g.matmul(bank(st["O"])[:, c0:nq], lhsT=V2[par][:, slot, :], rhs=PT[pt][:, c0:nq], start=first, stop=last),
                 [bV2[par], bPT[pt]], [bankB[st["O"]]], signal=False)
            P.op("pe", lambda g: g.matmul(bank(st["L"])[:, c0:nq], lhsT=ones_t, rhs=PT[pt][:, c0:nq], start=first, stop=last),
                 [b_const, bPT[pt]], [bankB[st["L"]]], signal=True)
            if last:
                finish(m)

        def finish(m):
            st = m["set"]; nq = m["nq"]; seq = m["seq"]; qb = m["qbase"]; h = m["h"]
            O = bank(st["O"])[:, 0:nq]; L = bank(st["L"])[:, 0:nq]
            bO = bankB[st["O"]]; bL = bankB[st["L"]]
            P.op("dve", lambda g: g.reciprocal(out=rl[:, 0:nq], in_=L), [bL], [brl])
            if m["mla"]:
                ms = m["G"] % 2
                P.op("dve", lambda g: g.tensor_tensor(out=mix_st[ms][:, 0:nq], in0=O, in1=rl[:, 0:nq], op=ALU.mult), [bO, brl], [bmix[ms]])
                P.dma("pool", lambda g: g.dma_start(out=seq["MixT"][NH + h][:, qb:qb + nq], in_=mix_st[ms][:, 0:nq]), [bmix[ms]], ())
                return
            if m["comp"] == 0:
                P.op("dve", lambda g: g.tensor_tensor(out=on0[:, 0:nq], in0=O, in1=rl[:, 0:nq], op=ALU.mult), [bO, brl], [bon0])
                return
            P.op("dve", lambda g: g.tensor_tensor(out=on1[:, 0:nq], in0=O, in1=rl[:, 0:nq], op=ALU.mult), [bO, brl], [bon1])
            P.op("dve", lambda g: g.scalar_tensor_tensor(out=od[:, 0:nq], in0=on1[:, 0:nq], scalar=neg_lam, in1=on0[:, 0:nq],
                                                         op0=ALU.mult, op1=ALU.add), [bon0, bon1, b_const], [bod])
            P.op("act", lambda g: g.activation(out=sq[:, 0:nq], in_=od[:, 0:nq], func=AF.Square), [bod], [bsq])
            Sb = st["S"][0]
            P.op("pe", lambda g: g.matmul(bank(Sb)[:, 0:nq], lhsT=ones_f, rhs=sq[:, 0:nq], start=True, stop=True), [bsq, b_const], [bankB[Sb]])
            P.op("dve", lambda g: g.tensor_scalar(out=rs[:, 0:nq], in0=bank(Sb)[:, 0:nq], scalar1=1.0 / 128, scalar2=EPS,
                                                  op0=ALU.mult, op1=ALU.add), [bankB[Sb]], [brs])
            P.op("act", lambda g: g.activation(out=rs[:, 0:nq], in_=rs[:, 0:nq], func=AF.Sqrt), [brs], [brs])
            P.op("dve", lambda g: g.reciprocal(out=rs[:, 0:nq], in_=rs[:, 0:nq]), [brs], [brs])
            ms = m["G"] % 2
            P.op("dve", lambda g: g.scalar_tensor_tensor(out=mix_st[ms][:, 0:nq], in0=od[:, 0:nq], scalar=gsub8, in1=rs[:, 0:nq],
                                                         op0=ALU.mult, op1=ALU.mult), [bod, brs, b_const], [bmix[ms]])
            P.dma("pool", lambda g: g.dma_start(out=seq["MixT"][h][:, qb:qb + nq], in_=mix_st[ms][:, 0:nq]), [bmix[ms]], ())

        head_first = [i for i, mm in enumerate(maps) if mm["first_of_head"]]
        load_head(maps[head_first[0]])
        nxt = {head_first[k]: head_first[k + 1] for k in range(len(head_first) - 1)}
        seen = set()
        prev = None
        every = max(1, len(steps) // (len(deferred_late) + 1))
        for si_, (m, ti) in enumerate(steps):
            if deferred_late and si_ % every == every - 1:
                deferred_late.pop(0)()
            stage1(m, ti)
            if prev is not None:
                stage2(*prev)
            prev = (m, ti)
            idm = id(m)
            if idm not in seen:
                seen.add(idm)
                k = maps.index(m)
                if k in nxt:
                    load_head(maps[nxt[k]])
        stage2(*prev)

    def phase3():
        AR.off = persist_mark
        TMAX = 576
        g_bc = AR.alloc([D], F32)
        mixh = AR.alloc([16, TMAX], BF16)
        x1 = AR.alloc([5, D], F32)
        gated = AR.alloc([NFC, TMAX], BF16)
        _o = AR.off
        wo2 = [AR.alloc([16, 512], BF16) for _ in range(2)]
        AR.off = _o
        wgu = [[AR.alloc([16, 256], BF16) for _ in range(2)] for _ in range(2)]
        wd2 = [AR.alloc([NFC, 128], BF16) for _ in range(2)]
        hb2 = [AR.alloc([D], BF16) for _ in range(2)]
        bhb2 = [Buf(), Buf()]
        sg = [AR.alloc([TMAX], BF16) for _ in range(2)]
        dT = [AR.alloc([TMAX], F32) for _ in range(2)]
        ss = AR.alloc([1], F32); rstd = AR.alloc([1], F32)
        bg = Buf(); bmixh = Buf(); bx1 = [Buf() for _ in range(5)]; bgated = Buf()
        bwgu = [[Buf(), Buf()], [Buf(), Buf()]]; bwd2 = [Buf(), Buf()]; bsg = [Buf(), Buf()]; bdT = [Buf(), Buf()]
        bst = Buf()
        own = lambda i: dict(seq=SP_, q0=i * 128, n=128, x=xo[i * 128:(i + 1) * 128], y=y_o[i * 128:(i + 1) * 128])
        groups = [[dict(seq=SS_, q0=0, n=DSEQ, x=xsam, y=y_s)] + [own(i) for i in range(4)]]
        for G in range(1, 4):
            groups.append([own(i) for i in range(4 * G, 4 * G + 4)])
        cnt = {"wo": 0, "gu": 0, "wd": 0, "sg": 0, "dT": 0}

        def do_group(tiles):
            off = 0
            for t in tiles:
                t["off"] = off
                off += t["n"]
            T = off
            chunks = []
            cur = []
            for t in tiles:
                if cur and (t["off"] + t["n"] - cur[0]["off"] > 512 or t["seq"] is not cur[0]["seq"]):
                    chunks.append(cur); cur = []
                cur.append(t)
            chunks.append(cur)
            chunks = [(c[0]["off"], c[-1]["off"] + c[-1]["n"], c) for c in chunks]
            P.dma("sp", lambda g: g.dma_start(out=g_bc, in_=g_ffn.partition_broadcast(128)), (), [bg])
            for (c0, c1, ct) in chunks:
                seq = ct[0]["seq"]; q0 = ct[0]["q0"]
                dma_mid("sp", mixh[:, :, c0:c1], seq["MixT"][:, :, q0:q0 + (c1 - c0)].rearrange("c p t -> p c t"), 4, (), [bmixh])
            for k, t in enumerate(tiles):
                P.dma("sp", lambda g, k=k, t=t: g.dma_start(out=x1[0:t["n"], k, :], in_=t["x"]), (), [bx1[k]])
            for cg in range(4):
                p = cnt["wo"] % 2; cnt["wo"] += 1
                P.dma("sp", lambda g, p=p, cg=cg: g.dma_start(out=wo2[p], in_=w_out_b[cg]), [wbuf["w_out"]], bwgu[p])
                for k, t in enumerate(tiles):
                    n = t["n"]; o_ = t["off"]
                    bi = next_bank()
                    mm_group(bi, n, 512, lambda c, o_=o_, n=n: mixh[:, c, o_:o_ + n], lambda c, p=p: wo2[p][:, c, :], 16, [bmixh] + bwgu[p])
                    P.op("dve", lambda g, bi=bi, k=k, cg=cg, n=n: g.tensor_tensor(out=x1[0:n, k, cg * 512:(cg + 1) * 512], in0=bank(bi)[0:n, 0:512],
                                                                                   in1=x1[0:n, k, cg * 512:(cg + 1) * 512], op=ALU.add),
                         [bankB[bi], bx1[k]], [bx1[k]])
            for k, t in enumerate(tiles):
                n = t["n"]; o_ = t["off"]
                hb = hb2[k % 2]; bhb = bhb2[k % 2]
                P.op("act", lambda g, k=k, n=n, hb=hb: g.activation(out=hb[0:n], in_=x1[0:n, k, :], func=AF.Square, accum_out=ss[0:n]), [bx1[k]], [bhb, bst])
                rstd_from_ss(ss, rstd, n, D, bst)
                P.op("dve", lambda g, k=k, n=n, hb=hb: g.scalar_tensor_tensor(out=hb[0:n], in0=x1[0:n, k, :], scalar=rstd[0:n], in1=g_bc[0:n],
                                                                               op0=ALU.mult, op1=ALU.mult), [bx1[k], bst, bg], [bhb])
                for j0 in (0, 8):
                    bi = next_bank()
                    bv = bank(bi, BF16).rearrange("p (j t) -> p j t", t=128)
                    for j in range(j0, j0 + 8):
                        P.op("pe", lambda g, j=j, j0=j0, bv=bv, n=n, hb=hb: g.transpose(out=bv[:, j - j0, 0:n], in_=hb[0:n, j * 128:(j + 1) * 128], identity=ident[0:n, 0:n]),
                             [bhb, b_const], [bankB[bi]], signal=(j == j0 + 7))
                    copy_op("dve", mixh[:, j0:j0 + 8, o_:o_ + n], bv[:, :, 0:n], [bankB[bi]], [bmixh])
            P.dma("sp", lambda g: g.dma_start(out=g_bc, in_=g_fin.partition_broadcast(128)), (), [bg])
            for f2 in range(NFC // 2):
                p = cnt["gu"] % 2; cnt["gu"] += 1
                P.dma("sp", lambda g, p=p, f2=f2: g.dma_start(out=wgu[0][p], in_=wg_b[f2]), [wbuf["wg"]], [bwgu[0][p]])
                P.dma("sp", lambda g, p=p, f2=f2: g.dma_start(out=wgu[1][p], in_=wu_b[f2]), [wbuf["wu"]], [bwgu[1][p]])
                for sub in range(2):
                    f = f2 * 2 + sub
                    s_ = cnt["sg"] % 2; cnt["sg"] += 1
                    for (c0, c1, ct) in chunks:
                        bg_ = next_bank(); bu_ = next_bank()
                        mm_group(bg_, 128, c1 - c0, lambda c, p=p, sub=sub: wgu[0][p][:, c, sub * 128:(sub + 1) * 128],
                                 lambda c, c0=c0, c1=c1: mixh[:, c, c0:c1], 16, [bmixh, bwgu[0][p]])
                        mm_group(bu_, 128, c1 - c0, lambda c, p=p, sub=sub: wgu[1][p][:, c, sub * 128:(sub + 1) * 128],
                                 lambda c, c0=c0, c1=c1: mixh[:, c, c0:c1], 16, [bmixh, bwgu[1][p]])
                        w = c1 - c0
                        P.op("act", lambda g, bg_=bg_, s_=s_, c0=c0, c1=c1, w=w: g.activation(out=sg[s_][:, c0:c1], in_=bank(bg_)[:, 0:w], func=AF.Silu),
                             [bankB[bg_]], [bsg[s_]])
                        P.op("dve", lambda g, bu_=bu_, s_=s_, f=f, c0=c0, c1=c1, w=w: g.tensor_tensor(out=gated[:, f, c0:c1], in0=bank(bu_)[:, 0:w],
                                                                                                     in1=sg[s_][:, c0:c1], op=ALU.mult),
                             [bankB[bu_], bsg[s_]], [bgated])
            for j in range(16):
                p = cnt["wd"] % 2; cnt["wd"] += 1
                P.dma("sp", lambda g, p=p, j=j: g.dma_start(out=wd2[p], in_=wd_b[j]), [wbuf["wd"]], [bwd2[p]])
                d_ = cnt["dT"] % 2; cnt["dT"] += 1
                for (c0, c1, ct) in chunks:
                    w = c1 - c0
                    bi = next_bank()
                    mm_group(bi, 128, w, lambda c, p=p: wd2[p][:, c, :], lambda c, c0=c0, c1=c1: gated[:, c, c0:c1], NFC, [bgated, bwd2[p]])
                    P.op("act", lambda g, bi=bi, d_=d_, c0=c0, c1=c1, w=w: g.activation(out=dT[d_][:, c0:c1], in_=bank(bi)[:, 0:w], func=AF.Copy),
                         [bankB[bi]], [bdT[d_]])
                    bt = next_bank()
                    for kk, t in enumerate(ct):
                        n = t["n"]; o_ = t["off"]
                        P.op("pe", lambda g, kk=kk, bt=bt, d_=d_, n=n, o_=o_: g.transpose(out=bank(bt)[0:n, kk * 128:(kk + 1) * 128],
                                                                                          in_=dT[d_][:, o_:o_ + n], identity=ident_f),
                             [bdT[d_], b_const], [bankB[bt]], signal=(kk == len(ct) - 1))
                    for kk, t in enumerate(ct):
                        n = t["n"]; k = tiles.index(t)
                        P.op("dve", lambda g, kk=kk, bt=bt, j=j, n=n, k=k: g.tensor_tensor(out=x1[0:n, k, j * 128:(j + 1) * 128],
                                                                                           in0=bank(bt)[0:n, kk * 128:(kk + 1) * 128],
                                                                                           in1=x1[0:n, k, j * 128:(j + 1) * 128], op=ALU.add),
                             [bankB[bt], bx1[k]], [bx1[k]])
            for k, t in enumerate(tiles):
                n = t["n"]
                P.op("act", lambda g, k=k, n=n: g.activation(out=hb2[0][0:n], in_=x1[0:n, k, :], func=AF.Square, accum_out=ss[0:n]), [bx1[k]], [bhb2[0], bst])
                rstd_from_ss(ss, rstd, n, D, bst)
                P.op("dve", lambda g, k=k, n=n: g.scalar_tensor_tensor(out=x1[0:n, k, :], in0=x1[0:n, k, :], scalar=rstd[0:n], in1=g_bc[0:n],
                                                                        op0=ALU.mult, op1=ALU.mult), [bx1[k], bst, bg], [bx1[k]])
                P.dma("pool", lambda g, k=k, t=t, n=n: g.dma_start(out=t["y"], in_=x1[0:n, k, :]), [bx1[k]], (), is_output=True)

        import os
        if os.environ.get("MK_P3", "") == "s":
            groups[:] = [groups[0][:1]]
        for tiles in groups:
            do_group(tiles)

    cast("w_ukv", w_ukv_b, w_ukv)
    cast("ck", ck_b, ck)
    cast("cckv", cckv_b, cckv)
    cast("ckr", ckr_b, ckr)
    winK_toks = []
    cast("w_in", w_in_b, w_in, cols=[(OFF_K, OFF_CQ), (OFF_CKV, DIN)], tokens=winK_toks)
    cast("w_in", w_in_b, w_in, cols=[(0, OFF_K), (OFF_CQ, OFF_CKV)], defer=True)
    cast("w_uq", w_uq_b, w_uq, defer=True)
    cast("cv", SS_["Vd"][0:PAST], cv, defer=True)
    cast("w_out", w_out_b, w_out, defer=True, slab=512)
    cast("wg", wg_b, wg, defer=True, slab=256, late=True)
    cast("wu", wu_b, wu, defer=True, slab=256, late=True)
    cast("wd", wd_b, wd, defer=True, slab=128, late=True)
    import os
    stop = int(os.environ.get("MK_STOP", "9"))
    setup()
    P.barrier()
    if stop >= 1:
        passA()
        P.barrier()
    if stop >= 2:
        passB()
        P.barrier()
    if stop >= 3:
        attention()
    while deferred_late:
        deferred_late.pop(0)()
    if stop >= 3:
        P.barrier(final=True)
    if stop >= 4:
        phase3()
    while deferred:
        deferred.pop(0)()
    P.finalize()
    P.emit()
    return nc, P


def _t5_onehot(delta):
    rel = (np.arange(255, dtype=np.int32) - 127 + delta)
    try:
        import jax
        import jax.numpy as jnp
        with jax.default_device(jax.devices("cpu")[0]):
            r = jnp.asarray(rel)
            nb = 16
            max_exact = 8
            ret = jnp.where(r > 0, nb, 0)
            n = jnp.abs(r)
            large = max_exact + (jnp.log(jnp.maximum(n, 1).astype(jnp.float32) / max_exact)
                                 / math.log(128 / max_exact) * (nb - max_exact)).astype(jnp.int32)
            large = jnp.minimum(large, nb - 1)
            bucket = np.asarray(ret + jnp.where(n < max_exact, n, large))
    except Exception:
        nb, max_exact = 16, 8
        ret = np.where(rel > 0, nb, 0)
        n = np.abs(rel)
        large = max_exact + (np.log(np.maximum(n, 1).astype(np.float32) / np.float32(max_exact))
                             / np.float32(math.log(128 / max_exact)) * np.float32(nb - max_exact)).astype(np.int32)
        large = np.minimum(large, nb - 1)
        bucket = ret + np.where(n < max_exact, n, large)
    ohm = np.zeros((32, 255), np.float32)
    ohm[bucket, np.arange(255)] = 1.0
    ohm[15, :] -= 1.0
    return ohm


def _rope_table(pos):
    half = 32
    inv_freq = (np.float32(10000.0) ** (-np.arange(half, dtype=np.float32) / np.float32(half))).astype(np.float32)
    ang = pos.astype(np.float32)[:, None] * inv_freq[None, :]
    return np.concatenate([np.cos(ang), np.sin(ang)], axis=1).astype(np.float32)


_CACHE = {}


def kernel(x_prompt, x_sample, cache_diff_k, cache_diff_v, cache_mla_ckv, cache_mla_krope,
           meta_tokens, rel_bias, norm_attn_g, w_in, diff_lambda, diff_subln_g,
           mla_q_norm_g, mla_w_uq, mla_kv_norm_g, mla_w_ukv, w_out, norm_ffn_g,
           ffn_w_gate, ffn_w_up, ffn_w_down, final_norm_g):
    f = lambda a: np.ascontiguousarray(np.asarray(a, dtype=np.float32))
    x_prompt = f(x_prompt); x_sample = f(x_sample)
    if "nc" not in _CACHE:
        _CACHE["nc"] = build_program()[0]
    nc = _CACHE["nc"]
    ohs = np.stack([_t5_onehot(0), _t5_onehot(-128), _t5_onehot(-16)])
    kq = np.arange(128)
    md = np.where((kq[None, :] < 64) & (kq[:, None] >= 64), MASKV, 0.0).astype(np.float32)
    shared = dict(
        w_in=f(w_in[0]), w_uq=f(mla_w_uq[0]), w_ukv=f(mla_w_ukv[0]), w_out=f(w_out[0]),
        wg=f(ffn_w_gate[0]), wu=f(ffn_w_up[0]), wd=f(ffn_w_down[0]),
        g_attn=f(norm_attn_g[0]), g_ffn=f(norm_ffn_g[0]), g_fin=f(final_norm_g),
        g_q=f(mla_q_norm_g[0]), g_kv=f(mla_kv_norm_g[0]), g_sub=f(diff_subln_g[0]),
        lam4=f(diff_lambda[0]).reshape(256), relb=f(rel_bias), xmeta=f(meta_tokens),
        oh=ohs, mask_diag=md, rope_m=_rope_table(np.arange(NMETA)), rope_s=_rope_table(PAST + np.arange(DSEQ)),
    )
    in_maps = []
    for c in range(8):
        b, par = c // 2, c % 2
        own = np.arange(NOWN) * 2 + par
        oth = np.arange(NOWN) * 2 + (1 - par)
        xb = x_prompt[b].reshape(32, 128, D)
        pos_o = (NMETA + own[:, None] * 128 + np.arange(128)[None, :]).reshape(-1)
        pos_t = (NMETA + oth[:, None] * 128 + np.arange(128)[None, :]).reshape(-1)
        selv = np.zeros((128, 4), np.float32)
        if par == 1:
            selv[:, 0] = 1.0
            mo = np.zeros((128, 128), np.float32)
        else:
            selv[:, 1] = 1.0
            selv[:, 2] = 1.0
            mo = np.full((128, 128), MASKV, np.float32)
        m = dict(shared)
        m.update(
            xo=np.ascontiguousarray(xb[own].reshape(-1, D)), xt=np.ascontiguousarray(xb[oth].reshape(-1, D)),
            xsam=x_sample[c],
            ck=f(cache_diff_k[0, c]).reshape(PAST, 1024), cv=f(cache_diff_v[0, c]).reshape(PAST, 1024),
            cckv=f(cache_mla_ckv[0, c]), ckr=f(cache_mla_krope[0, c]),
            rope_o=_rope_table(pos_o), rope_t=_rope_table(pos_t), mask_o0=mo, sel=selv,
        )
        in_maps.append(m)
    if _CACHE.get("prep_only"):
        return in_maps
    res = run_bass_kernel_spmd(nc, in_maps, core_ids=list(range(8))).results
    B = 4
    L = NMETA + SEQ
    y_prompt = np.zeros((B, SEQ, D), np.float32)
    nk = np.zeros((1, B, L, 1024), np.float32); nv = np.zeros((1, B, L, 1024), np.float32)
    nckv = np.zeros((1, B, L, KVL), np.float32); nkr = np.zeros((1, B, L, ROPE), np.float32)
    y_sample = np.zeros((8, DSEQ, D), np.float32)
    sk = np.zeros((1, 8, DSEQ, 1024), np.float32); sv = np.zeros((1, 8, DSEQ, 1024), np.float32)
    sckv = np.zeros((1, 8, DSEQ, KVL), np.float32); skr = np.zeros((1, 8, DSEQ, ROPE), np.float32)
    for c in range(8):
        b, par = c // 2, c % 2
        r = res[c]
        for i in range(NOWN):
            gt = 2 * i + par
            sl = slice(i * 128, (i + 1) * 128)
            y_prompt[b, gt * 128:(gt + 1) * 128] = r["y_o"][sl]
            dst = slice(NMETA + gt * 128, NMETA + (gt + 1) * 128)
            nk[0, b, dst] = r["kd_o"][sl]; nv[0, b, dst] = r["vd_o"][sl]
            nckv[0, b, dst] = r["ckv_o"][sl]; nkr[0, b, dst] = r["kr_o"][sl]
        if par == 0:
            nk[0, b, 0:NMETA] = r["kd_m"]; nv[0, b, 0:NMETA] = r["vd_m"]
            nckv[0, b, 0:NMETA] = r["ckv_m"]; nkr[0, b, 0:NMETA] = r["kr_m"]
        y_sample[c] = r["y_s"]
        sk[0, c] = r["kd_s"]; sv[0, c] = r["vd_s"]; sckv[0, c] = r["ckv_s"]; skr[0, c] = r["kr_s"]
    return (y_prompt, y_sample,
            nk.reshape(1, B, L, NH, 2, DH), nv.reshape(1, B, L, NH, 128), nckv, nkr,
            sk.reshape(1, 8, DSEQ, NH, 2, DH), sv.reshape(1, 8, DSEQ, NH, 128), sckv, skr)
```
